# Optimizing a Trainium2 kernel written in Bass

```python
import math
import jax, jax.numpy as jnp
from jax import lax
import numpy as np

D_MODEL = 2048
BATCH = 2
SEQ = 16384
DEPTH = 2

N_BRANCH = 4
BRANCH_W = D_MODEL // N_BRANCH
A_HEAD_DIM = 64
A_HEADS = BRANCH_W // A_HEAD_DIM
DILATED_PATTERNS = ((128, 1), (512, 4), (2048, 16))
BAND = 128
DIFF_QK_DIM = 64
DIFF_V_DIM = 2 * DIFF_QK_DIM
B_HEADS = BRANCH_W // DIFF_V_DIM
Q_BLOCK = 128
DN_HEAD_K = 128
DN_HEAD_V = 128
C_HEADS = BRANCH_W // DN_HEAD_V
CONV_K = 4
CHUNK = 64
HG_EXPAND = 128
HG_HEAD_V = 128
D_HEADS = BRANCH_W // HG_HEAD_V
NUM_BUCKETS = 32
MAX_DISTANCE = 2048
RMS_EPS = 1e-6

IN_SPLITS = (
    A_HEADS * A_HEAD_DIM, A_HEADS * A_HEAD_DIM, A_HEADS * A_HEAD_DIM, BRANCH_W,
    B_HEADS * 2 * DIFF_QK_DIM, B_HEADS * 2 * DIFF_QK_DIM, B_HEADS * DIFF_V_DIM, BRANCH_W,
    C_HEADS * (2 * DN_HEAD_K + DN_HEAD_V), BRANCH_W, C_HEADS, C_HEADS,
    D_HEADS * HG_EXPAND, D_HEADS * HG_EXPAND, D_HEADS * HG_HEAD_V, BRANCH_W,
    N_BRANCH * D_MODEL,
)
N_IN_COLS = sum(IN_SPLITS)

kernel_name = "hybrid_gated_four_mixer_block"


def rms_norm(x, gain):
    x32 = x.astype(jnp.float32)
    y = x32 * lax.rsqrt(jnp.mean(x32 * x32, axis=-1, keepdims=True) + RMS_EPS)
    return y * gain.astype(jnp.float32)


def l2norm(x):
    return x * lax.rsqrt(jnp.sum(x * x, axis=-1, keepdims=True) + 1e-6)


def t5_bucket(dist):
    n = jnp.maximum(dist, 0)
    max_exact = NUM_BUCKETS // 2
    nf = jnp.maximum(n, max_exact).astype(jnp.float32)
    large = max_exact + (jnp.log(nf / max_exact) / math.log(MAX_DISTANCE / max_exact)
                         * (NUM_BUCKETS - max_exact)).astype(jnp.int32)
    large = jnp.minimum(large, NUM_BUCKETS - 1)
    return jnp.where(n < max_exact, n, large)


def dilated_window_attention(q, k, v, bias_table):
    Bsz, T, H, dh = q.shape
    scale = dh ** -0.5
    qi = jnp.arange(BAND)[:, None]
    kj = jnp.arange(2 * BAND)[None, :]
    rel = qi + BAND - kj
    outs, lses = [], []
    for window, dil in DILATED_PATTERNS:
        n = T // dil
        nb = -(-n // BAND)
        n_pad = nb * BAND

        def to_blocks(t):
            t = t.reshape(Bsz, n, dil, H, dh).transpose(0, 2, 1, 3, 4)
            t = jnp.pad(t, ((0, 0), (0, 0), (0, n_pad - n), (0, 0), (0, 0)))
            return t.reshape(Bsz, dil, nb, BAND, H, dh)

        def with_prev(t):
            prev = jnp.pad(t[:, :, :-1], ((0, 0), (0, 0), (1, 0), (0, 0), (0, 0), (0, 0)))
            return jnp.concatenate([prev, t], axis=3)

        qb = to_blocks(q)
        kb = with_prev(to_blocks(k))
        vb = with_prev(to_blocks(v))
        bias = bias_table[t5_bucket(rel * dil)].transpose(2, 0, 1)
        key_idx = jnp.arange(nb)[:, None, None] * BAND + kj[None] - BAND
        valid = ((rel >= 0) & (rel <= window // dil))[None] & (key_idx >= 0)
        logits = jnp.einsum('bcnqhd,bcnkhd->bcnhqk', qb, kb) * scale + bias
        logits = jnp.where(valid[:, None], logits, -jnp.inf)
        m = jnp.max(logits, axis=-1, keepdims=True)
        p = jnp.exp(logits - m)
        l = jnp.sum(p, axis=-1, keepdims=True)
        o = jnp.einsum('bcnhqk,bcnkhd->bcnhqd', p, vb) / l
        lse = (m + jnp.log(l))[..., 0]
        o = o.transpose(0, 1, 2, 4, 3, 5).reshape(Bsz, dil, n_pad, H, dh)[:, :, :n]
        lse = lse.transpose(0, 1, 2, 4, 3).reshape(Bsz, dil, n_pad, H)[:, :, :n]
        outs.append(o.transpose(0, 2, 1, 3, 4).reshape(Bsz, T, H, dh))
        lses.append(lse.transpose(0, 2, 1, 3).reshape(Bsz, T, H))
    w = jax.nn.softmax(jnp.stack(lses), axis=0)
    return jnp.einsum('pbth,pbthd->bthd', w, jnp.stack(outs))


def diff_attention(q, k, v, bias_table, lam):
    Bsz, T, H, _, dqk = q.shape
    scale = dqk ** -0.5
    nb = T // Q_BLOCK
    qb = q.reshape(Bsz, nb, Q_BLOCK, H, 2, dqk).transpose(1, 0, 2, 3, 4, 5)
    kpos = jnp.arange(T)

    def block(args):
        qblk, bi = args
        qpos = bi * Q_BLOCK + jnp.arange(Q_BLOCK)
        rel = qpos[:, None] - kpos[None, :]
        bias = bias_table[t5_bucket(rel)].transpose(2, 0, 1)
        logits = jnp.einsum('bqhmd,bkhmd->bhmqk', qblk, k) * scale + bias[:, None]
        logits = jnp.where(rel >= 0, logits, -jnp.inf)
        p = jax.nn.softmax(logits, axis=-1)
        attn = p[:, :, 0] - lam * p[:, :, 1]
        return jnp.einsum('bhqk,bkhd->bqhd', attn, v)

    o = lax.map(block, (qb, jnp.arange(nb)))
    return o.transpose(1, 0, 2, 3, 4).reshape(Bsz, T, H, -1)


def to_chunks(t):
    Bsz, T, H = t.shape[:3]
    t = t.reshape(Bsz, T // CHUNK, CHUNK, H, *t.shape[3:])
    return jnp.moveaxis(t, (1, 3), (0, 2))


def from_chunks(o):
    o = jnp.moveaxis(o, (0, 2), (1, 3))
    Bsz, nc, C, H, d = o.shape
    return o.reshape(Bsz, nc * C, H, d)


def short_conv(x, w):
    K, C = w.shape
    return lax.conv_general_dilated(x, w[:, None, :], window_strides=(1,), padding=[(K - 1, 0)],
                                    dimension_numbers=('NWC', 'WIO', 'NWC'), feature_group_count=C)


def gated_delta_net(q, k, v, beta, g):
    Bsz, T, H, dk = q.shape
    dv = v.shape[-1]
    qc, kc, vc = to_chunks(q), to_chunks(k), to_chunks(v)
    bc, gc = to_chunks(beta), to_chunks(g)
    G = jnp.cumsum(gc, axis=-1)
    tri = jnp.tril(jnp.ones((CHUNK, CHUNK), bool))
    strict = jnp.tril(jnp.ones((CHUNK, CHUNK), bool), -1)
    gamma = jnp.exp(jnp.where(tri, G[..., :, None] - G[..., None, :], -jnp.inf))
    kk = jnp.einsum('nbhid,nbhjd->nbhij', kc, kc)
    M = jnp.where(strict, bc[..., :, None] * kk * gamma, 0.0)
    eye = jnp.eye(CHUNK, dtype=M.dtype)
    rhs = jnp.concatenate([vc * bc[..., None], kc * (bc * jnp.exp(G))[..., None]], axis=-1)
    sol = lax.linalg.triangular_solve(M + eye, rhs, left_side=True, lower=True, unit_diagonal=True)
    U, W = sol[..., :dv], sol[..., dv:]
    Aqk = jnp.einsum('nbhid,nbhjd->nbhij', qc, kc) * gamma
    q_dec = qc * jnp.exp(G)[..., None]
    k_dec = kc * jnp.exp(G[..., -1:] - G)[..., None]
    g_last = jnp.exp(G[..., -1])

    def step(S, inp):
        u, w, a, qd, kd, gl = inp
        v_new = u - jnp.einsum('bhcd,bhde->bhce', w, S)
        o = jnp.einsum('bhcd,bhde->bhce', qd, S) + jnp.einsum('bhij,bhje->bhie', a, v_new)
        S = gl[..., None, None] * S + jnp.einsum('bhcd,bhce->bhde', kd, v_new)
        return S, o

    S0 = jnp.zeros((Bsz, H, dk, dv), jnp.float32)
    _, o = lax.scan(step, S0, (U, W, Aqk, q_dec, k_dec, g_last))
    return from_chunks(o)


def hgrn2(q, k, logf, v):
    Bsz, T, H, dk = q.shape
    dv = v.shape[-1]
    tri = jnp.tril(jnp.ones((CHUNK, CHUNK), bool))

    def step(S, inp):
        qc, kc, gc, vc = inp
        Bc = jnp.cumsum(gc, axis=2)
        o_inter = jnp.einsum('bhcd,bhde->bhce', qc * jnp.exp(Bc), S)
        diff = Bc[:, :, :, None, :] - Bc[:, :, None, :, :]
        dec = jnp.exp(jnp.where(tri[:, :, None], diff, -jnp.inf))
        A = jnp.einsum('bhid,bhjd,bhijd->bhij', qc, kc, dec)
        o = o_inter + jnp.einsum('bhij,bhje->bhie', A, vc)
        bl = Bc[:, :, -1]
        S = jnp.exp(bl)[..., None] * S + jnp.einsum('bhcd,bhce->bhde', kc * jnp.exp(bl[:, :, None] - Bc), vc)
        return S, o

    S0 = jnp.zeros((Bsz, H, dk, dv), jnp.float32)
    _, o = lax.scan(step, S0, (to_chunks(q), to_chunks(k), to_chunks(logf), to_chunks(v)))
    return from_chunks(o)


def setup_inputs(seed: int = 0) -> dict:
    key = jax.random.key(seed)
    ks = jax.random.split(key, 16)
    f32 = jnp.float32
    nrm = lambda k, s: jax.random.normal(k, s, f32)
    x = nrm(ks[0], (BATCH, SEQ, D_MODEL))
    norm_gain = 1.0 + 0.02 * nrm(ks[1], (DEPTH, D_MODEL))
    w_in = nrm(ks[2], (DEPTH, D_MODEL, N_IN_COLS)) * D_MODEL ** -0.5
    rel_bias = 0.1 * nrm(ks[3], (NUM_BUCKETS, A_HEADS + B_HEADS))
    diff_lambda = 0.1 * nrm(ks[4], (DEPTH, 4, DIFF_QK_DIM))
    diff_subln_gain = 1.0 + 0.02 * nrm(ks[5], (DEPTH, DIFF_V_DIM))
    dn_conv = nrm(ks[6], (DEPTH, CONV_K, C_HEADS * (2 * DN_HEAD_K + DN_HEAD_V))) * CONV_K ** -0.5
    dn_a_log = jnp.log(jax.random.uniform(ks[7], (DEPTH, C_HEADS), f32, 1.0, 16.0))
    dt = jnp.exp(jax.random.uniform(ks[8], (DEPTH, C_HEADS), f32, math.log(1e-3), math.log(1e-1)))
    dn_dt_bias = dt + jnp.log(-jnp.expm1(-dt))
    dn_norm_gain = 1.0 + 0.02 * nrm(ks[9], (DEPTH, DN_HEAD_V))
    hg_lb_logits = 0.5 * nrm(ks[10], (DEPTH, D_HEADS * HG_EXPAND))
    hg_norm_gain = 1.0 + 0.02 * nrm(ks[11], (DEPTH, HG_HEAD_V))
    w_branch = nrm(ks[12], (DEPTH, N_BRANCH, BRANCH_W, D_MODEL)) * BRANCH_W ** -0.5
    w_out = nrm(ks[13], (DEPTH, D_MODEL, D_MODEL)) * D_MODEL ** -0.5
    final_gain = 1.0 + 0.02 * nrm(ks[14], (D_MODEL,))
    return {"x": x, "norm_gain": norm_gain, "w_in": w_in, "rel_bias": rel_bias,
            "diff_lambda": diff_lambda, "diff_subln_gain": diff_subln_gain, "dn_conv": dn_conv,
            "dn_a_log": dn_a_log, "dn_dt_bias": dn_dt_bias, "dn_norm_gain": dn_norm_gain,
            "hg_lb_logits": hg_lb_logits, "hg_norm_gain": hg_norm_gain, "w_branch": w_branch,
            "w_out": w_out, "final_gain": final_gain}


def reference(x, norm_gain, w_in, rel_bias, diff_lambda, diff_subln_gain, dn_conv, dn_a_log,
              dn_dt_bias, dn_norm_gain, hg_lb_logits, hg_norm_gain, w_branch, w_out, final_gain):
    f32 = jnp.float32
    Bsz, T, _ = x.shape
    silu = jax.nn.silu
    heads = lambda t, n: t.reshape(Bsz, T, n, -1)
    lb_p = jax.nn.softmax(hg_lb_logits.astype(f32), axis=0)
    hg_lb = jnp.clip(jnp.cumsum(lb_p, axis=0) - lb_p[0], 0.0, 1.0)
    bias_a = rel_bias[:, :A_HEADS].astype(f32)
    bias_b = rel_bias[:, A_HEADS:].astype(f32)
    split_at = [int(s) for s in np.cumsum(IN_SPLITS)[:-1]]
    for layer in range(DEPTH):
        h = rms_norm(x, norm_gain[layer])
        (a_q, a_k, a_v, a_gate, b_q, b_k, b_v, b_gate, c_qkv, c_z, c_beta, c_a,
         d_q, d_f, d_i, d_gate, merge) = jnp.split(h @ w_in[layer].astype(f32), split_at, axis=-1)

        o_a = dilated_window_attention(heads(a_q, A_HEADS), heads(a_k, A_HEADS), heads(a_v, A_HEADS), bias_a)
        y_a = o_a.reshape(Bsz, T, -1) * silu(a_gate)

        lam_init = 0.8 - 0.6 * math.exp(-0.3 * layer)
        lq1, lk1, lq2, lk2 = diff_lambda[layer].astype(f32)
        lam = jnp.exp(jnp.sum(lq1 * lk1)) - jnp.exp(jnp.sum(lq2 * lk2)) + lam_init
        o_b = diff_attention(b_q.reshape(Bsz, T, B_HEADS, 2, DIFF_QK_DIM),
                             b_k.reshape(Bsz, T, B_HEADS, 2, DIFF_QK_DIM),
                             heads(b_v, B_HEADS), bias_b, lam)
        o_b = rms_norm(o_b, diff_subln_gain[layer]) * (1.0 - lam_init)
        y_b = o_b.reshape(Bsz, T, -1) * silu(b_gate)

        c_qkv = silu(short_conv(c_qkv, dn_conv[layer].astype(f32)))
        c_q, c_k, c_v = jnp.split(c_qkv, [C_HEADS * DN_HEAD_K, 2 * C_HEADS * DN_HEAD_K], axis=-1)
        c_q = l2norm(heads(c_q, C_HEADS)) * DN_HEAD_K ** -0.5
        c_k = l2norm(heads(c_k, C_HEADS))
        beta = jax.nn.sigmoid(c_beta)
        g = -jnp.exp(dn_a_log[layer].astype(f32)) * jax.nn.softplus(c_a + dn_dt_bias[layer].astype(f32))
        o_c = gated_delta_net(c_q, c_k, heads(c_v, C_HEADS), beta, g)
        y_c = rms_norm(o_c, dn_norm_gain[layer]).reshape(Bsz, T, -1) * silu(c_z)

        lb = hg_lb[layer]
        logf = jnp.logaddexp(jnp.log(lb), jnp.log1p(-lb) + jax.nn.log_sigmoid(d_f))
        d_k = (1.0 - lb) * jax.nn.sigmoid(-d_f)
        o_d = hgrn2(heads(d_q, D_HEADS), heads(d_k, D_HEADS), heads(logf, D_HEADS), heads(d_i, D_HEADS))
        y_d = rms_norm(o_d, hg_norm_gain[layer]).reshape(Bsz, T, -1) * silu(d_gate)

        gates = jax.nn.sigmoid(merge.reshape(Bsz, T, N_BRANCH, D_MODEL))
        w_br = w_branch[layer].astype(f32)
        mixed = gates[:, :, 0] * (y_a @ w_br[0])
        mixed = mixed + gates[:, :, 1] * (y_b @ w_br[1])
        mixed = mixed + gates[:, :, 2] * (y_c @ w_br[2])
        mixed = mixed + gates[:, :, 3] * (y_d @ w_br[3])
        x = x + (mixed @ w_out[layer].astype(f32)).astype(x.dtype)
    return rms_norm(x, final_gain).astype(x.dtype)
```

```python
import contextlib
import math
import numpy as np
import concourse.bass as bass
import concourse.mybir as mybir
from concourse.bass_utils import run_bass_kernel_spmd

F32 = mybir.dt.float32
BF16 = mybir.dt.bfloat16
AF = mybir.ActivationFunctionType
ALU = mybir.AluOpType

D = 2048
NDS = 24
UNIQ = [0]


def uq(n):
    return "%s_L%d" % (n, UNIQ[0])


class Sched:
    def __init__(self, nc, es, selfdeps=True):
        self.nc = nc
        self.selfdeps = selfdeps
        self.E = {}
        self.sem = {}
        for name, eng in (("pe", nc.tensor), ("act", nc.scalar), ("dve", nc.vector),
                          ("pool", nc.gpsimd), ("sp", nc.sync)):
            sem = es.enter_context(nc.semaphore("s_" + name))
            self.E[name] = {"eng": eng, "cnt": 0, "seen": {}}
            self.sem[name] = sem
        self.dq = {}
        for q in ("sp", "pool", "act"):
            names = []
            for i in range(NDS):
                nm = "d_%s_%d" % (q, i)
                self.sem[nm] = es.enter_context(nc.semaphore(nm))
                names.append(nm)
            self.dq[q] = {"sems": names, "n": 0}
        self.sem["cc"] = es.enter_context(nc.semaphore("s_cc"))
        self.ncc = 0
        self.lw = {}
        self.rd = {}
        self.psum_keys = set()

    def _deps(self, r, w):
        deps = {}

        def add(tok):
            if tok is None:
                return
            s, v = tok
            if deps.get(s, 0) < v:
                deps[s] = v
        for k in r:
            add(self.lw.get(k))
        for k in w:
            add(self.lw.get(k))
            for s, v in self.rd.get(k, {}).items():
                add((s, v))
        return deps

    def _wait(self, ename, deps):
        E = self.E[ename]
        for s, v in deps.items():
            if s == ename and (ename == "pe" or not self.selfdeps):
                continue
            if E["seen"].get(s, 0) >= v:
                continue
            E["eng"].wait_ge(self.sem[s], v)
            E["seen"][s] = v

    def _commit(self, tok, r, w):
        s, v = tok
        for k in w:
            self.lw[k] = tok
            self.rd[k] = {}
        for k in r:
            d = self.rd.setdefault(k, {})
            if d.get(s, 0) < v:
                d[s] = v

    def op(self, ename, fn, r=(), w=()):
        xs = [k for k in r if k in self.psum_keys and k not in w]
        if xs:
            w = list(w) + xs
        deps = self._deps(r, w)
        self._wait(ename, deps)
        E = self.E[ename]
        ins = fn(E["eng"])
        E["cnt"] += 1
        ins.then_inc(self.sem[ename], 1)
        self._commit((ename, E["cnt"]), r, w)

    def dma(self, q, out, in_, r=(), w=()):
        deps = self._deps(r, w)
        Q = self.dq[q]
        n = Q["n"]
        nm = Q["sems"][n % NDS]
        use = n // NDS
        if use > 0:
            deps[nm] = max(deps.get(nm, 0), 16 * use)
        self._wait(q, deps)
        ins = self.E[q]["eng"].dma_start(out=out, in_=in_)
        ins.then_inc(self.sem[nm], 16)
        Q["n"] += 1
        self._commit((nm, 16 * (use + 1)), r, w)

    def coll(self, src, dst, r=(), w=()):
        deps = self._deps(r, w)
        self._wait("pool", deps)
        ins = self.nc.gpsimd.collective_compute("AllGather", ALU.bypass, replica_groups=[[0, 1, 2, 3], [4, 5, 6, 7]],
                                                ins=[src], outs=[dst])
        ins.then_inc(self.sem["cc"])
        self.ncc += 1
        self._commit(("cc", self.ncc), r, w)

    def barrier(self):
        deps = {}
        for q, Q in self.dq.items():
            for i, nm in enumerate(Q["sems"]):
                if Q["n"] > i:
                    deps[nm] = 16 * ((Q["n"] - i + NDS - 1) // NDS)
        for name, E in self.E.items():
            if E["cnt"]:
                deps[name] = E["cnt"]
        for name in self.E:
            d = {k: v for k, v in deps.items() if k != name}
            E = self.E[name]
            for sname, v in d.items():
                if E["seen"].get(sname, 0) >= v:
                    continue
                E["eng"].wait_ge(self.sem[sname], v)
                E["seen"][sname] = v

    def finish(self):
        deps = {}
        for q, Q in self.dq.items():
            for i, nm in enumerate(Q["sems"]):
                if Q["n"] > i:
                    deps[nm] = 16 * ((Q["n"] - i + NDS - 1) // NDS)
        for name, E in self.E.items():
            if name != "sp" and E["cnt"]:
                deps[name] = E["cnt"]
        if self.ncc:
            deps["cc"] = self.ncc
        self._wait("sp", deps)


FM_AQ, FM_AK, FM_BQ, FM_BK = 0, 1, 2, 3
G_AG, G_BG, G_CQ, G_CK, G_CV, G_CZ, G_DQ, G_DF, G_DG = range(9)
NQK = 4
NFM32 = 9
NFM = NQK + NFM32
NTM = 512
SILU_GROUPS = (G_AG, G_BG, G_CZ, G_DG)


def phase_proj(nc, S, T, xtile, wfm, wtm, gain16, scr_qk, scr_fm, scr_v, scr_tm, scr_bd, scr_h, lay=0):
    TT = 512
    NT = T // TT
    with contextlib.ExitStack() as es:
        sb = lambda n, s, d: es.enter_context(nc.sbuf_tensor(uq(n), s, d))
        ps = lambda n: (S.psum_keys.add(n), es.enter_context(nc.psum_tensor(uq(n), [128, 512], F32)))[1]
        w_fm = sb("p_wfm", [128, 16, NFM * 128], BF16)
        w_tm = sb("p_wtm", [128, 16, NTM + 2], BF16)
        wst = [sb("p_wst%d" % i, [128, NFM * 128], F32) for i in range(2)]
        gain = sb("p_gain", [128, 16], F32)
        ones = sb("p_ones", [128, 128], BF16)
        xs = [sb("p_xs0", [128, 16, TT], F32)] * 2
        sq = sb("p_sq", [128, 16, TT], BF16)
        rstd = sb("p_rstd", [128, TT], F32)
        hT = sb("p_hT", [128, 16, TT], BF16)
        o_qk = [sb("p_oqk%d" % i, [128, NQK, TT], BF16) for i in range(2)]
        o_fm = [sb("p_ofm%d" % i, [128, NFM32, TT], F32) for i in range(2)]
        o_tm = [sb("p_otm%d" % i, [128, 4, 256], F32) for i in range(2)]
        o_v = [sb("p_ov%d" % i, [128, 4, 256], BF16) for i in range(2)]
        o_bd = [sb("p_obd%d" % i, [128, 4, 2], F32) for i in range(2)]
        p_ssq = ps("p_pssq")
        p_mm = [ps("p_pmm%d" % i) for i in range(4)]
        p_bd = ps("p_pbd")

        S.op("dve", lambda e: e.memset(ones[:], 1.0 / D), w=["p_ones"])
        S.dma("sp", gain[:], gain16, w=["p_gain"])
        wfm_v = wfm.rearrange("(c p) n -> c p n", p=128)
        wtm_v = wtm.rearrange("(c p) n -> c p n", p=128)
        for c in range(16):
            st = wst[c % 2]
            k = "p_wst%d" % (c % 2)
            S.dma("sp", st[:, :], wfm_v[c], w=[k])
            S.op("act" if c % 2 else "dve",
                 (lambda e, st=st, c=c: e.activation(out=w_fm[:, c, :], in_=st[:, :], func=AF.Copy))
                 if c % 2 else
                 (lambda e, st=st, c=c: e.tensor_copy(out=w_fm[:, c, :], in_=st[:, :])),
                 r=[k], w=["p_wfm"])
        for c in range(16):
            st = wst[c % 2]
            k = "p_wst%d" % (c % 2)
            S.dma("sp", st[:, 0:NTM + 2], wtm_v[c], w=[k])
            S.op("dve", lambda e, st=st, c=c: e.tensor_copy(out=w_tm[:, c, :], in_=st[:, 0:NTM + 2]),
                 r=[k], w=["p_wtm"])

        mmi = 0
        for t in range(NT):
            sl = t % 2
            t0 = t * TT
            kx = "p_xs0"
            for c4 in range(4):
                S.dma("sp", xs[sl][:, 4 * c4:4 * c4 + 4, :], xtile(t, c4), r=[("xdst", lay - 1, t)], w=[kx])
            S.op("act", lambda e: e.activation(out=sq[:], in_=xs[sl][:], func=AF.Square),
                 r=[kx], w=["p_sq"])
            for c in range(16):
                S.op("pe", lambda e, c=c: e.matmul(p_ssq[:], ones[:], sq[:, c, :], start=(c == 0), stop=(c == 15)),
                     r=["p_sq", "p_ones"], w=["p_pssq"])
            S.op("act", lambda e: e.activation(out=rstd[:], in_=p_ssq[:], func=AF.Sqrt, bias=1e-6, scale=1.0),
                 r=["p_pssq"], w=["p_rstd"])
            S.op("dve", lambda e: e.reciprocal(out=rstd[:], in_=rstd[:]), r=["p_rstd"], w=["p_rstd"])
            for c in range(16):
                S.op("dve", lambda e, c=c: e.scalar_tensor_tensor(
                    out=hT[:, c, :], in0=xs[sl][:, c, :], scalar=gain[:, c:c + 1], in1=rstd[:],
                    op0=ALU.mult, op1=ALU.mult),
                    r=[kx, "p_gain", "p_rstd"], w=["p_hT"])
            S.dma("pool", scr_h[t], hT[:], r=["p_hT"], w=["scr_h"])
            for g in range(NFM):
                pm = p_mm[mmi % 4]
                kp = "p_pmm%d" % (mmi % 4)
                mmi += 1
                for c in range(16):
                    S.op("pe", lambda e, c=c, g=g, pm=pm: e.matmul(
                        pm[:], w_fm[:, c, g * 128:(g + 1) * 128], hT[:, c, :], start=(c == 0), stop=(c == 15)),
                        r=["p_wfm", "p_hT"], w=[kp])
                if g < NQK:
                    S.op("dve", lambda e, g=g, pm=pm: e.tensor_copy(out=o_qk[sl][:, g, :], in_=pm[:]),
                         r=[kp], w=["p_oqk%d" % sl])
                else:
                    gg = g - NQK
                    fn = AF.Silu if gg in SILU_GROUPS else AF.Copy
                    S.op("act", lambda e, gg=gg, pm=pm, fn=fn: e.activation(out=o_fm[sl][:, gg, :], in_=pm[:], func=fn),
                         r=[kp], w=["p_ofm%d" % sl])
            S.dma("pool", scr_qk.rearrange("g p t -> p g t")[:, :, t0:t0 + TT], o_qk[sl][:],
                  r=["p_oqk%d" % sl], w=["scr_qk"])
            S.dma("pool", scr_fm.rearrange("g p t -> p g t")[:, :, t0:t0 + TT], o_fm[sl][:],
                  r=["p_ofm%d" % sl], w=["scr_fm"])
            for s4 in range(4):
                pm = p_mm[mmi % 4]
                kp = "p_pmm%d" % (mmi % 4)
                mmi += 1
                for c in range(16):
                    S.op("pe", lambda e, c=c, pm=pm, s4=s4: e.matmul(
                        pm[:], hT[:, c, s4 * 128:(s4 + 1) * 128], w_tm[:, c, 0:NTM], start=(c == 0), stop=(c == 15)),
                        r=["p_wtm", "p_hT"], w=[kp])
                S.op("dve", lambda e, pm=pm, s4=s4: e.tensor_copy(out=o_v[sl][:, s4, :], in_=pm[:, 0:256]),
                     r=[kp], w=["p_ov%d" % sl])
                S.op("act", lambda e, pm=pm, s4=s4: e.activation(out=o_tm[sl][:, s4, :], in_=pm[:, 256:512], func=AF.Copy),
                     r=[kp], w=["p_otm%d" % sl])
                for c in range(16):
                    S.op("pe", lambda e, c=c, s4=s4: e.matmul(
                        p_bd[:, 0:2], hT[:, c, s4 * 128:(s4 + 1) * 128], w_tm[:, c, NTM:NTM + 2],
                        start=(c == 0), stop=(c == 15)),
                        r=["p_wtm", "p_hT"], w=["p_pbd"])
                S.op("dve", lambda e, s4=s4: e.tensor_copy(out=o_bd[sl][:, s4, :], in_=p_bd[:, 0:2]),
                     r=["p_pbd"], w=["p_obd%d" % sl])
            S.dma("pool", scr_tm[t0:t0 + TT, :].rearrange("(s p) c -> p s c", p=128), o_tm[sl][:],
                  r=["p_otm%d" % sl], w=["scr_tm"])
            S.dma("pool", scr_v[t0:t0 + TT, :].rearrange("(s p) c -> p s c", p=128), o_v[sl][:],
                  r=["p_ov%d" % sl], w=["scr_v"])
            S.dma("pool", scr_bd[t0:t0 + TT, :].rearrange("(s p) c -> p s c", p=128), o_bd[sl][:],
                  r=["p_obd%d" % sl], w=["scr_bd"])
        S.barrier()


def phase_attn(nc, S, T, kind, scr_qk, scr_fm, scr_v, btab, mtab, aux, yT):
    isA = kind == "A"
    NB = T // 128
    NG = T // 512
    qg, kg = (FM_AQ, FM_AK) if isA else (FM_BQ, FM_BK)
    vcol = 0 if isA else 128
    dv = 64 if isA else 128
    ntab = 2 if isA else 1
    W = btab.shape[2]
    dmax = 16 if isA else 12
    pf = "a%s_" % kind
    with contextlib.ExitStack() as es:
        sb = lambda n, s, d: es.enter_context(nc.sbuf_tensor(uq(pf + n), s, d))
        ps = lambda n, m=128: (S.psum_keys.add(pf + n), es.enter_context(nc.psum_tensor(uq(pf + n), [m, 512], F32)))[1]
        K = lambda n: pf + n
        qT = sb("qT", [128, T], BF16)
        kT = sb("kT", [128, T], BF16)
        v = sb("v", [128, NB, 128], BF16)
        wgt = sb("wgt", [128, ntab, W], F32)
        mst = sb("mst", [128, W], F32)
        ones = sb("ones", [128, 128], BF16)
        e32 = [sb("e32_%d" % i, [128, 512], F32) for i in range(4)]
        pT = [sb("pT%d" % i, [128, 512], BF16) for i in range(8)]
        rr = sb("rr", [128, 512], F32)
        dacc = [sb("dacc%d" % i, [128, 512], F32) for i in range(2)]
        dhi = sb("dhi", [128, 512], BF16)
        dlo = sb("dlo", [128, 512], BF16)
        o1 = sb("o1", [128, 512], F32)
        o2 = sb("o2", [128, 512], F32)
        gt = [sb("gt%d" % i, [128, 512], F32) for i in range(2)]
        yb = [sb("yb%d" % i, [128, 512], BF16) for i in range(2)]
        gtA = [sb("gtA%d" % i, [64, 2, 512], F32) for i in range(2)] if isA else None
        ybA = [sb("ybA%d" % i, [64, 2, 512], BF16) for i in range(2)] if isA else None
        p_s = [ps("ps%d" % i) for i in range(4)]
        p_o = [ps("po%d" % i, dv) for i in range(2)]
        p_d = [ps("pd%d" % i, dv) for i in range(2)]
        S.op("dve", lambda e: e.memset(ones[:], 1.0), w=[K("ones")])
        for c in range(0, T, 4096):
            c1 = min(T, c + 4096)
            S.dma("sp", qT[:, c:c1], scr_qk[qg][:, c:c1], r=["scr_qk"], w=[K("qT")])
            S.dma("sp", kT[:, c:c1], scr_qk[kg][:, c:c1], r=["scr_qk"], w=[K("kT")])
        for n0 in range(0, NB, 8):
            S.dma("sp", v[:, n0:n0 + 8, :],
                  scr_v[n0 * 128:(n0 + 8) * 128, vcol:vcol + 128].rearrange("(n p) c -> p n c", p=128),
                  r=["scr_v"], w=[K("v")])
        for i in range(ntab):
            S.dma("sp", wgt[:, i, :], btab[i], w=[K("wgt")])
            S.dma("sp", mst[:], mtab[i], w=[K("mst")])
            S.op("act", lambda e, i=i: e.activation(out=wgt[:, i, :], in_=wgt[:, i, :], func=AF.Exp),
                 r=[K("wgt")], w=[K("wgt")])
            S.op("dve", lambda e, i=i: e.tensor_tensor(out=wgt[:, i, :], in0=wgt[:, i, :], in1=mst[:], op=ALU.mult),
                 r=[K("wgt"), K("mst")], w=[K("wgt")])
        if not isA:
            b31 = sb("b31", [128, 1], F32)
            dl = sb("dl", [128, 256], F32)
            tmp = sb("tmp", [128, 128], F32)
            ss = sb("ss", [128, 2], F32)
            neglam = sb("neglam", [128, 1], F32)
            gcol = sb("gcol", [128, 1], F32)
            ones_n = sb("ones_n", [128, 128], BF16)
            sqb = sb("sqb", [128, 512], BF16)
            rstd = sb("rstd", [128, 512], F32)
            p_q = p_s[3]
            lam_init = aux["lam_init"]
            S.op("dve", lambda e: e.memset(ones_n[:], 1.0 / 128), w=[K("ones_n")])
            S.dma("sp", b31[:], aux["b31"], w=[K("b31")])
            S.dma("sp", dl[:], aux["dl"], w=[K("dl")])
            S.dma("sp", gcol[:], aux["subln"], w=[K("gcol")])
            S.op("dve", lambda e: e.tensor_tensor(out=tmp[:, 0:64], in0=dl[:, 0:64], in1=dl[:, 64:128], op=ALU.mult),
                 r=[K("dl")], w=[K("tmp")])
            S.op("dve", lambda e: e.tensor_tensor(out=tmp[:, 64:128], in0=dl[:, 128:192], in1=dl[:, 192:256], op=ALU.mult),
                 r=[K("dl")], w=[K("tmp")])
            S.op("dve", lambda e: e.reduce_sum(out=ss[:, 0:1], in_=tmp[:, 0:64], axis=mybir.AxisListType.X),
                 r=[K("tmp")], w=[K("ss")])
            S.op("dve", lambda e: e.reduce_sum(out=ss[:, 1:2], in_=tmp[:, 64:128], axis=mybir.AxisListType.X),
                 r=[K("tmp")], w=[K("ss")])
            S.op("act", lambda e: e.activation(out=ss[:], in_=ss[:], func=AF.Exp), r=[K("ss")], w=[K("ss")])
            S.op("dve", lambda e: e.tensor_tensor(out=neglam[:], in0=ss[:, 1:2], in1=ss[:, 0:1], op=ALU.subtract),
                 r=[K("ss")], w=[K("neglam")])
            S.op("dve", lambda e: e.tensor_scalar(out=neglam[:], in0=neglam[:], scalar1=-lam_init, scalar2=None, op0=ALU.add),
                 r=[K("neglam")], w=[K("neglam")])
            S.op("dve", lambda e: e.tensor_scalar(out=gcol[:], in0=gcol[:], scalar1=1.0 - lam_init, scalar2=None, op0=ALU.mult),
                 r=[K("gcol")], w=[K("gcol")])

        step = 0
        for G in range(NG):
            q0 = G * 512
            gs = G % 2
            ggrp = G_AG if isA else G_BG
            if isA:
                for h in range(2):
                    S.dma("sp", gtA[gs][:, h, :],
                          scr_fm[ggrp][64 * h:64 * h + 64, q0:q0 + 512], r=["scr_fm"], w=[K("gt%d" % gs)])
            else:
                S.dma("sp", gt[gs][:], scr_fm[ggrp][:, q0:q0 + 512], r=["scr_fm"], w=[K("gt%d" % gs)])
            kb_lo = max(0, 4 * G - 16) if isA else 0
            kb_hi = 4 * G + 3
            steps = [(kb, sub) for kb in range(kb_lo, kb_hi + 1) for sub in range(2)]

            def emit_qk(i):
                kb, sub = steps[i]
                st_ = step0 + i
                psl, ptl = st_ % 4, st_ % 8
                delta0 = 4 * G - kb
                near = delta0 <= dmax
                j0 = 128 * delta0 + 384
                S.op("pe", lambda e: e.matmul(p_s[psl][:], kT[64 * sub:64 * sub + 64, kb * 128:(kb + 1) * 128],
                                              qT[64 * sub:64 * sub + 64, q0:q0 + 512], start=True, stop=True),
                     r=[K("kT"), K("qT")], w=[K("ps%d" % psl)])
                if near:
                    S.op("act", lambda e: e.activation(out=e32[psl][:], in_=p_s[psl][:], func=AF.Exp, scale=0.125),
                         r=[K("ps%d" % psl)], w=[K("e32_%d" % psl)])
                    ti = sub if isA else 0
                    S.op("dve" if (st_ % 2) else "pool",
                         lambda e: e.tensor_tensor(out=pT[ptl][:], in0=e32[psl][:], in1=wgt[:, ti, j0:j0 + 512], op=ALU.mult),
                         r=[K("e32_%d" % psl), K("wgt")], w=[K("pT%d" % ptl)])
                else:
                    S.op("act", lambda e: e.activation(out=pT[ptl][:], in_=p_s[psl][:], func=AF.Exp,
                                                       bias=b31[:, 0:1], scale=0.125),
                         r=[K("ps%d" % psl), K("b31")], w=[K("pT%d" % ptl)])

            def emit_pv(i):
                kb, sub = steps[i]
                ptl = (step0 + i) % 8
                first = kb == kb_lo
                last = kb == kb_hi
                vs = v[:, kb, 64 * sub:64 * sub + 64] if isA else v[:, kb, :]
                S.op("pe", lambda e: e.matmul(p_o[sub][:], vs, pT[ptl][:], start=first, stop=last),
                     r=[K("v"), K("pT%d" % ptl)], w=[K("po%d" % sub)])
                en = "pool" if ((step0 + i) % 2) else "dve"
                if first:
                    S.op(en, lambda e: e.tensor_copy(out=dacc[sub][:], in_=pT[ptl][:]),
                         r=[K("pT%d" % ptl)], w=[K("dacc%d" % sub)])
                else:
                    S.op(en, lambda e: e.tensor_tensor(out=dacc[sub][:], in0=dacc[sub][:], in1=pT[ptl][:], op=ALU.add),
                         r=[K("pT%d" % ptl), K("dacc%d" % sub)], w=[K("dacc%d" % sub)])

            step0 = step
            nbu = len(steps) // 4
            for j in (0, 2, 1, 3):
                emit_qk(j)
            for bi in range(nbu):
                nx = bi + 1 < nbu
                if nx:
                    emit_qk(4 * bi + 4)
                    emit_qk(4 * bi + 6)
                emit_pv(4 * bi + 0)
                emit_pv(4 * bi + 2)
                if nx:
                    emit_qk(4 * bi + 5)
                    emit_qk(4 * bi + 7)
                emit_pv(4 * bi + 1)
                emit_pv(4 * bi + 3)
            step += len(steps)
            for sub in range(2):
                S.op("act", lambda e: e.activation(out=dhi[:], in_=dacc[sub][:], func=AF.Copy), r=[K("dacc%d" % sub)], w=[K("dhi")])
                S.op("dve", lambda e: e.tensor_tensor(out=dlo[:], in0=dacc[sub][:], in1=dhi[:], op=ALU.subtract),
                     r=[K("dacc%d" % sub), K("dhi")], w=[K("dlo")])
                S.op("pe", lambda e: e.matmul(p_d[sub][:], ones[:, 0:dv], dhi[:], start=True, stop=False),
                     r=[K("ones"), K("dhi")], w=[K("pd%d" % sub)])
                S.op("pe", lambda e: e.matmul(p_d[sub][:], ones[:, 0:dv], dlo[:], start=False, stop=True),
                     r=[K("ones"), K("dlo")], w=[K("pd%d" % sub)])
            if isA:
                for h in range(2):
                    S.op("dve", lambda e: e.reciprocal(out=rr[0:64, :], in_=p_d[h][:]), r=[K("pd%d" % h)], w=[K("rr")])
                    S.op("dve", lambda e: e.tensor_tensor(out=o1[0:64, :], in0=p_o[h][:], in1=rr[0:64, :], op=ALU.mult),
                         r=[K("po%d" % h), K("rr")], w=[K("o1")])
                    S.op("dve", lambda e: e.tensor_tensor(out=ybA[gs][:, h, :], in0=o1[0:64, :], in1=gtA[gs][:, h, :], op=ALU.mult),
                         r=[K("o1"), K("gt%d" % gs)], w=[K("yb%d" % gs)])
                S.dma("pool", yT[0:128, q0:q0 + 512].rearrange("(h p) t -> p h t", p=64), ybA[gs][:],
                      r=[K("yb%d" % gs)], w=[("ysrc", q0 // 1024)])
            else:
                S.op("dve", lambda e: e.reciprocal(out=rr[:], in_=p_d[0][:]), r=[K("pd0")], w=[K("rr")])
                S.op("dve", lambda e: e.tensor_tensor(out=o1[:], in0=p_o[0][:], in1=rr[:], op=ALU.mult),
                     r=[K("po0"), K("rr")], w=[K("o1")])
                S.op("dve", lambda e: e.reciprocal(out=rr[:], in_=p_d[1][:]), r=[K("pd1")], w=[K("rr")])
                S.op("dve", lambda e: e.tensor_tensor(out=o2[:], in0=p_o[1][:], in1=rr[:], op=ALU.mult),
                     r=[K("po1"), K("rr")], w=[K("o2")])
                S.op("dve", lambda e: e.scalar_tensor_tensor(out=o1[:], in0=o2[:], scalar=neglam[:, 0:1], in1=o1[:],
                                                             op0=ALU.mult, op1=ALU.add),
                     r=[K("o1"), K("o2"), K("neglam")], w=[K("o1")])
                S.op("act", lambda e: e.activation(out=sqb[:], in_=o1[:], func=AF.Square), r=[K("o1")], w=[K("sqb")])
                S.op("pe", lambda e: e.matmul(p_q[:], ones_n[:], sqb[:], start=True, stop=True),
                     r=[K("ones_n"), K("sqb")], w=[K("ps3")])
                S.op("act", lambda e: e.activation(out=rstd[:], in_=p_q[:], func=AF.Sqrt, bias=1e-6, scale=1.0),
                     r=[K("ps3")], w=[K("rstd")])
                S.op("dve", lambda e: e.reciprocal(out=rstd[:], in_=rstd[:]), r=[K("rstd")], w=[K("rstd")])
                S.op("dve", lambda e: e.tensor_tensor(out=o1[:], in0=o1[:], in1=rstd[:], op=ALU.mult),
                     r=[K("o1"), K("rstd")], w=[K("o1")])
                S.op("dve", lambda e: e.scalar_tensor_tensor(out=yb[gs][:], in0=o1[:], scalar=gcol[:, 0:1], in1=gt[gs][:],
                                                             op0=ALU.mult, op1=ALU.mult),
                     r=[K("o1"), K("gcol"), K("gt%d" % gs)], w=[K("yb%d" % gs)])
                S.dma("pool", yT[128:256, q0:q0 + 512], yb[gs][:], r=[K("yb%d" % gs)], w=[("ysrc", q0 // 1024)])
        S.barrier()


def t5_bucket_np(dist):
    n = np.maximum(dist, 0)
    max_exact = 16
    nf = np.maximum(n, max_exact).astype(np.float32)
    large = max_exact + (np.log(nf / max_exact) / math.log(2048 / max_exact) * (32 - max_exact)).astype(np.int32)
    large = np.minimum(large, 31)
    return np.where(n < max_exact, n, large)


def attn_tables(rel_bias, kind, g):
    W = 2944 if kind == "A" else 2432
    ki = np.arange(128)[:, None]
    d = (np.arange(W)[None, :] - 384) - ki
    bk = t5_bucket_np(d)
    if kind == "A":
        mult = ((d >= 0) & (d <= 128)).astype(np.float32) \
            + ((d >= 0) & (d <= 512) & (d % 4 == 0)).astype(np.float32) \
            + ((d >= 0) & (d <= 2048) & (d % 16 == 0)).astype(np.float32)
        heads = [2 * g, 2 * g + 1]
    else:
        mult = (d >= 0).astype(np.float32)
        heads = [8 + g]
    bt = np.stack([np.where(mult > 0, rel_bias[:, h][bk], np.float32(0)) for h in heads]).astype(np.float32)
    mt = np.stack([mult for _ in heads]).astype(np.float32)
    return np.ascontiguousarray(bt), np.ascontiguousarray(mt)


def lb_compute(S, K, layer, z, e, w1, lb, oml, negoml, sl):
    S.op("act", lambda en: en.activation(out=e[:], in_=z[:], func=AF.Exp), r=[K("lbz")], w=[K("lbe")])
    S.op("dve", lambda en: en.tensor_tensor(out=w1[:], in0=e[:, 0, :], in1=e[:, 1, :], op=ALU.add), r=[K("lbe")], w=[K("lbw")])
    S.op("dve", lambda en: en.reciprocal(out=w1[:], in_=w1[:]), r=[K("lbw")], w=[K("lbw")])
    S.op("dve", lambda en: en.tensor_tensor(out=e[:, 0, :], in0=e[:, 0, :], in1=w1[:], op=ALU.mult), r=[K("lbe"), K("lbw")], w=[K("lbe")])
    S.op("dve", lambda en: en.tensor_tensor(out=e[:, 1, :], in0=e[:, 1, :], in1=w1[:], op=ALU.mult), r=[K("lbe"), K("lbw")], w=[K("lbe")])
    if layer == 0:
        S.op("dve", lambda en: en.tensor_tensor(out=lb[:], in0=e[:, 0, :], in1=e[:, 0, :], op=ALU.subtract), r=[K("lbe")], w=[K("lb")])
    else:
        S.op("dve", lambda en: en.tensor_tensor(out=lb[:], in0=e[:, 0, :], in1=e[:, 1, :], op=ALU.add), r=[K("lbe")], w=[K("lb")])
        S.op("dve", lambda en: en.tensor_tensor(out=lb[:], in0=lb[:], in1=e[:, 0, :], op=ALU.subtract), r=[K("lbe"), K("lb")], w=[K("lb")])
    S.op("dve", lambda en: en.tensor_scalar(out=lb[:], in0=lb[:], scalar1=0.0, scalar2=1.0, op0=ALU.max, op1=ALU.min),
         r=[K("lb")], w=[K("lb")])
    S.op("dve", lambda en: en.tensor_scalar(out=oml[:], in0=lb[:], scalar1=-1.0, scalar2=1.0, op0=ALU.mult, op1=ALU.add),
         r=[K("lb")], w=[K("oml")])
    S.op("dve", lambda en: en.tensor_scalar(out=negoml[:], in0=lb[:], scalar1=1.0, scalar2=-1.0, op0=ALU.mult, op1=ALU.add),
         r=[K("lb")], w=[K("oml")])


def phase_hgrn(nc, S, T, layer, scr_fm, scr_tm, consts, aux, yT, on_tile=None):
    NB = T // 128
    NG = T // 512
    pf = "d_"
    with contextlib.ExitStack() as es:
        sb = lambda n, s, d: es.enter_context(nc.sbuf_tensor(uq(pf + n), s, d))
        ps = lambda n: (S.psum_keys.add(pf + n), es.enter_context(nc.psum_tensor(uq(pf + n), [128, 512], F32)))[1]
        K = lambda n: pf + n
        triu = sb("triu", [128, 128], F32)
        ones32 = sb("ones32", [128, 128], F32)
        ones_n = sb("ones_n", [128, 128], BF16)
        zc = sb("zc", [128, 2, 1], F32); ec = sb("ec", [128, 2, 1], F32); wc = sb("wc", [128, 1], F32)
        lbc = sb("lbc", [128, 1], F32); omlc = sb("omlc", [128, 1], F32); nomlc = sb("nomlc", [128, 1], F32)
        zr = sb("zr", [128, 2, 128], F32); er = sb("er", [128, 2, 128], F32); wr = sb("wr", [128, 128], F32)
        lbr = sb("lbr", [128, 128], F32); omlr = sb("omlr", [128, 128], F32); nomlr = sb("nomlr", [128, 128], F32)
        gcol = sb("gcol", [128, 1], F32)
        tm = [sb("tm%d" % i, [128, 4, 256], F32) for i in range(2)]
        fT = [sb("fT%d" % i, [128, 512], F32) for i in range(2)]
        qT = [sb("qT%d" % i, [128, 512], F32) for i in range(2)]
        gt = [sb("gt%d" % i, [128, 512], F32) for i in range(2)]
        sg = sb("sg", [128, 4, 128], F32)
        logf = sb("logf", [128, 4, 128], F32)
        ktm = sb("ktm", [128, 4, 128], F32)
        vbf = sb("vbf", [128, 4, 128], BF16)
        kTt = sb("kTt", [128, 512], F32)
        bct = sb("bct", [128, 128], F32)
        bcs = sb("bcs", [128, 128], F32)
        aq = sb("aq", [128, 128], F32)
        eq1 = sb("eq1", [128, 128], F32)
        qtl = sb("qtl", [128, 128], BF16)
        q1 = sb("q1", [128, 128], BF16)
        ak = sb("ak", [128, 4, 128], F32)
        ktl = sb("ktl", [128, 4, 128], BF16)
        atb = sb("atb", [128, 128], BF16)
        dlt = sb("dlt", [128, 128], F32)
        kd = sb("kd", [128, 128], BF16)
        s32 = sb("s32", [128, 128], F32)
        sbf = sb("sbf", [128, 128], BF16)
        osb = sb("osb", [128, 512], F32)
        sqb = sb("sqb", [128, 512], BF16)
        rstd = sb("rstd", [128, 512], F32)
        yb = [sb("yb%d" % i, [128, 512], BF16) for i in range(2)]
        p_bc = ps("pbc"); p_bct = ps("pbct"); p_bcl = ps("pbcl"); p_a = ps("pa"); p_o = ps("po"); p_s = ps("pS"); p_q = ps("pq")

        S.dma("sp", triu[:], consts["triu"], w=[K("triu")])
        S.op("dve", lambda e: e.memset(ones32[:], 1.0), w=[K("ones32")])
        S.op("dve", lambda e: e.memset(ones_n[:], 1.0 / 128), w=[K("ones_n")])
        S.op("dve", lambda e: e.memset(s32[:], 0.0), w=[K("s32")])
        S.op("dve", lambda e: e.memset(sbf[:], 0.0), w=[K("sbf")])
        S.dma("sp", gcol[:], aux["hg_gain"], w=[K("gcol")])
        S.dma("sp", zc[:, :, 0], aux["lbz_col"], w=[K("c_lbz")])
        S.dma("sp", zr[:], aux["lbz_row"], w=[K("r_lbz")])
        lb_compute(S, lambda n: K("c_" + n), layer, zc, ec, wc, lbc, omlc, nomlc, None)
        lb_compute(S, lambda n: K("r_" + n), layer, zr, er, wr, lbr, omlr, nomlr, None)
        KC = lambda n: K("c_" + n)
        KR = lambda n: K("r_" + n)

        for G in range(NG):
            gs = G % 2
            q0 = G * 512
            S.dma("sp", tm[gs][:], scr_tm[q0:q0 + 512, :].rearrange("(s p) c -> p s c", p=128), r=["scr_tm"], w=[K("tm%d" % gs)])
            S.dma("sp", fT[gs][:], scr_fm[G_DF][:, q0:q0 + 512], r=["scr_fm"], w=[K("fT%d" % gs)])
            S.dma("sp", qT[gs][:], scr_fm[G_DQ][:, q0:q0 + 512], r=["scr_fm"], w=[K("qT%d" % gs)])
            S.dma("sp", gt[gs][:], scr_fm[G_DG][:, q0:q0 + 512], r=["scr_fm"], w=[K("gt%d" % gs)])
            S.op("act", lambda e: e.activation(out=sg[:], in_=tm[gs][:, :, 0:128], func=AF.Sigmoid), r=[K("tm%d" % gs)], w=[K("sg")])
            for s4 in range(4):
                S.op("dve", lambda e: e.tensor_tensor(out=sg[:, s4, :], in0=sg[:, s4, :], in1=omlr[:], op=ALU.mult),
                     r=[K("sg"), KR("oml")], w=[K("sg")])
                S.op("dve", lambda e: e.tensor_tensor(out=ktm[:, s4, :], in0=omlr[:], in1=sg[:, s4, :], op=ALU.subtract),
                     r=[K("sg"), KR("oml")], w=[K("ktm")])
                S.op("dve", lambda e: e.tensor_tensor(out=sg[:, s4, :], in0=sg[:, s4, :], in1=lbr[:], op=ALU.add),
                     r=[K("sg"), KR("lb")], w=[K("sg")])
            S.op("act", lambda e: e.activation(out=logf[:], in_=sg[:], func=AF.Ln), r=[K("sg")], w=[K("logf")])
            S.op("pool", lambda e: e.tensor_copy(out=vbf[:], in_=tm[gs][:, :, 128:256]), r=[K("tm%d" % gs)], w=[K("vbf")])
            S.op("act", lambda e: e.activation(out=kTt[:], in_=fT[gs][:], func=AF.Sigmoid), r=[K("fT%d" % gs)], w=[K("kTt")])
            S.op("dve", lambda e: e.tensor_scalar(out=kTt[:], in0=kTt[:], scalar1=nomlc[:, 0:1], scalar2=omlc[:, 0:1],
                                                   op0=ALU.mult, op1=ALU.add),
                 r=[K("kTt"), KC("oml")], w=[K("kTt")])
            for s4 in range(4):
                c0 = s4 * 128
                lf = logf[:, s4, :]
                S.op("pe", lambda e: e.matmul(p_bc[:, 0:128], triu[:], lf, start=True, stop=True), r=[K("triu"), K("logf")], w=[K("pbc")])
                S.op("pe", lambda e: e.matmul(p_bct[:, 0:128], lf, triu[:], start=True, stop=True), r=[K("triu"), K("logf")], w=[K("pbct")])
                S.op("pe", lambda e: e.matmul(p_bcl[:, 0:128], ones32[:], lf, start=True, stop=True), r=[K("ones32"), K("logf")], w=[K("pbcl")])
                S.op("dve", lambda e: e.tensor_copy(out=bct[:], in_=p_bct[:, 0:128]), r=[K("pbct")], w=[K("bct")])
                S.op("act", lambda e: e.activation(out=bcs[:], in_=p_bc[:, 0:128], func=AF.Copy), r=[K("pbc")], w=[K("bcs")])
                S.op("act", lambda e: e.activation(out=eq1[:], in_=bct[:], func=AF.Exp), r=[K("bct")], w=[K("eq1")])
                for I in range(4):
                    if I == 0:
                        S.op("dve", lambda e: e.tensor_copy(out=aq[:, 0:32], in_=bct[:, 0:32]), r=[K("bct")], w=[K("aq")])
                        S.op("dve", lambda e: e.tensor_scalar(out=ak[:, 0, :], in0=bct[:], scalar1=-60.0, scalar2=None, op0=ALU.max),
                             r=[K("bct")], w=[K("ak")])
                    else:
                        rc = bct[:, 32 * I - 1:32 * I]
                        S.op("dve", lambda e: e.tensor_scalar(out=aq[:, 32 * I:32 * I + 32], in0=bct[:, 32 * I:32 * I + 32],
                                                               scalar1=rc, scalar2=None, op0=ALU.subtract),
                             r=[K("bct")], w=[K("aq")])
                        S.op("dve", lambda e: e.tensor_scalar(out=ak[:, I, :], in0=bct[:], scalar1=rc, scalar2=-60.0,
                                                               op0=ALU.subtract, op1=ALU.max),
                             r=[K("bct")], w=[K("ak")])
                S.op("act", lambda e: e.activation(out=aq[:], in_=aq[:], func=AF.Exp), r=[K("aq")], w=[K("aq")])
                S.op("act", lambda e: e.activation(out=ak[:], in_=ak[:], func=AF.Exp, scale=-1.0), r=[K("ak")], w=[K("ak")])
                S.op("dve", lambda e: e.tensor_tensor(out=qtl[:], in0=aq[:], in1=qT[gs][:, c0:c0 + 128], op=ALU.mult),
                     r=[K("aq"), K("qT%d" % gs)], w=[K("qtl")])
                S.op("dve", lambda e: e.tensor_tensor(out=q1[:], in0=eq1[:], in1=qT[gs][:, c0:c0 + 128], op=ALU.mult),
                     r=[K("eq1"), K("qT%d" % gs)], w=[K("q1")])
                for I in range(4):
                    S.op("pool" if I % 2 else "dve",
                         lambda e: e.tensor_tensor(out=ktl[:, I, :], in0=ak[:, I, :], in1=kTt[:, c0:c0 + 128], op=ALU.mult),
                         r=[K("ak"), K("kTt")], w=[K("ktl")])
                for I in range(4):
                    S.op("pe", lambda e: e.matmul(p_a[:, 32 * I:32 * I + 32], ktl[:, I, :], qtl[:, 32 * I:32 * I + 32],
                                                  start=True, stop=True),
                         r=[K("ktl"), K("qtl")], w=[K("pa")])
                S.op("dve", lambda e: e.tensor_tensor(out=atb[:], in0=p_a[:, 0:128], in1=triu[:], op=ALU.mult),
                     r=[K("pa"), K("triu")], w=[K("atb")])
                S.op("dve", lambda e: e.tensor_tensor(out=dlt[:], in0=p_bcl[:, 0:128], in1=bcs[:], op=ALU.subtract),
                     r=[K("pbcl"), K("bcs")], w=[K("dlt")])
                S.op("act", lambda e: e.activation(out=dlt[:], in_=dlt[:], func=AF.Exp), r=[K("dlt")], w=[K("dlt")])
                S.op("dve", lambda e: e.tensor_tensor(out=kd[:], in0=dlt[:], in1=ktm[:, s4, :], op=ALU.mult),
                     r=[K("dlt"), K("ktm")], w=[K("kd")])
                S.op("pe", lambda e: e.matmul(p_o[:, c0:c0 + 128], sbf[:], q1[:], start=True, stop=False),
                     r=[K("sbf"), K("q1")], w=[K("po")])
                S.op("pe", lambda e: e.matmul(p_o[:, c0:c0 + 128], vbf[:, s4, :], atb[:], start=False, stop=True),
                     r=[K("vbf"), K("atb")], w=[K("po")])
                S.op("pe", lambda e: e.matmul(p_s[:, 0:128], kd[:], vbf[:, s4, :], start=True, stop=True),
                     r=[K("kd"), K("vbf")], w=[K("pS")])
                S.op("dve", lambda e: e.scalar_tensor_tensor(out=s32[:], in0=s32[:], scalar=eq1[:, 127:128], in1=p_s[:, 0:128],
                                                             op0=ALU.mult, op1=ALU.add),
                     r=[K("s32"), K("eq1"), K("pS")], w=[K("s32")])
                S.op("act", lambda e: e.activation(out=sbf[:], in_=s32[:], func=AF.Copy), r=[K("s32")], w=[K("sbf")])
            S.op("dve", lambda e: e.tensor_copy(out=osb[:], in_=p_o[:]), r=[K("po")], w=[K("osb")])
            S.op("act", lambda e: e.activation(out=sqb[:], in_=osb[:], func=AF.Square), r=[K("osb")], w=[K("sqb")])
            S.op("pe", lambda e: e.matmul(p_q[:], ones_n[:], sqb[:], start=True, stop=True), r=[K("ones_n"), K("sqb")], w=[K("pq")])
            S.op("act", lambda e: e.activation(out=rstd[:], in_=p_q[:], func=AF.Sqrt, bias=1e-6, scale=1.0), r=[K("pq")], w=[K("rstd")])
            S.op("dve", lambda e: e.reciprocal(out=rstd[:], in_=rstd[:]), r=[K("rstd")], w=[K("rstd")])
            S.op("dve", lambda e: e.tensor_tensor(out=osb[:], in0=osb[:], in1=rstd[:], op=ALU.mult), r=[K("osb"), K("rstd")], w=[K("osb")])
            S.op("dve", lambda e: e.scalar_tensor_tensor(out=yb[gs][:], in0=osb[:], scalar=gcol[:, 0:1], in1=gt[gs][:],
                                                         op0=ALU.mult, op1=ALU.mult),
                 r=[K("osb"), K("gcol"), K("gt%d" % gs)], w=[K("yb%d" % gs)])
            S.dma("pool", yT[384:512, q0:q0 + 512], yb[gs][:], r=[K("yb%d" % gs)], w=[("ysrc", q0 // 1024)])
            if on_tile is not None:
                on_tile(G)
        S.barrier()


def phase_gdn(nc, S, T, scr_fm, scr_bd, consts, aux, yT):
    NB = T // 128
    NG = T // 512
    pf = "c_"
    with contextlib.ExitStack() as es:
        sb = lambda n, s, d: es.enter_context(nc.sbuf_tensor(uq(pf + n), s, d))
        ps = lambda n: (S.psum_keys.add(pf + n), es.enter_context(nc.psum_tensor(uq(pf + n), [128, 512], F32)))[1]
        K = lambda n: pf + n
        triu = sb("triu", [128, 128], F32)
        trius = sb("trius", [128, 128], F32)
        trils = sb("trils", [128, 128], F32)
        ident = sb("ident", [128, 128], F32)
        ones32 = sb("ones32", [128, 128], F32)
        ones_b = sb("ones_b", [128, 128], BF16)
        ones_n = sb("ones_n", [128, 128], BF16)
        cw = sb("cw", [128, 3, 4], F32)
        ab = sb("ab", [128, 2], F32)
        negA = sb("negA", [128, 1], F32)
        gcol = sb("gcol", [128, 1], F32)
        bd = sb("bd", [128, NB, 2], F32)
        beta = sb("beta", [128, NB], F32)
        nbeta = sb("nbeta", [128, NB], F32)
        gall = sb("gall", [128, NB], F32)
        Gc = sb("Gc", [128, NB], F32)
        bg = sb("bg", [128, NB], F32)
        xin = [sb("xin%d" % i, [128, 3, 515], F32) for i in range(2)]
        gt = [sb("gt%d" % i, [128, 512], F32) for i in range(2)]
        cv = sb("cv", [128, 3, 512], F32)
        sq = sb("sq", [128, 512], BF16)
        rs = sb("rs", [128, 512], F32)
        qTb = sb("qTb", [128, 512], BF16)
        kTb = sb("kTb", [128, 512], BF16)
        vTb = sb("vTb", [128, 512], BF16)
        identb = sb("identb", [128, 128], BF16)
        ktm = sb("ktm", [128, 128], F32)
        vtm = sb("vtm", [128, 128], F32)
        bv = sb("bv", [128, 128], F32)
        kbg = sb("kbg", [128, 128], F32)
        kdec = sb("kdec", [128, 128], BF16)
        gb = sb("gb", [128, 128], F32)
        grow = sb("grow", [128, 128], F32)
        x1 = sb("x1", [128, 128], F32)
        x2 = sb("x2", [128, 128], F32)
        gTi = sb("gTi", [128, 128], F32)
        gLs = sb("gLs", [128, 128], F32)
        eg = sb("eg", [128, 128], F32)
        Y = [sb("Y%d" % i, [128, 128], F32) for i in range(2)]
        YT = [sb("YT%d" % i, [128, 128], F32) for i in range(2)]
        XT = sb("XT", [128, 128], F32)
        yh = sb("yh", [128, 128], BF16)
        yl = sb("yl", [128, 128], BF16)
        yh32 = sb("yh32", [128, 128], F32)
        usb = sb("usb", [128, 128], F32)
        wT = sb("wT", [128, 128], BF16)
        aqk = sb("aqk", [128, 128], BF16)
        qdec = sb("qdec", [128, 128], BF16)
        cl = sb("cl", [128, 1], F32)
        vnew = sb("vnew", [128, 128], BF16)
        s32 = sb("s32", [128, 128], F32)
        sbf = sb("sbf", [128, 128], BF16)
        osb = sb("osb", [128, 512], F32)
        rstd = sb("rstd", [128, 512], F32)
        yb = [sb("yb%d" % i, [128, 512], BF16) for i in range(2)]
        p_ss = ps("pss"); p_tr = ps("ptr"); p_gk = ps("pgk"); p_yy = ps("pyy"); p_yt = ps("pyt"); p_yx = ps("pyx")
        p_o = ps("po"); p_s = ps("pS")

        S.dma("sp", triu[:], consts["triu"], w=[K("triu")])
        S.dma("sp", trius[:], consts["trius"], w=[K("trius")])
        S.dma("sp", trils[:], consts["trils"], w=[K("trils")])
        S.dma("sp", ident[:], consts["ident"], w=[K("ident")])
        S.op("dve", lambda e: e.memset(ones32[:], 1.0), w=[K("ones32")])
        S.op("dve", lambda e: e.tensor_copy(out=identb[:], in_=ident[:]), r=[K("ident")], w=[K("identb")])
        S.op("dve", lambda e: e.memset(ones_b[:], 1.0), w=[K("ones_b")])
        S.op("dve", lambda e: e.memset(ones_n[:], 1.0 / 128), w=[K("ones_n")])
        S.op("dve", lambda e: e.memset(s32[:], 0.0), w=[K("s32")])
        S.op("dve", lambda e: e.memset(sbf[:], 0.0), w=[K("sbf")])
        S.dma("sp", cw[:], aux["conv"], w=[K("cw")])
        S.dma("sp", ab[:], aux["a_dt"], w=[K("ab")])
        S.dma("sp", gcol[:], aux["dn_gain"], w=[K("gcol")])
        for n0 in range(0, NB, 16):
            n1 = min(NB, n0 + 16)
            S.dma("sp", bd[:, n0:n1, :], scr_bd[n0 * 128:n1 * 128, :].rearrange("(n p) c -> p n c", p=128),
                  r=["scr_bd"], w=[K("bd")])
        S.op("act", lambda e: e.activation(out=beta[:], in_=bd[:, :, 0], func=AF.Sigmoid), r=[K("bd")], w=[K("beta")])
        S.op("dve", lambda e: e.tensor_scalar(out=nbeta[:], in0=beta[:], scalar1=-1.0, scalar2=None, op0=ALU.mult),
             r=[K("beta")], w=[K("nbeta")])
        S.op("act", lambda e: e.activation(out=negA[:], in_=ab[:, 0:1], func=AF.Exp), r=[K("ab")], w=[K("negA")])
        S.op("dve", lambda e: e.tensor_scalar(out=negA[:], in0=negA[:], scalar1=-1.0, scalar2=None, op0=ALU.mult),
             r=[K("negA")], w=[K("negA")])
        S.op("act", lambda e: e.activation(out=gall[:], in_=bd[:, :, 1], func=AF.Exp, bias=ab[:, 1:2], scale=1.0),
             r=[K("bd"), K("ab")], w=[K("gall")])
        S.op("act", lambda e: e.activation(out=gall[:], in_=gall[:], func=AF.Ln, bias=1.0, scale=1.0),
             r=[K("gall")], w=[K("gall")])
        S.op("dve", lambda e: e.tensor_scalar(out=gall[:], in0=gall[:], scalar1=negA[:, 0:1], scalar2=None, op0=ALU.mult),
             r=[K("gall"), K("negA")], w=[K("gall")])
        S.op("pe", lambda e: e.matmul(p_ss[:, 0:NB], triu[:], gall[:], start=True, stop=True),
             r=[K("triu"), K("gall")], w=[K("pss")])
        S.op("dve", lambda e: e.tensor_copy(out=Gc[:], in_=p_ss[:, 0:NB]), r=[K("pss")], w=[K("Gc")])
        S.op("act", lambda e: e.activation(out=bg[:], in_=Gc[:], func=AF.Exp), r=[K("Gc")], w=[K("bg")])
        S.op("dve", lambda e: e.tensor_tensor(out=bg[:], in0=bg[:], in1=beta[:], op=ALU.mult), r=[K("bg"), K("beta")], w=[K("bg")])

        for G in range(NG):
            gs = G % 2
            q0 = G * 512
            kx = K("xin%d" % gs)
            for j, grp in enumerate((G_CQ, G_CK, G_CV)):
                if G == 0:
                    S.op("dve", lambda e: e.memset(xin[gs][:, j, 0:3], 0.0), w=[kx])
                    S.dma("sp", xin[gs][:, j, 3:515], scr_fm[grp][:, 0:512], r=["scr_fm"], w=[kx])
                else:
                    S.dma("sp", xin[gs][:, j, :], scr_fm[grp][:, q0 - 3:q0 + 512], r=["scr_fm"], w=[kx])
            S.dma("sp", gt[gs][:], scr_fm[G_CZ][:, q0:q0 + 512], r=["scr_fm"], w=[K("gt%d" % gs)])
            for j in range(3):
                en = "dve"
                S.op(en, lambda e: e.tensor_scalar(out=cv[:, j, :], in0=xin[gs][:, j, 3:515], scalar1=cw[:, j, 3:4],
                                                   scalar2=None, op0=ALU.mult), r=[kx, K("cw")], w=[K("cv%d" % j)])
                for tap in (2, 1, 0):
                    S.op(en, lambda e: e.scalar_tensor_tensor(out=cv[:, j, :], in0=xin[gs][:, j, tap:tap + 512],
                                                              scalar=cw[:, j, tap:tap + 1], in1=cv[:, j, :],
                                                              op0=ALU.mult, op1=ALU.add),
                         r=[kx, K("cw"), K("cv%d" % j)], w=[K("cv%d" % j)])
                S.op("act", lambda e: e.activation(out=cv[:, j, :], in_=cv[:, j, :], func=AF.Silu),
                     r=[K("cv%d" % j)], w=[K("cv%d" % j)])
            S.op("pool", lambda e: e.tensor_copy(out=vTb[:], in_=cv[:, 2, :]), r=[K("cv2")], w=[K("vTb")])
            for j in range(2):
                S.op("act", lambda e: e.activation(out=sq[:], in_=cv[:, j, :], func=AF.Square), r=[K("cv%d" % j)], w=[K("sq")])
                S.op("pe", lambda e: e.matmul(p_ss[:], ones_b[:], sq[:], start=True, stop=True), r=[K("ones_b"), K("sq")], w=[K("pss")])
                S.op("act", lambda e: e.activation(out=rs[:], in_=p_ss[:], func=AF.Sqrt, bias=1e-6, scale=1.0), r=[K("pss")], w=[K("rs")])
                S.op("dve", lambda e: e.reciprocal(out=rs[:], in_=rs[:]), r=[K("rs")], w=[K("rs")])
                if j == 0:
                    S.op("dve", lambda e: e.scalar_tensor_tensor(out=cv[:, 0, :], in0=cv[:, 0, :], scalar=128.0 ** -0.5, in1=rs[:],
                                                                 op0=ALU.mult, op1=ALU.mult),
                         r=[K("cv0"), K("rs")], w=[K("cv0")])
                    S.op("act", lambda e: e.activation(out=qTb[:], in_=cv[:, 0, :], func=AF.Copy), r=[K("cv0")], w=[K("qTb")])
                else:
                    S.op("dve", lambda e: e.tensor_tensor(out=cv[:, 1, :], in0=cv[:, 1, :], in1=rs[:], op=ALU.mult),
                         r=[K("cv1"), K("rs")], w=[K("cv1")])
                    S.op("act", lambda e: e.activation(out=kTb[:], in_=cv[:, 1, :], func=AF.Copy), r=[K("cv1")], w=[K("kTb")])
            for s4 in range(4):
                n = 4 * G + s4
                c0 = s4 * 128
                gcn = Gc[:, n:n + 1]
                S.op("pe", lambda e: e.matmul(p_tr[:, 0:128], kTb[:, c0:c0 + 128], identb[:], start=True, stop=True),
                     r=[K("kTb"), K("identb")], w=[K("ptr")])
                S.op("pe", lambda e: e.matmul(p_tr[:, 128:256], vTb[:, c0:c0 + 128], identb[:], start=True, stop=True),
                     r=[K("vTb"), K("identb")], w=[K("ptr")])
                S.op("dve", lambda e: e.tensor_copy(out=ktm[:], in_=p_tr[:, 0:128]), r=[K("ptr")], w=[K("ktm")])
                S.op("dve", lambda e: e.tensor_scalar(out=bv[:], in0=p_tr[:, 128:256], scalar1=beta[:, n:n + 1], scalar2=None, op0=ALU.mult),
                     r=[K("ptr"), K("beta")], w=[K("bv")])
                S.op("dve", lambda e: e.tensor_scalar(out=kbg[:], in0=ktm[:], scalar1=bg[:, n:n + 1], scalar2=None, op0=ALU.mult),
                     r=[K("ktm"), K("bg")], w=[K("kbg")])
                S.op("dve", lambda e: e.tensor_scalar(out=gb[:], in0=ones32[:], scalar1=gall[:, n:n + 1], scalar2=None, op0=ALU.mult),
                     r=[K("ones32"), K("gall")], w=[K("gb")])
                S.op("pe", lambda e: e.matmul(p_gk[:, 0:128], gb[:], triu[:], start=True, stop=True), r=[K("gb"), K("triu")], w=[K("pgk")])
                S.op("act", lambda e: e.activation(out=grow[:], in_=p_gk[:, 0:128], func=AF.Copy), r=[K("pgk")], w=[K("grow")])
                S.op("dve", lambda e: e.tensor_scalar(out=x1[:], in0=grow[:], scalar1=gcn, scalar2=0.0, op0=ALU.subtract, op1=ALU.min),
                     r=[K("grow"), K("Gc")], w=[K("x1")])
                S.op("dve", lambda e: e.tensor_scalar(out=x2[:], in0=grow[:], scalar1=gcn, scalar2=0.0, op0=ALU.subtract, op1=ALU.max),
                     r=[K("grow"), K("Gc")], w=[K("x2")])
                S.op("act", lambda e: e.activation(out=x1[:], in_=x1[:], func=AF.Exp), r=[K("x1")], w=[K("x1")])
                S.op("act", lambda e: e.activation(out=x2[:], in_=x2[:], func=AF.Exp, scale=-1.0), r=[K("x2")], w=[K("x2")])
                S.op("act", lambda e: e.activation(out=eg[:], in_=grow[:], func=AF.Exp), r=[K("grow")], w=[K("eg")])
                S.op("pool", lambda e: e.tensor_tensor(out=gTi[:], in0=x1[:], in1=triu[:], op=ALU.mult), r=[K("x1"), K("triu")], w=[K("gTi")])
                S.op("pool", lambda e: e.tensor_tensor(out=gLs[:], in0=x2[:], in1=trils[:], op=ALU.mult), r=[K("x2"), K("trils")], w=[K("gLs")])
                S.op("pe", lambda e: e.matmul(p_tr[:, 256:384], kTb[:, c0:c0 + 128], kTb[:, c0:c0 + 128], start=True, stop=True),
                     r=[K("kTb")], w=[K("ptr")])
                S.op("dve", lambda e: e.scalar_tensor_tensor(out=Y[0][:], in0=p_tr[:, 256:384], scalar=nbeta[:, n:n + 1], in1=gLs[:],
                                                             op0=ALU.mult, op1=ALU.mult),
                     r=[K("ptr"), K("nbeta"), K("gLs")], w=[K("Y0")])
                S.op("act", lambda e: e.activation(out=yh[:], in_=Y[0][:], func=AF.Copy), r=[K("Y0")], w=[K("yh")])
                S.op("dve", lambda e: e.tensor_copy(out=yh32[:], in_=yh[:]), r=[K("yh")], w=[K("yh32")])
                S.op("dve", lambda e: e.tensor_tensor(out=yl[:], in0=Y[0][:], in1=yh32[:], op=ALU.subtract), r=[K("Y0"), K("yh32")], w=[K("yl")])
                S.op("pe", lambda e: e.matmul(p_tr[:, 384:512], yh[:], identb[:], start=True, stop=False), r=[K("yh"), K("identb")], w=[K("ptr")])
                S.op("pe", lambda e: e.matmul(p_tr[:, 384:512], yl[:], identb[:], start=False, stop=True), r=[K("yl"), K("identb")], w=[K("ptr")])
                S.op("dve", lambda e: e.tensor_copy(out=YT[0][:], in_=p_tr[:, 384:512]), r=[K("ptr")], w=[K("YT0")])
                S.op("dve", lambda e: e.tensor_tensor(out=XT[:], in0=p_tr[:, 384:512], in1=ident[:], op=ALU.add),
                     r=[K("ptr"), K("ident")], w=[K("XT")])
                cur = 0
                for lvl in range(1, 7):
                    nxt = 1 - cur
                    S.op("pe", lambda e: e.matmul(p_yy[:, 0:128], YT[cur][:], Y[cur][:], start=True, stop=True),
                         r=[K("Y%d" % cur), K("YT%d" % cur)], w=[K("pyy")])
                    if lvl < 6:
                        S.op("pe", lambda e: e.matmul(p_yt[:, 0:128], Y[cur][:], YT[cur][:], start=True, stop=True),
                             r=[K("Y%d" % cur), K("YT%d" % cur)], w=[K("pyt")])
                    S.op("act", lambda e: e.activation(out=Y[nxt][:], in_=p_yy[:, 0:128], func=AF.Copy), r=[K("pyy")], w=[K("Y%d" % nxt)])
                    if lvl < 6:
                        S.op("dve", lambda e: e.tensor_copy(out=YT[nxt][:], in_=p_yt[:, 0:128]), r=[K("pyt")], w=[K("YT%d" % nxt)])
                    S.op("pe", lambda e: e.matmul(p_yx[:, 0:128], Y[nxt][:], XT[:], start=True, stop=True),
                         r=[K("Y%d" % nxt), K("XT")], w=[K("pyx")])
                    S.op("dve", lambda e: e.tensor_tensor(out=XT[:], in0=p_yx[:, 0:128], in1=XT[:], op=ALU.add),
                         r=[K("pyx"), K("XT")], w=[K("XT")])
                    cur = nxt
                S.op("pe", lambda e: e.matmul(p_yy[:, 0:128], XT[:], bv[:], start=True, stop=True), r=[K("XT"), K("bv")], w=[K("pyy")])
                S.op("pe", lambda e: e.matmul(p_yt[:, 0:128], kbg[:], XT[:], start=True, stop=True), r=[K("XT"), K("kbg")], w=[K("pyt")])
                S.op("pe", lambda e: e.matmul(p_gk[:, 256:384], kTb[:, c0:c0 + 128], qTb[:, c0:c0 + 128], start=True, stop=True),
                     r=[K("kTb"), K("qTb")], w=[K("pgk")])
                S.op("act", lambda e: e.activation(out=usb[:], in_=p_yy[:, 0:128], func=AF.Copy), r=[K("pyy")], w=[K("usb")])
                S.op("dve", lambda e: e.tensor_copy(out=wT[:], in_=p_yt[:, 0:128]), r=[K("pyt")], w=[K("wT")])
                S.op("dve", lambda e: e.tensor_tensor(out=aqk[:], in0=p_gk[:, 256:384], in1=gTi[:], op=ALU.mult),
                     r=[K("pgk"), K("gTi")], w=[K("aqk")])
                S.op("dve", lambda e: e.tensor_tensor(out=qdec[:], in0=cv[:, 0, c0:c0 + 128], in1=eg[:], op=ALU.mult),
                     r=[K("cv0"), K("eg")], w=[K("qdec")])
                S.op("dve", lambda e: e.tensor_tensor(out=cl[:], in0=grow[:, 127:128], in1=gcn, op=ALU.subtract),
                     r=[K("grow"), K("Gc")], w=[K("cl")])
                S.op("act", lambda e: e.activation(out=cl[:], in_=cl[:], func=AF.Exp), r=[K("cl")], w=[K("cl")])
                S.op("dve", lambda e: e.tensor_scalar(out=kdec[:], in0=ktm[:], scalar1=cl[:, 0:1], scalar2=None, op0=ALU.mult),
                     r=[K("ktm"), K("cl")], w=[K("kdec")])
                S.op("pe", lambda e: e.matmul(p_gk[:, 128:256], wT[:], sbf[:], start=True, stop=True), r=[K("wT"), K("sbf")], w=[K("pgk")])
                S.op("dve", lambda e: e.tensor_tensor(out=vnew[:], in0=usb[:], in1=p_gk[:, 128:256], op=ALU.subtract),
                     r=[K("usb"), K("pgk")], w=[K("vnew")])
                S.op("pe", lambda e: e.matmul(p_o[:, c0:c0 + 128], sbf[:], qdec[:], start=True, stop=False), r=[K("sbf"), K("qdec")], w=[K("po")])
                S.op("pe", lambda e: e.matmul(p_o[:, c0:c0 + 128], vnew[:], aqk[:], start=False, stop=True), r=[K("vnew"), K("aqk")], w=[K("po")])
                S.op("pe", lambda e: e.matmul(p_s[:, 0:128], kdec[:], vnew[:], start=True, stop=True), r=[K("kdec"), K("vnew")], w=[K("pS")])
                S.op("dve", lambda e: e.scalar_tensor_tensor(out=s32[:], in0=s32[:], scalar=eg[:, 127:128], in1=p_s[:, 0:128],
                                                             op0=ALU.mult, op1=ALU.add),
                     r=[K("s32"), K("eg"), K("pS")], w=[K("s32")])
                S.op("act", lambda e: e.activation(out=sbf[:], in_=s32[:], func=AF.Copy), r=[K("s32")], w=[K("sbf")])
            S.op("dve", lambda e: e.tensor_copy(out=osb[:], in_=p_o[:]), r=[K("po")], w=[K("osb")])
            S.op("act", lambda e: e.activation(out=sq[:], in_=osb[:], func=AF.Square), r=[K("osb")], w=[K("sq")])
            S.op("pe", lambda e: e.matmul(p_ss[:], ones_n[:], sq[:], start=True, stop=True), r=[K("ones_n"), K("sq")], w=[K("pss")])
            S.op("act", lambda e: e.activation(out=rstd[:], in_=p_ss[:], func=AF.Sqrt, bias=1e-6, scale=1.0), r=[K("pss")], w=[K("rstd")])
            S.op("dve", lambda e: e.reciprocal(out=rstd[:], in_=rstd[:]), r=[K("rstd")], w=[K("rstd")])
            S.op("dve", lambda e: e.tensor_tensor(out=osb[:], in0=osb[:], in1=rstd[:], op=ALU.mult), r=[K("osb"), K("rstd")], w=[K("osb")])
            S.op("dve", lambda e: e.scalar_tensor_tensor(out=yb[gs][:], in0=osb[:], scalar=gcol[:, 0:1], in1=gt[gs][:],
                                                         op0=ALU.mult, op1=ALU.mult),
                 r=[K("osb"), K("gcol"), K("gt%d" % gs)], w=[K("yb%d" % gs)])
            S.dma("pool", yT[256:384, q0:q0 + 512], yb[gs][:], r=[K("yb%d" % gs)], w=[("ysrc", q0 // 1024)])
        S.barrier()


class YOut:
    def __init__(self, ysrc):
        self.y = ysrc

    def __getitem__(self, idx):
        rs, cs = idx
        k, off = cs.start // 1024, cs.start % 1024
        return self.y[k][rs.start:rs.stop, off:off + (cs.stop - cs.start)]


def exchange(S, src, dst, n, rkey, wkey):
    for k in range(n):
        S.coll(src[k], dst[k], r=[rkey], w=[wkey])
    S.barrier()


def phase_m1(nc, S, T, wm_g, wbr_g, scr_h, ydst, msrc, mdst):
    NT = T // 512
    with contextlib.ExitStack() as es:
        sb = lambda n, s, d: es.enter_context(nc.sbuf_tensor(uq("m_" + n), s, d))
        ps = lambda n: (S.psum_keys.add("m_" + n), es.enter_context(nc.psum_tensor(uq("m_" + n), [128, 512], F32)))[1]
        K = lambda n: "m_" + n
        wm = sb("wm", [128, 16, 4, 512], BF16)
        wb = sb("wb", [128, 4, 4, 512], BF16)
        st = [sb("st%d" % i, [128, 2048], F32) for i in range(2)]
        hT = [sb("hT%d" % i, [128, 16, 512], BF16) for i in range(2)]
        yt = [sb("yt%d" % i, [128, 16, 512], BF16) for i in range(2)]
        mx = [sb("mx%d" % i, [128, 4, 512], BF16) for i in range(2)]
        gsb = [sb("g%d" % i, [128, 512], F32) for i in range(2)]
        acc = sb("acc", [128, 512], F32)
        tmp = sb("tmp", [128, 512], F32)
        p_l = [ps("pl%d" % i) for i in range(3)]
        p_z = [ps("pz%d" % i) for i in range(3)]
        wm_v = wm_g.rearrange("(c p) b j -> c p (b j)", p=128)
        for c in range(16):
            sl = c % 2
            S.dma("sp", st[sl][:, :], wm_v[c], w=[K("st%d" % sl)])
            if c % 2:
                S.op("act", lambda e: e.activation(out=wm[:, c, :, :].rearrange("p b j -> p (b j)"), in_=st[sl][:, :], func=AF.Copy),
                     r=[K("st%d" % sl)], w=[K("wm")])
            else:
                S.op("dve", lambda e: e.tensor_copy(out=wm[:, c, :, :].rearrange("p b j -> p (b j)"), in_=st[sl][:, :]),
                     r=[K("st%d" % sl)], w=[K("wm")])
        for br in range(4):
            sl = br % 2
            S.dma("sp", st[sl][:, :].rearrange("p (c j) -> p c j", c=4), wbr_g[br].rearrange("(c p) j -> p c j", p=128), w=[K("st%d" % sl)])
            S.op("dve", lambda e: e.tensor_copy(out=wb[:, :, br, :], in_=st[sl][:, :].rearrange("p (c j) -> p c j", c=4)),
                 r=[K("st%d" % sl)], w=[K("wb")])
        li = zi = 0
        for t in range(NT):
            sl = t % 2
            t0 = t * 512
            k, off = t0 // 1024, t0 % 1024
            S.dma("sp", hT[sl][:], scr_h[t], r=["scr_h"], w=[K("hT%d" % sl)])
            yv = ydst[k].rearrange("(g b p) t -> p b g t", g=4, b=4, p=128)
            for br in range(4):
                S.dma("sp", yt[sl][:, 4 * br:4 * br + 4, :], yv[:, br, :, off:off + 512], r=[("ydst", k)], w=[K("yt%d" % sl)])
            for fo in range(4):
                for br in range(4):
                    pl = p_l[li % 3]; kl = K("pl%d" % (li % 3)); li += 1
                    pz = p_z[zi % 3]; kz = K("pz%d" % (zi % 3)); zi += 1
                    g = gsb[br % 2]; kg = K("g%d" % (br % 2))
                    for c in range(16):
                        S.op("pe", lambda e: e.matmul(pl[:], wm[:, c, br, fo * 128:(fo + 1) * 128], hT[sl][:, c, :], start=(c == 0), stop=(c == 15)),
                             r=[K("wm"), K("hT%d" % sl)], w=[kl])
                    S.op("act", lambda e: e.activation(out=g[:], in_=pl[:], func=AF.Sigmoid), r=[kl], w=[kg])
                    for c4 in range(4):
                        S.op("pe", lambda e: e.matmul(pz[:], wb[:, c4, br, fo * 128:(fo + 1) * 128], yt[sl][:, 4 * br + c4, :],
                                                      start=(c4 == 0), stop=(c4 == 3)),
                             r=[K("wb"), K("yt%d" % sl)], w=[kz])
                    if br == 0:
                        S.op("dve", lambda e: e.tensor_tensor(out=acc[:], in0=pz[:], in1=g[:], op=ALU.mult), r=[kz, kg], w=[K("acc")])
                    else:
                        S.op("dve", lambda e: e.tensor_tensor(out=tmp[:], in0=pz[:], in1=g[:], op=ALU.mult), r=[kz, kg], w=[K("tmp")])
                        if br < 3:
                            S.op("dve", lambda e: e.tensor_tensor(out=acc[:], in0=acc[:], in1=tmp[:], op=ALU.add),
                                 r=[K("acc"), K("tmp")], w=[K("acc")])
                        else:
                            S.op("dve", lambda e: e.tensor_tensor(out=mx[sl][:, fo, :], in0=acc[:], in1=tmp[:], op=ALU.add),
                                 r=[K("acc"), K("tmp")], w=[K("mx%d" % sl)])
            S.dma("act", msrc[k].rearrange("(f p) t -> p f t", p=128)[:, :, off:off + 512], mx[sl][:],
                  r=[K("mx%d" % sl)], w=[("msrc", k)])
            if t % 2 == 1:
                S.coll(msrc[k], mdst[k], r=[("msrc", k)], w=[("mdst", k)])
        S.barrier()


def phase_m2(nc, S, T, wout_g, mdst, xmine_tile, xsrc_out, xdst_out, lay):
    NT = T // 512
    with contextlib.ExitStack() as es:
        sb = lambda n, s, d: es.enter_context(nc.sbuf_tensor(uq("o_" + n), s, d))
        ps = lambda n: (S.psum_keys.add("o_" + n), es.enter_context(nc.psum_tensor(uq("o_" + n), [128, 512], F32)))[1]
        K = lambda n: "o_" + n
        wo = sb("wo", [128, 16, 512], BF16)
        st = [sb("st%d" % i, [128, 2048], F32) for i in range(2)]
        mt = [sb("mt%d" % i, [128, 16, 512], BF16) for i in range(2)]
        xr = [sb("xr%d" % i, [128, 4, 512], F32) for i in range(2)]
        xn = [sb("xn%d" % i, [128, 4, 512], F32) for i in range(2)]
        p_o = [ps("po%d" % i) for i in range(4)]
        wv = wout_g.rearrange("(c p) j -> p c j", p=128)
        for c4 in range(4):
            sl = c4 % 2
            S.dma("sp", st[sl][:, :].rearrange("p (c j) -> p c j", c=4), wv[:, 4 * c4:4 * c4 + 4, :], w=[K("st%d" % sl)])
            S.op("dve", lambda e: e.tensor_copy(out=wo[:, 4 * c4:4 * c4 + 4, :], in_=st[sl][:, :].rearrange("p (c j) -> p c j", c=4)),
                 r=[K("st%d" % sl)], w=[K("wo")])
        oi = 0
        for t in range(NT):
            sl = t % 2
            t0 = t * 512
            k, off = t0 // 1024, t0 % 1024
            S.dma("sp", mt[sl][:], mdst[k].rearrange("(c p) t -> p c t", p=128)[:, :, off:off + 512], r=[("mdst", k)], w=[K("mt%d" % sl)])
            S.dma("sp", xr[sl][:], xmine_tile(t), r=[("xsrc", lay - 1, t)], w=[K("xr%d" % sl)])
            for fo in range(4):
                po = p_o[oi % 4]; kpo = K("po%d" % (oi % 4)); oi += 1
                for c in range(16):
                    S.op("pe", lambda e: e.matmul(po[:], wo[:, c, fo * 128:(fo + 1) * 128], mt[sl][:, c, :], start=(c == 0), stop=(c == 15)),
                         r=[K("wo"), K("mt%d" % sl)], w=[kpo])
                S.op("dve", lambda e: e.tensor_tensor(out=xn[sl][:, fo, :], in0=po[:], in1=xr[sl][:, fo, :], op=ALU.add),
                     r=[kpo, K("xr%d" % sl)], w=[K("xn%d" % sl)])
            S.dma("pool", xsrc_out[t].rearrange("(f p) t -> p f t", p=128), xn[sl][:], r=[K("xn%d" % sl)], w=[("xsrc", lay, t)])
            S.coll(xsrc_out[t], xdst_out[t], r=[("xsrc", lay, t)], w=[("xdst", lay, t)])
        S.barrier()


def phase_fnorm(nc, S, T, xdst, xmine, fgain4, outT):
    NT = T // 512
    with contextlib.ExitStack() as es:
        sb = lambda n, s, d: es.enter_context(nc.sbuf_tensor(uq("f_" + n), s, d))
        ps = lambda n: (S.psum_keys.add("f_" + n), es.enter_context(nc.psum_tensor(uq("f_" + n), [128, 512], F32)))[1]
        K = lambda n: "f_" + n
        fg = sb("fg", [128, 4], F32)
        ones = sb("ones", [128, 128], BF16)
        xs = [sb("xs%d" % i, [128, 16, 512], F32) for i in range(2)]
        xm = [sb("xm%d" % i, [128, 4, 512], F32) for i in range(2)]
        sq = sb("sq", [128, 16, 512], BF16)
        rstd = sb("rstd", [128, 512], F32)
        ot = [sb("ot%d" % i, [128, 4, 512], F32) for i in range(2)]
        p_q = ps("pq")
        S.op("dve", lambda e: e.memset(ones[:], 1.0 / D), w=[K("ones")])
        S.dma("sp", fg[:], fgain4, w=[K("fg")])
        for t in range(NT):
            sl = t % 2
            t0 = t * 512
            for c4 in range(4):
                S.dma("sp", xs[sl][:, 4 * c4:4 * c4 + 4, :], xdst[t].rearrange("(c p) t -> p c t", p=128)[:, 4 * c4:4 * c4 + 4, :],
                      r=[("xdst", DEPTH - 1, t)], w=[K("xs%d" % sl)])
            S.dma("sp", xm[sl][:], xmine[t].rearrange("(f p) t -> p f t", p=128), r=[("xsrc", DEPTH - 1, t)], w=[K("xm%d" % sl)])
            S.op("act", lambda e: e.activation(out=sq[:], in_=xs[sl][:], func=AF.Square), r=[K("xs%d" % sl)], w=[K("sq")])
            for c in range(16):
                S.op("pe", lambda e: e.matmul(p_q[:], ones[:], sq[:, c, :], start=(c == 0), stop=(c == 15)),
                     r=[K("sq"), K("ones")], w=[K("pq")])
            S.op("act", lambda e: e.activation(out=rstd[:], in_=p_q[:], func=AF.Sqrt, bias=1e-6, scale=1.0), r=[K("pq")], w=[K("rstd")])
            S.op("dve", lambda e: e.reciprocal(out=rstd[:], in_=rstd[:]), r=[K("rstd")], w=[K("rstd")])
            for f in range(4):
                S.op("dve", lambda e: e.scalar_tensor_tensor(out=ot[sl][:, f, :], in0=xm[sl][:, f, :], scalar=fg[:, f:f + 1], in1=rstd[:],
                                                             op0=ALU.mult, op1=ALU.mult),
                     r=[K("xm%d" % sl), K("fg"), K("rstd")], w=[K("ot%d" % sl)])
            S.dma("pool", outT.rearrange("(f p) t -> p f t", p=128)[:, :, t0:t0 + 512], ot[sl][:], r=[K("ot%d" % sl)], w=["outT"])
        S.barrier()


OFF = {"a_q": 0, "a_k": 512, "a_v": 1024, "a_gate": 1536, "b_q": 2048, "b_k": 2560, "b_v": 3072, "b_gate": 3584,
       "c_q": 4096, "c_k": 4608, "c_v": 5120, "c_z": 5632, "c_beta": 6144, "c_a": 6148,
       "d_q": 6152, "d_f": 6664, "d_i": 7176, "d_gate": 7688, "merge": 8200}
FM_ORDER = ("a_q", "a_k", "b_q", "b_k", "a_gate", "b_gate", "c_q", "c_k", "c_v", "c_z", "d_q", "d_f", "d_gate")
TM_ORDER = ("a_v", "b_v", "d_f", "d_i")
DEPTH = 2


def build_fused(T):
    nc = bass.Bass("TRN2", target_bir_lowering=False)
    inp = lambda n, s, d=F32: nc.dram_tensor(n, s, d, kind="ExternalInput").ap()
    NT = T // 512
    NK = T // 1024
    xT = inp("xT", [D, T])
    xmine0 = inp("xmine0", [512, T])
    btA = inp("btA", [2, 128, 2944]); mtA = inp("mtA", [2, 128, 2944])
    btB = inp("btB", [1, 128, 2432]); mtB = inp("mtB", [1, 128, 2432])
    b31 = inp("b31", [128, 1])
    consts = {k: inp(k, [128, 128]) for k in ("triu", "trius", "trils", "ident")}
    lbz_col = inp("lbz_col", [128, 2]); lbz_row = inp("lbz_row", [128, 2, 128])
    fgain4 = inp("fgain4", [128, 4])
    L = []
    for l in range(DEPTH):
        L.append({
            "wfm": inp("wfm%d" % l, [D, NFM * 128]), "wtm": inp("wtm%d" % l, [D, NTM + 2]), "gain16": inp("gain16_%d" % l, [128, 16]),
            "auxB": {"b31": b31, "dl": inp("dl%d" % l, [128, 256]), "subln": inp("subln%d" % l, [128, 1]),
                     "lam_init": 0.8 - 0.6 * math.exp(-0.3 * l)},
            "auxC": {"conv": inp("conv%d" % l, [128, 3, 4]), "a_dt": inp("a_dt%d" % l, [128, 2]), "dn_gain": inp("dn_gain%d" % l, [128, 1])},
            "auxD": {"hg_gain": inp("hg_gain%d" % l, [128, 1]), "lbz_col": lbz_col, "lbz_row": lbz_row},
            "wm_g": inp("wm_g%d" % l, [D, 4, 512]), "wbr_g": inp("wbr_g%d" % l, [4, 512, 512]), "wout_g": inp("wout_g%d" % l, [D, 512]),
        })
    scr_qk = nc.dram_tensor("scr_qk", [NQK, 128, T], BF16).ap()
    scr_fm = nc.dram_tensor("scr_fm", [NFM32, 128, T], F32).ap()
    scr_v = nc.dram_tensor("scr_v", [T, 256], BF16).ap()
    scr_tm = nc.dram_tensor("scr_tm", [T, 256], F32).ap()
    scr_bd = nc.dram_tensor("scr_bd", [T, 2], F32).ap()
    scr_h = nc.dram_tensor("scr_h", [NT, 128, 16, 512], BF16).ap()
    ysrc = nc.dram_tensor("ysrc", [NK, 512, 1024], BF16).ap()
    ydst = nc.dram_tensor("ydst", [NK, 2048, 1024], BF16).ap()
    msrc = nc.dram_tensor("msrc", [NK, 512, 1024], BF16).ap()
    mdst = nc.dram_tensor("mdst", [NK, 2048, 1024], BF16).ap()
    xsrc = [nc.dram_tensor("xsrc%d" % l, [NT, 512, 512], F32).ap() for l in range(DEPTH)]
    xdst = [nc.dram_tensor("xdst%d" % l, [NT, 2048, 512], F32).ap() for l in range(DEPTH)]
    outT = nc.dram_tensor("outT", [512, T], F32, kind="ExternalOutput").ap()
    yT = YOut(ysrc)
    xT_v = xT.rearrange("(c p) t -> p c t", p=128)
    xm0_v = xmine0.rearrange("(f p) t -> p f t", p=128)
    with contextlib.ExitStack() as es:
        S = Sched(nc, es)
        for l in range(DEPTH):
            P = L[l]
            UNIQ[0] = l
            if l == 0:
                xtile = lambda t, c4: xT_v[:, 4 * c4:4 * c4 + 4, t * 512:(t + 1) * 512]
                xmine_tile = lambda t: xm0_v[:, :, t * 512:(t + 1) * 512]
            else:
                xd = xdst[l - 1]
                xp = xsrc[l - 1]
                xtile = lambda t, c4: xd[t].rearrange("(c p) t -> p c t", p=128)[:, 4 * c4:4 * c4 + 4, :]
                xmine_tile = lambda t: xp[t].rearrange("(f p) t -> p f t", p=128)
            phase_proj(nc, S, T, xtile, P["wfm"], P["wtm"], P["gain16"], scr_qk, scr_fm, scr_v, scr_tm, scr_bd, scr_h, lay=l)
            phase_attn(nc, S, T, "A", scr_qk, scr_fm, scr_v, btA, mtA, None, yT)
            phase_attn(nc, S, T, "B", scr_qk, scr_fm, scr_v, btB, mtB, P["auxB"], yT)
            phase_gdn(nc, S, T, scr_fm, scr_bd, consts, P["auxC"], yT)

            def ytile_done(G):
                if G % 2 == 1:
                    k = G // 2
                    S.coll(ysrc[k], ydst[k], r=[("ysrc", k)], w=[("ydst", k)])
            phase_hgrn(nc, S, T, l, scr_fm, scr_tm, consts, P["auxD"], yT, on_tile=ytile_done)
            phase_m1(nc, S, T, P["wm_g"], P["wbr_g"], scr_h, ydst, msrc, mdst)
            phase_m2(nc, S, T, P["wout_g"], mdst, xmine_tile, xsrc[l], xdst[l], l)
        phase_fnorm(nc, S, T, xdst[DEPTH - 1], xsrc[DEPTH - 1], fgain4, outT)
        S.finish()
    return nc


def core_inputs(inputs, xT_b, g):
    l16 = lambda v: np.ascontiguousarray(v.reshape(16, 128).T)
    col = lambda v: np.ascontiguousarray(v.reshape(128, 1))
    rel_bias = inputs["rel_bias"]
    btA, mtA = attn_tables(rel_bias, "A", g)
    btB, mtB = attn_tables(rel_bias, "B", g)
    lbz = inputs["hg_lb_logits"][:, 128 * g:128 * (g + 1)]
    one = np.ones((128, 128), np.float32)
    m = {
        "xT": xT_b, "xmine0": np.ascontiguousarray(xT_b[512 * g:512 * (g + 1)]),
        "btA": btA, "mtA": mtA, "btB": btB, "mtB": mtB,
        "b31": np.full((128, 1), rel_bias[31, 8 + g], np.float32),
        "triu": np.triu(one), "trius": np.triu(one, 1), "trils": np.tril(one, -1), "ident": np.eye(128, dtype=np.float32),
        "lbz_col": np.ascontiguousarray(lbz.T),
        "lbz_row": np.ascontiguousarray(np.broadcast_to(lbz[None], (128, 2, 128))),
        "fgain4": np.ascontiguousarray(inputs["final_gain"][512 * g:512 * (g + 1)].reshape(4, 128).T),
    }
    for l in range(DEPTH):
        w = inputs["w_in"][l]
        sl = lambda name, wd=128: w[:, OFF[name] + wd * g:OFF[name] + wd * (g + 1)]
        conv = inputs["dn_conv"][l]
        convl = np.stack([conv[:, j * 512 + 128 * g:j * 512 + 128 * (g + 1)] for j in range(3)])
        wmg = w[:, OFF["merge"]:].reshape(D, 4, D)[:, :, 512 * g:512 * (g + 1)]
        m.update({
            "wfm%d" % l: np.ascontiguousarray(np.concatenate([sl(n) for n in FM_ORDER], axis=1)),
            "wtm%d" % l: np.ascontiguousarray(np.concatenate([sl(n) for n in TM_ORDER] + [sl("c_beta", 1), sl("c_a", 1)], axis=1)),
            "gain16_%d" % l: l16(inputs["norm_gain"][l]),
            "dl%d" % l: np.ascontiguousarray(np.broadcast_to(inputs["diff_lambda"][l].reshape(1, 256), (128, 256))),
            "subln%d" % l: col(inputs["diff_subln_gain"][l]),
            "conv%d" % l: np.ascontiguousarray(convl.transpose(2, 0, 1)),
            "a_dt%d" % l: np.ascontiguousarray(np.broadcast_to(
                np.array([inputs["dn_a_log"][l, g], inputs["dn_dt_bias"][l, g]], np.float32), (128, 2))),
            "dn_gain%d" % l: col(inputs["dn_norm_gain"][l]),
            "hg_gain%d" % l: col(inputs["hg_norm_gain"][l]),
            "wm_g%d" % l: np.ascontiguousarray(wmg),
            "wbr_g%d" % l: np.ascontiguousarray(inputs["w_branch"][l][:, :, 512 * g:512 * (g + 1)]),
            "wout_g%d" % l: np.ascontiguousarray(inputs["w_out"][l][:, 512 * g:512 * (g + 1)]),
        })
    return m


def kernel_impl(inputs, n_cores=8):
    inputs = {k: np.asarray(v, dtype=np.float32) for k, v in inputs.items()}
    x = inputs["x"]
    Bsz, T, _ = x.shape
    assert Bsz * 4 == n_cores
    nc = build_fused(T)
    in_maps = []
    for b in range(Bsz):
        xT_b = np.ascontiguousarray(x[b].T)
        for g in range(4):
            in_maps.append(core_inputs(inputs, xT_b, g))
    res = run_bass_kernel_spmd(nc, in_maps, core_ids=list(range(n_cores)))
    out = np.empty((Bsz, T, D), np.float32)
    for b in range(Bsz):
        for g in range(4):
            out[b, :, 512 * g:512 * (g + 1)] = np.asarray(res.results[b * 4 + g]["outT"]).T
    return out


def kernel(**inputs):
    return kernel_impl(inputs)
```

```python
import contextlib
import math
import numpy as np
import concourse.bass as bass
import concourse.mybir as mybir
from concourse.bass_utils import run_bass_kernel_spmd

F32 = mybir.dt.float32
BF16 = mybir.dt.bfloat16
AF = mybir.ActivationFunctionType
ALU = mybir.AluOpType

D = 2048
NDS = 24
UNIQ = [0]


def uq(n):
    return "%s_L%d" % (n, UNIQ[0])


class Sched:
    def __init__(self, nc, es, selfdeps=True):
        self.nc = nc
        self.selfdeps = selfdeps
        self.E = {}
        self.sem = {}
        for name, eng in (("pe", nc.tensor), ("act", nc.scalar), ("dve", nc.vector),
                          ("pool", nc.gpsimd), ("sp", nc.sync)):
            sem = es.enter_context(nc.semaphore("s_" + name))
            self.E[name] = {"eng": eng, "cnt": 0, "seen": {}}
            self.sem[name] = sem
        self.dq = {}
        for q in ("sp", "pool", "act"):
            names = []
            for i in range(NDS):
                nm = "d_%s_%d" % (q, i)
                self.sem[nm] = es.enter_context(nc.semaphore(nm))
                names.append(nm)
            self.dq[q] = {"sems": names, "n": 0}
        self.sem["cc"] = es.enter_context(nc.semaphore("s_cc"))
        self.ncc = 0
        self.lw = {}
        self.rd = {}
        self.psum_keys = set()

    def _deps(self, r, w):
        deps = {}

        def add(tok):
            if tok is None:
                return
            s, v = tok
            if deps.get(s, 0) < v:
                deps[s] = v
        for k in r:
            add(self.lw.get(k))
        for k in w:
            add(self.lw.get(k))
            for s, v in self.rd.get(k, {}).items():
                add((s, v))
        return deps

    def _wait(self, ename, deps):
        E = self.E[ename]
        for s, v in deps.items():
            if s == ename and (ename == "pe" or not self.selfdeps):
                continue
            if E["seen"].get(s, 0) >= v:
                continue
            E["eng"].wait_ge(self.sem[s], v)
            E["seen"][s] = v

    def _commit(self, tok, r, w):
        s, v = tok
        for k in w:
            self.lw[k] = tok
            self.rd[k] = {}
        for k in r:
            d = self.rd.setdefault(k, {})
            if d.get(s, 0) < v:
                d[s] = v

    def op(self, ename, fn, r=(), w=()):
        xs = [k for k in r if k in self.psum_keys and k not in w]
        if xs:
            w = list(w) + xs
        deps = self._deps(r, w)
        self._wait(ename, deps)
        E = self.E[ename]
        ins = fn(E["eng"])
        E["cnt"] += 1
        ins.then_inc(self.sem[ename], 1)
        self._commit((ename, E["cnt"]), r, w)

    def dma(self, q, out, in_, r=(), w=()):
        deps = self._deps(r, w)
        Q = self.dq[q]
        n = Q["n"]
        nm = Q["sems"][n % NDS]
        use = n // NDS
        if use > 0:
            deps[nm] = max(deps.get(nm, 0), 16 * use)
        self._wait(q, deps)
        ins = self.E[q]["eng"].dma_start(out=out, in_=in_)
        ins.then_inc(self.sem[nm], 16)
        Q["n"] += 1
        self._commit((nm, 16 * (use + 1)), r, w)

    def coll(self, src, dst, r=(), w=()):
        deps = self._deps(r, w)
        self._wait("pool", deps)
        ins = self.nc.gpsimd.collective_compute("AllGather", ALU.bypass, replica_groups=[[0, 1, 2, 3], [4, 5, 6, 7]],
                                                ins=[src], outs=[dst])
        ins.then_inc(self.sem["cc"])
        self.ncc += 1
        self._commit(("cc", self.ncc), r, w)

    def barrier(self):
        deps = {}
        for q, Q in self.dq.items():
            for i, nm in enumerate(Q["sems"]):
                if Q["n"] > i:
                    deps[nm] = 16 * ((Q["n"] - i + NDS - 1) // NDS)
        for name, E in self.E.items():
            if E["cnt"]:
                deps[name] = E["cnt"]
        for name in self.E:
            d = {k: v for k, v in deps.items() if k != name}
            E = self.E[name]
            for sname, v in d.items():
                if E["seen"].get(sname, 0) >= v:
                    continue
                E["eng"].wait_ge(self.sem[sname], v)
                E["seen"][sname] = v

    def finish(self):
        deps = {}
        for q, Q in self.dq.items():
            for i, nm in enumerate(Q["sems"]):
                if Q["n"] > i:
                    deps[nm] = 16 * ((Q["n"] - i + NDS - 1) // NDS)
        for name, E in self.E.items():
            if name != "sp" and E["cnt"]:
                deps[name] = E["cnt"]
        if self.ncc:
            deps["cc"] = self.ncc
        self._wait("sp", deps)


FM_AQ, FM_AK, FM_BQ, FM_BK = 0, 1, 2, 3
G_AG, G_BG, G_CQ, G_CK, G_CV, G_CZ, G_DQ, G_DF, G_DG = range(9)
NQK = 4
NFM32 = 9
NFM = NQK + NFM32
NTM = 512
SILU_GROUPS = (G_AG, G_BG, G_CZ, G_DG)


def phase_proj(nc, S, T, xtile, wfm, wtm, gain16, scr_qk, scr_fm, scr_v, scr_tm, scr_bd, scr_h, lay=0):
    TT = 512
    NT = T // TT
    with contextlib.ExitStack() as es:
        sb = lambda n, s, d: es.enter_context(nc.sbuf_tensor(uq(n), s, d))
        ps = lambda n: (S.psum_keys.add(n), es.enter_context(nc.psum_tensor(uq(n), [128, 512], F32)))[1]
        w_fm = sb("p_wfm", [128, 16, NFM * 128], BF16)
        w_tm = sb("p_wtm", [128, 16, NTM + 2], BF16)
        wst = [sb("p_wst%d" % i, [128, NFM * 128], F32) for i in range(2)]
        gain = sb("p_gain", [128, 16], F32)
        ones = sb("p_ones", [128, 128], BF16)
        xs = [sb("p_xs0", [128, 16, TT], F32)] * 2
        sq = sb("p_sq", [128, 16, TT], BF16)
        rstd = sb("p_rstd", [128, TT], F32)
        hT = sb("p_hT", [128, 16, TT], BF16)
        o_qk = [sb("p_oqk%d" % i, [128, NQK, TT], BF16) for i in range(2)]
        o_fm = [sb("p_ofm%d" % i, [128, NFM32, TT], F32) for i in range(2)]
        o_tm = [sb("p_otm%d" % i, [128, 4, 256], F32) for i in range(2)]
        o_v = [sb("p_ov%d" % i, [128, 4, 256], BF16) for i in range(2)]
        o_bd = [sb("p_obd%d" % i, [128, 4, 2], F32) for i in range(2)]
        p_ssq = ps("p_pssq")
        p_mm = [ps("p_pmm%d" % i) for i in range(4)]
        p_bd = ps("p_pbd")

        S.op("dve", lambda e: e.memset(ones[:], 1.0 / D), w=["p_ones"])
        S.dma("sp", gain[:], gain16, w=["p_gain"])
        wfm_v = wfm.rearrange("(c p) n -> c p n", p=128)
        wtm_v = wtm.rearrange("(c p) n -> c p n", p=128)
        for c in range(16):
            st = wst[c % 2]
            k = "p_wst%d" % (c % 2)
            S.dma("sp", st[:, :], wfm_v[c], w=[k])
            S.op("act" if c % 2 else "dve",
                 (lambda e, st=st, c=c: e.activation(out=w_fm[:, c, :], in_=st[:, :], func=AF.Copy))
                 if c % 2 else
                 (lambda e, st=st, c=c: e.tensor_copy(out=w_fm[:, c, :], in_=st[:, :])),
                 r=[k], w=["p_wfm"])
        for c in range(16):
            st = wst[c % 2]
            k = "p_wst%d" % (c % 2)
            S.dma("sp", st[:, 0:NTM + 2], wtm_v[c], w=[k])
            S.op("dve", lambda e, st=st, c=c: e.tensor_copy(out=w_tm[:, c, :], in_=st[:, 0:NTM + 2]),
                 r=[k], w=["p_wtm"])

        mmi = 0
        for t in range(NT):
            sl = t % 2
            t0 = t * TT
            kx = "p_xs0"
            for c4 in range(4):
                S.dma("sp", xs[sl][:, 4 * c4:4 * c4 + 4, :], xtile(t, c4), r=[("xdst", lay - 1, t)], w=[kx])
            S.op("act", lambda e: e.activation(out=sq[:], in_=xs[sl][:], func=AF.Square),
                 r=[kx], w=["p_sq"])
            for c in range(16):
                S.op("pe", lambda e, c=c: e.matmul(p_ssq[:], ones[:], sq[:, c, :], start=(c == 0), stop=(c == 15)),
                     r=["p_sq", "p_ones"], w=["p_pssq"])
            S.op("act", lambda e: e.activation(out=rstd[:], in_=p_ssq[:], func=AF.Sqrt, bias=1e-6, scale=1.0),
                 r=["p_pssq"], w=["p_rstd"])
            S.op("dve", lambda e: e.reciprocal(out=rstd[:], in_=rstd[:]), r=["p_rstd"], w=["p_rstd"])
            for c in range(16):
                S.op("dve", lambda e, c=c: e.scalar_tensor_tensor(
                    out=hT[:, c, :], in0=xs[sl][:, c, :], scalar=gain[:, c:c + 1], in1=rstd[:],
                    op0=ALU.mult, op1=ALU.mult),
                    r=[kx, "p_gain", "p_rstd"], w=["p_hT"])
            S.dma("pool", scr_h[t], hT[:], r=["p_hT"], w=["scr_h"])
            for g in range(NFM):
                pm = p_mm[mmi % 4]
                kp = "p_pmm%d" % (mmi % 4)
                mmi += 1
                for c in range(16):
                    S.op("pe", lambda e, c=c, g=g, pm=pm: e.matmul(
                        pm[:], w_fm[:, c, g * 128:(g + 1) * 128], hT[:, c, :], start=(c == 0), stop=(c == 15)),
                        r=["p_wfm", "p_hT"], w=[kp])
                if g < NQK:
                    S.op("dve", lambda e, g=g, pm=pm: e.tensor_copy(out=o_qk[sl][:, g, :], in_=pm[:]),
                         r=[kp], w=["p_oqk%d" % sl])
                else:
                    gg = g - NQK
                    fn = AF.Silu if gg in SILU_GROUPS else AF.Copy
                    S.op("act", lambda e, gg=gg, pm=pm, fn=fn: e.activation(out=o_fm[sl][:, gg, :], in_=pm[:], func=fn),
                         r=[kp], w=["p_ofm%d" % sl])
            S.dma("pool", scr_qk.rearrange("g p t -> p g t")[:, :, t0:t0 + TT], o_qk[sl][:],
                  r=["p_oqk%d" % sl], w=["scr_qk"])
            S.dma("pool", scr_fm.rearrange("g p t -> p g t")[:, :, t0:t0 + TT], o_fm[sl][:],
                  r=["p_ofm%d" % sl], w=["scr_fm"])
            for s4 in range(4):
                pm = p_mm[mmi % 4]
                kp = "p_pmm%d" % (mmi % 4)
                mmi += 1
                for c in range(16):
                    S.op("pe", lambda e, c=c, pm=pm, s4=s4: e.matmul(
                        pm[:], hT[:, c, s4 * 128:(s4 + 1) * 128], w_tm[:, c, 0:NTM], start=(c == 0), stop=(c == 15)),
                        r=["p_wtm", "p_hT"], w=[kp])
                S.op("dve", lambda e, pm=pm, s4=s4: e.tensor_copy(out=o_v[sl][:, s4, :], in_=pm[:, 0:256]),
                     r=[kp], w=["p_ov%d" % sl])
                S.op("act", lambda e, pm=pm, s4=s4: e.activation(out=o_tm[sl][:, s4, :], in_=pm[:, 256:512], func=AF.Copy),
                     r=[kp], w=["p_otm%d" % sl])
                for c in range(16):
                    S.op("pe", lambda e, c=c, s4=s4: e.matmul(
                        p_bd[:, 0:2], hT[:, c, s4 * 128:(s4 + 1) * 128], w_tm[:, c, NTM:NTM + 2],
                        start=(c == 0), stop=(c == 15)),
                        r=["p_wtm", "p_hT"], w=["p_pbd"])
                S.op("dve", lambda e, s4=s4: e.tensor_copy(out=o_bd[sl][:, s4, :], in_=p_bd[:, 0:2]),
                     r=["p_pbd"], w=["p_obd%d" % sl])
            S.dma("pool", scr_tm[t0:t0 + TT, :].rearrange("(s p) c -> p s c", p=128), o_tm[sl][:],
                  r=["p_otm%d" % sl], w=["scr_tm"])
            S.dma("pool", scr_v[t0:t0 + TT, :].rearrange("(s p) c -> p s c", p=128), o_v[sl][:],
                  r=["p_ov%d" % sl], w=["scr_v"])
            S.dma("pool", scr_bd[t0:t0 + TT, :].rearrange("(s p) c -> p s c", p=128), o_bd[sl][:],
                  r=["p_obd%d" % sl], w=["scr_bd"])
        S.barrier()


def phase_attn(nc, S, T, kind, scr_qk, scr_fm, scr_v, btab, mtab, aux, yT):
    isA = kind == "A"
    NB = T // 128
    NG = T // 512
    qg, kg = (FM_AQ, FM_AK) if isA else (FM_BQ, FM_BK)
    vcol = 0 if isA else 128
    dv = 64 if isA else 128
    ntab = 2 if isA else 1
    W = btab.shape[2]
    dmax = 16 if isA else 12
    pf = "a%s_" % kind
    with contextlib.ExitStack() as es:
        sb = lambda n, s, d: es.enter_context(nc.sbuf_tensor(uq(pf + n), s, d))
        ps = lambda n, m=128: (S.psum_keys.add(pf + n), es.enter_context(nc.psum_tensor(uq(pf + n), [m, 512], F32)))[1]
        K = lambda n: pf + n
        qT = sb("qT", [128, T], BF16)
        kT = sb("kT", [128, T], BF16)
        v = sb("v", [128, NB, 128], BF16)
        wgt = sb("wgt", [128, ntab, W], F32)
        mst = sb("mst", [128, W], F32)
        ones = sb("ones", [128, 128], BF16)
        e32 = [sb("e32_%d" % i, [128, 512], F32) for i in range(4)]
        pT = [sb("pT%d" % i, [128, 512], BF16) for i in range(8)]
        rr = sb("rr", [128, 512], F32)
        dacc = [sb("dacc%d" % i, [128, 512], F32) for i in range(2)]
        dhi = sb("dhi", [128, 512], BF16)
        dlo = sb("dlo", [128, 512], BF16)
        o1 = sb("o1", [128, 512], F32)
        o2 = sb("o2", [128, 512], F32)
        gt = [sb("gt%d" % i, [128, 512], F32) for i in range(2)]
        yb = [sb("yb%d" % i, [128, 512], BF16) for i in range(2)]
        gtA = [sb("gtA%d" % i, [64, 2, 512], F32) for i in range(2)] if isA else None
        ybA = [sb("ybA%d" % i, [64, 2, 512], BF16) for i in range(2)] if isA else None
        p_s = [ps("ps%d" % i) for i in range(4)]
        p_o = [ps("po%d" % i, dv) for i in range(2)]
        p_d = [ps("pd%d" % i, dv) for i in range(2)]
        S.op("dve", lambda e: e.memset(ones[:], 1.0), w=[K("ones")])
        for c in range(0, T, 4096):
            c1 = min(T, c + 4096)
            S.dma("sp", qT[:, c:c1], scr_qk[qg][:, c:c1], r=["scr_qk"], w=[K("qT")])
            S.dma("sp", kT[:, c:c1], scr_qk[kg][:, c:c1], r=["scr_qk"], w=[K("kT")])
        for n0 in range(0, NB, 8):
            S.dma("sp", v[:, n0:n0 + 8, :],
                  scr_v[n0 * 128:(n0 + 8) * 128, vcol:vcol + 128].rearrange("(n p) c -> p n c", p=128),
                  r=["scr_v"], w=[K("v")])
        for i in range(ntab):
            S.dma("sp", wgt[:, i, :], btab[i], w=[K("wgt")])
            S.dma("sp", mst[:], mtab[i], w=[K("mst")])
            S.op("act", lambda e, i=i: e.activation(out=wgt[:, i, :], in_=wgt[:, i, :], func=AF.Exp),
                 r=[K("wgt")], w=[K("wgt")])
            S.op("dve", lambda e, i=i: e.tensor_tensor(out=wgt[:, i, :], in0=wgt[:, i, :], in1=mst[:], op=ALU.mult),
                 r=[K("wgt"), K("mst")], w=[K("wgt")])
        if not isA:
            b31 = sb("b31", [128, 1], F32)
            dl = sb("dl", [128, 256], F32)
            tmp = sb("tmp", [128, 128], F32)
            ss = sb("ss", [128, 2], F32)
            neglam = sb("neglam", [128, 1], F32)
            gcol = sb("gcol", [128, 1], F32)
            ones_n = sb("ones_n", [128, 128], BF16)
            sqb = sb("sqb", [128, 512], BF16)
            rstd = sb("rstd", [128, 512], F32)
            p_q = p_s[3]
            lam_init = aux["lam_init"]
            S.op("dve", lambda e: e.memset(ones_n[:], 1.0 / 128), w=[K("ones_n")])
            S.dma("sp", b31[:], aux["b31"], w=[K("b31")])
            S.dma("sp", dl[:], aux["dl"], w=[K("dl")])
            S.dma("sp", gcol[:], aux["subln"], w=[K("gcol")])
            S.op("dve", lambda e: e.tensor_tensor(out=tmp[:, 0:64], in0=dl[:, 0:64], in1=dl[:, 64:128], op=ALU.mult),
                 r=[K("dl")], w=[K("tmp")])
            S.op("dve", lambda e: e.tensor_tensor(out=tmp[:, 64:128], in0=dl[:, 128:192], in1=dl[:, 192:256], op=ALU.mult),
                 r=[K("dl")], w=[K("tmp")])
            S.op("dve", lambda e: e.reduce_sum(out=ss[:, 0:1], in_=tmp[:, 0:64], axis=mybir.AxisListType.X),
                 r=[K("tmp")], w=[K("ss")])
            S.op("dve", lambda e: e.reduce_sum(out=ss[:, 1:2], in_=tmp[:, 64:128], axis=mybir.AxisListType.X),
                 r=[K("tmp")], w=[K("ss")])
            S.op("act", lambda e: e.activation(out=ss[:], in_=ss[:], func=AF.Exp), r=[K("ss")], w=[K("ss")])
            S.op("dve", lambda e: e.tensor_tensor(out=neglam[:], in0=ss[:, 1:2], in1=ss[:, 0:1], op=ALU.subtract),
                 r=[K("ss")], w=[K("neglam")])
            S.op("dve", lambda e: e.tensor_scalar(out=neglam[:], in0=neglam[:], scalar1=-lam_init, scalar2=None, op0=ALU.add),
                 r=[K("neglam")], w=[K("neglam")])
            S.op("dve", lambda e: e.tensor_scalar(out=gcol[:], in0=gcol[:], scalar1=1.0 - lam_init, scalar2=None, op0=ALU.mult),
                 r=[K("gcol")], w=[K("gcol")])

        step = 0
        for G in range(NG):
            q0 = G * 512
            gs = G % 2
            ggrp = G_AG if isA else G_BG
            if isA:
                for h in range(2):
                    S.dma("sp", gtA[gs][:, h, :],
                          scr_fm[ggrp][64 * h:64 * h + 64, q0:q0 + 512], r=["scr_fm"], w=[K("gt%d" % gs)])
            else:
                S.dma("sp", gt[gs][:], scr_fm[ggrp][:, q0:q0 + 512], r=["scr_fm"], w=[K("gt%d" % gs)])
            kb_lo = max(0, 4 * G - 16) if isA else 0
            kb_hi = 4 * G + 3
            steps = [(kb, sub) for kb in range(kb_lo, kb_hi + 1) for sub in range(2)]

            def emit_qk(i):
                kb, sub = steps[i]
                st_ = step0 + i
                psl, ptl = st_ % 4, st_ % 8
                delta0 = 4 * G - kb
                near = delta0 <= dmax
                j0 = 128 * delta0 + 384
                S.op("pe", lambda e: e.matmul(p_s[psl][:], kT[64 * sub:64 * sub + 64, kb * 128:(kb + 1) * 128],
                                              qT[64 * sub:64 * sub + 64, q0:q0 + 512], start=True, stop=True),
                     r=[K("kT"), K("qT")], w=[K("ps%d" % psl)])
                if near:
                    S.op("act", lambda e: e.activation(out=e32[psl][:], in_=p_s[psl][:], func=AF.Exp, scale=0.125),
                         r=[K("ps%d" % psl)], w=[K("e32_%d" % psl)])
                    ti = sub if isA else 0
                    S.op("dve" if (st_ % 2) else "pool",
                         lambda e: e.tensor_tensor(out=pT[ptl][:], in0=e32[psl][:], in1=wgt[:, ti, j0:j0 + 512], op=ALU.mult),
                         r=[K("e32_%d" % psl), K("wgt")], w=[K("pT%d" % ptl)])
                else:
                    S.op("act", lambda e: e.activation(out=pT[ptl][:], in_=p_s[psl][:], func=AF.Exp,
                                                       bias=b31[:, 0:1], scale=0.125),
                         r=[K("ps%d" % psl), K("b31")], w=[K("pT%d" % ptl)])

            def emit_pv(i):
                kb, sub = steps[i]
                ptl = (step0 + i) % 8
                first = kb == kb_lo
                last = kb == kb_hi
                vs = v[:, kb, 64 * sub:64 * sub + 64] if isA else v[:, kb, :]
                S.op("pe", lambda e: e.matmul(p_o[sub][:], vs, pT[ptl][:], start=first, stop=last),
                     r=[K("v"), K("pT%d" % ptl)], w=[K("po%d" % sub)])
                if isA:
                    S.op("pe", lambda e: e.matmul(p_d[sub][:], ones[:, 0:dv], pT[ptl][:], start=first, stop=last),
                         r=[K("ones"), K("pT%d" % ptl)], w=[K("pd%d" % sub)])
                    return
                en = "pool" if ((step0 + i) % 2) else "dve"
                if first:
                    S.op(en, lambda e: e.tensor_copy(out=dacc[sub][:], in_=pT[ptl][:]),
                         r=[K("pT%d" % ptl)], w=[K("dacc%d" % sub)])
                else:
                    S.op(en, lambda e: e.tensor_tensor(out=dacc[sub][:], in0=dacc[sub][:], in1=pT[ptl][:], op=ALU.add),
                         r=[K("pT%d" % ptl), K("dacc%d" % sub)], w=[K("dacc%d" % sub)])

            step0 = step
            nbu = len(steps) // 4
            for j in (0, 2, 1, 3):
                emit_qk(j)
            for bi in range(nbu):
                nx = bi + 1 < nbu
                if nx:
                    emit_qk(4 * bi + 4)
                    emit_qk(4 * bi + 6)
                emit_pv(4 * bi + 0)
                emit_pv(4 * bi + 2)
                if nx:
                    emit_qk(4 * bi + 5)
                    emit_qk(4 * bi + 7)
                emit_pv(4 * bi + 1)
                emit_pv(4 * bi + 3)
            step += len(steps)
            for sub in ([] if isA else range(2)):
                S.op("act", lambda e: e.activation(out=dhi[:], in_=dacc[sub][:], func=AF.Copy), r=[K("dacc%d" % sub)], w=[K("dhi")])
                S.op("dve", lambda e: e.tensor_tensor(out=dlo[:], in0=dacc[sub][:], in1=dhi[:], op=ALU.subtract),
                     r=[K("dacc%d" % sub), K("dhi")], w=[K("dlo")])
                S.op("pe", lambda e: e.matmul(p_d[sub][:], ones[:, 0:dv], dhi[:], start=True, stop=False),
                     r=[K("ones"), K("dhi")], w=[K("pd%d" % sub)])
                S.op("pe", lambda e: e.matmul(p_d[sub][:], ones[:, 0:dv], dlo[:], start=False, stop=True),
                     r=[K("ones"), K("dlo")], w=[K("pd%d" % sub)])
            if isA:
                for h in range(2):
                    S.op("dve", lambda e: e.reciprocal(out=rr[0:64, :], in_=p_d[h][:]), r=[K("pd%d" % h)], w=[K("rr")])
                    S.op("dve", lambda e: e.tensor_tensor(out=o1[0:64, :], in0=p_o[h][:], in1=rr[0:64, :], op=ALU.mult),
                         r=[K("po%d" % h), K("rr")], w=[K("o1")])
                    S.op("dve", lambda e: e.tensor_tensor(out=ybA[gs][:, h, :], in0=o1[0:64, :], in1=gtA[gs][:, h, :], op=ALU.mult),
                         r=[K("o1"), K("gt%d" % gs)], w=[K("yb%d" % gs)])
                S.dma("pool", yT[0:128, q0:q0 + 512].rearrange("(h p) t -> p h t", p=64), ybA[gs][:],
                      r=[K("yb%d" % gs)], w=[("ysrc", q0 // 1024)])
            else:
                S.op("dve", lambda e: e.reciprocal(out=rr[:], in_=p_d[0][:]), r=[K("pd0")], w=[K("rr")])
                S.op("dve", lambda e: e.tensor_tensor(out=o1[:], in0=p_o[0][:], in1=rr[:], op=ALU.mult),
                     r=[K("po0"), K("rr")], w=[K("o1")])
                S.op("dve", lambda e: e.reciprocal(out=rr[:], in_=p_d[1][:]), r=[K("pd1")], w=[K("rr")])
                S.op("dve", lambda e: e.tensor_tensor(out=o2[:], in0=p_o[1][:], in1=rr[:], op=ALU.mult),
                     r=[K("po1"), K("rr")], w=[K("o2")])
                S.op("dve", lambda e: e.scalar_tensor_tensor(out=o1[:], in0=o2[:], scalar=neglam[:, 0:1], in1=o1[:],
                                                             op0=ALU.mult, op1=ALU.add),
                     r=[K("o1"), K("o2"), K("neglam")], w=[K("o1")])
                S.op("act", lambda e: e.activation(out=sqb[:], in_=o1[:], func=AF.Square), r=[K("o1")], w=[K("sqb")])
                S.op("pe", lambda e: e.matmul(p_q[:], ones_n[:], sqb[:], start=True, stop=True),
                     r=[K("ones_n"), K("sqb")], w=[K("ps3")])
                S.op("act", lambda e: e.activation(out=rstd[:], in_=p_q[:], func=AF.Sqrt, bias=1e-6, scale=1.0),
                     r=[K("ps3")], w=[K("rstd")])
                S.op("dve", lambda e: e.reciprocal(out=rstd[:], in_=rstd[:]), r=[K("rstd")], w=[K("rstd")])
                S.op("dve", lambda e: e.tensor_tensor(out=o1[:], in0=o1[:], in1=rstd[:], op=ALU.mult),
                     r=[K("o1"), K("rstd")], w=[K("o1")])
                S.op("dve", lambda e: e.scalar_tensor_tensor(out=yb[gs][:], in0=o1[:], scalar=gcol[:, 0:1], in1=gt[gs][:],
                                                             op0=ALU.mult, op1=ALU.mult),
                     r=[K("o1"), K("gcol"), K("gt%d" % gs)], w=[K("yb%d" % gs)])
                S.dma("pool", yT[128:256, q0:q0 + 512], yb[gs][:], r=[K("yb%d" % gs)], w=[("ysrc", q0 // 1024)])
        S.barrier()


def t5_bucket_np(dist):
    n = np.maximum(dist, 0)
    max_exact = 16
    nf = np.maximum(n, max_exact).astype(np.float32)
    large = max_exact + (np.log(nf / max_exact) / math.log(2048 / max_exact) * (32 - max_exact)).astype(np.int32)
    large = np.minimum(large, 31)
    return np.where(n < max_exact, n, large)


def attn_tables(rel_bias, kind, g):
    W = 2944 if kind == "A" else 2432
    ki = np.arange(128)[:, None]
    d = (np.arange(W)[None, :] - 384) - ki
    bk = t5_bucket_np(d)
    if kind == "A":
        mult = ((d >= 0) & (d <= 128)).astype(np.float32) \
            + ((d >= 0) & (d <= 512) & (d % 4 == 0)).astype(np.float32) \
            + ((d >= 0) & (d <= 2048) & (d % 16 == 0)).astype(np.float32)
        heads = [2 * g, 2 * g + 1]
    else:
        mult = (d >= 0).astype(np.float32)
        heads = [8 + g]
    bt = np.stack([np.where(mult > 0, rel_bias[:, h][bk], np.float32(0)) for h in heads]).astype(np.float32)
    mt = np.stack([mult for _ in heads]).astype(np.float32)
    return np.ascontiguousarray(bt), np.ascontiguousarray(mt)


def lb_compute(S, K, layer, z, e, w1, lb, oml, negoml, sl):
    S.op("act", lambda en: en.activation(out=e[:], in_=z[:], func=AF.Exp), r=[K("lbz")], w=[K("lbe")])
    S.op("dve", lambda en: en.tensor_tensor(out=w1[:], in0=e[:, 0, :], in1=e[:, 1, :], op=ALU.add), r=[K("lbe")], w=[K("lbw")])
    S.op("dve", lambda en: en.reciprocal(out=w1[:], in_=w1[:]), r=[K("lbw")], w=[K("lbw")])
    S.op("dve", lambda en: en.tensor_tensor(out=e[:, 0, :], in0=e[:, 0, :], in1=w1[:], op=ALU.mult), r=[K("lbe"), K("lbw")], w=[K("lbe")])
    S.op("dve", lambda en: en.tensor_tensor(out=e[:, 1, :], in0=e[:, 1, :], in1=w1[:], op=ALU.mult), r=[K("lbe"), K("lbw")], w=[K("lbe")])
    if layer == 0:
        S.op("dve", lambda en: en.tensor_tensor(out=lb[:], in0=e[:, 0, :], in1=e[:, 0, :], op=ALU.subtract), r=[K("lbe")], w=[K("lb")])
    else:
        S.op("dve", lambda en: en.tensor_tensor(out=lb[:], in0=e[:, 0, :], in1=e[:, 1, :], op=ALU.add), r=[K("lbe")], w=[K("lb")])
        S.op("dve", lambda en: en.tensor_tensor(out=lb[:], in0=lb[:], in1=e[:, 0, :], op=ALU.subtract), r=[K("lbe"), K("lb")], w=[K("lb")])
    S.op("dve", lambda en: en.tensor_scalar(out=lb[:], in0=lb[:], scalar1=0.0, scalar2=1.0, op0=ALU.max, op1=ALU.min),
         r=[K("lb")], w=[K("lb")])
    S.op("dve", lambda en: en.tensor_scalar(out=oml[:], in0=lb[:], scalar1=-1.0, scalar2=1.0, op0=ALU.mult, op1=ALU.add),
         r=[K("lb")], w=[K("oml")])
    S.op("dve", lambda en: en.tensor_scalar(out=negoml[:], in0=lb[:], scalar1=1.0, scalar2=-1.0, op0=ALU.mult, op1=ALU.add),
         r=[K("lb")], w=[K("oml")])


def phase_hgrn(nc, S, T, layer, scr_fm, scr_tm, consts, aux, yT, on_tile=None):
    NB = T // 128
    NG = T // 512
    pf = "d_"
    with contextlib.ExitStack() as es:
        sb = lambda n, s, d: es.enter_context(nc.sbuf_tensor(uq(pf + n), s, d))
        ps = lambda n: (S.psum_keys.add(pf + n), es.enter_context(nc.psum_tensor(uq(pf + n), [128, 512], F32)))[1]
        K = lambda n: pf + n
        triu = sb("triu", [128, 128], F32)
        ones32 = sb("ones32", [128, 128], F32)
        ones_n = sb("ones_n", [128, 128], BF16)
        zc = sb("zc", [128, 2, 1], F32); ec = sb("ec", [128, 2, 1], F32); wc = sb("wc", [128, 1], F32)
        lbc = sb("lbc", [128, 1], F32); omlc = sb("omlc", [128, 1], F32); nomlc = sb("nomlc", [128, 1], F32)
        zr = sb("zr", [128, 2, 128], F32); er = sb("er", [128, 2, 128], F32); wr = sb("wr", [128, 128], F32)
        lbr = sb("lbr", [128, 128], F32); omlr = sb("omlr", [128, 128], F32); nomlr = sb("nomlr", [128, 128], F32)
        gcol = sb("gcol", [128, 1], F32)
        tm = [sb("tm%d" % i, [128, 4, 256], F32) for i in range(2)]
        fT = [sb("fT%d" % i, [128, 512], F32) for i in range(2)]
        qT = [sb("qT%d" % i, [128, 512], F32) for i in range(2)]
        gt = [sb("gt%d" % i, [128, 512], F32) for i in range(2)]
        sg = sb("sg", [128, 4, 128], F32)
        logf = sb("logf", [128, 4, 128], F32)
        ktm = sb("ktm", [128, 4, 128], F32)
        vbf = sb("vbf", [128, 4, 128], BF16)
        kTt = sb("kTt", [128, 512], F32)
        bct = sb("bct", [128, 128], F32)
        bcs = sb("bcs", [128, 128], F32)
        aq = sb("aq", [128, 128], F32)
        eq1 = sb("eq1", [128, 128], F32)
        qtl = sb("qtl", [128, 128], BF16)
        q1 = sb("q1", [128, 128], BF16)
        ak = sb("ak", [128, 4, 128], F32)
        ktl = sb("ktl", [128, 4, 128], BF16)
        atb = sb("atb", [128, 128], BF16)
        dlt = sb("dlt", [128, 128], F32)
        kd = sb("kd", [128, 128], BF16)
        s32 = sb("s32", [128, 128], F32)
        sbf = sb("sbf", [128, 128], BF16)
        osb = sb("osb", [128, 512], F32)
        sqb = sb("sqb", [128, 512], BF16)
        rstd = sb("rstd", [128, 512], F32)
        yb = [sb("yb%d" % i, [128, 512], BF16) for i in range(2)]
        p_bc = ps("pbc"); p_bct = ps("pbct"); p_bcl = ps("pbcl"); p_a = ps("pa"); p_o = ps("po"); p_s = ps("pS"); p_q = ps("pq")

        S.dma("sp", triu[:], consts["triu"], w=[K("triu")])
        S.op("dve", lambda e: e.memset(ones32[:], 1.0), w=[K("ones32")])
        S.op("dve", lambda e: e.memset(ones_n[:], 1.0 / 128), w=[K("ones_n")])
        S.op("dve", lambda e: e.memset(s32[:], 0.0), w=[K("s32")])
        S.op("dve", lambda e: e.memset(sbf[:], 0.0), w=[K("sbf")])
        S.dma("sp", gcol[:], aux["hg_gain"], w=[K("gcol")])
        S.dma("sp", zc[:, :, 0], aux["lbz_col"], w=[K("c_lbz")])
        S.dma("sp", zr[:], aux["lbz_row"], w=[K("r_lbz")])
        lb_compute(S, lambda n: K("c_" + n), layer, zc, ec, wc, lbc, omlc, nomlc, None)
        lb_compute(S, lambda n: K("r_" + n), layer, zr, er, wr, lbr, omlr, nomlr, None)
        KC = lambda n: K("c_" + n)
        KR = lambda n: K("r_" + n)

        for G in range(NG):
            gs = G % 2
            q0 = G * 512
            S.dma("sp", tm[gs][:], scr_tm[q0:q0 + 512, :].rearrange("(s p) c -> p s c", p=128), r=["scr_tm"], w=[K("tm%d" % gs)])
            S.dma("sp", fT[gs][:], scr_fm[G_DF][:, q0:q0 + 512], r=["scr_fm"], w=[K("fT%d" % gs)])
            S.dma("sp", qT[gs][:], scr_fm[G_DQ][:, q0:q0 + 512], r=["scr_fm"], w=[K("qT%d" % gs)])
            S.dma("sp", gt[gs][:], scr_fm[G_DG][:, q0:q0 + 512], r=["scr_fm"], w=[K("gt%d" % gs)])
            S.op("act", lambda e: e.activation(out=sg[:], in_=tm[gs][:, :, 0:128], func=AF.Sigmoid), r=[K("tm%d" % gs)], w=[K("sg")])
            for s4 in range(4):
                S.op("dve", lambda e: e.tensor_tensor(out=sg[:, s4, :], in0=sg[:, s4, :], in1=omlr[:], op=ALU.mult),
                     r=[K("sg"), KR("oml")], w=[K("sg")])
                S.op("dve", lambda e: e.tensor_tensor(out=ktm[:, s4, :], in0=omlr[:], in1=sg[:, s4, :], op=ALU.subtract),
                     r=[K("sg"), KR("oml")], w=[K("ktm")])
                S.op("dve", lambda e: e.tensor_tensor(out=sg[:, s4, :], in0=sg[:, s4, :], in1=lbr[:], op=ALU.add),
                     r=[K("sg"), KR("lb")], w=[K("sg")])
            S.op("act", lambda e: e.activation(out=logf[:], in_=sg[:], func=AF.Ln), r=[K("sg")], w=[K("logf")])
            S.op("pool", lambda e: e.tensor_copy(out=vbf[:], in_=tm[gs][:, :, 128:256]), r=[K("tm%d" % gs)], w=[K("vbf")])
            S.op("act", lambda e: e.activation(out=kTt[:], in_=fT[gs][:], func=AF.Sigmoid), r=[K("fT%d" % gs)], w=[K("kTt")])
            S.op("dve", lambda e: e.tensor_scalar(out=kTt[:], in0=kTt[:], scalar1=nomlc[:, 0:1], scalar2=omlc[:, 0:1],
                                                   op0=ALU.mult, op1=ALU.add),
                 r=[K("kTt"), KC("oml")], w=[K("kTt")])
            for s4 in range(4):
                c0 = s4 * 128
                lf = logf[:, s4, :]
                S.op("pe", lambda e: e.matmul(p_bc[:, 0:128], triu[:], lf, start=True, stop=True), r=[K("triu"), K("logf")], w=[K("pbc")])
                S.op("pe", lambda e: e.matmul(p_bct[:, 0:128], lf, triu[:], start=True, stop=True), r=[K("triu"), K("logf")], w=[K("pbct")])
                S.op("pe", lambda e: e.matmul(p_bcl[:, 0:128], ones32[:], lf, start=True, stop=True), r=[K("ones32"), K("logf")], w=[K("pbcl")])
                S.op("dve", lambda e: e.tensor_copy(out=bct[:], in_=p_bct[:, 0:128]), r=[K("pbct")], w=[K("bct")])
                S.op("act", lambda e: e.activation(out=bcs[:], in_=p_bc[:, 0:128], func=AF.Copy), r=[K("pbc")], w=[K("bcs")])
                S.op("act", lambda e: e.activation(out=eq1[:], in_=bct[:], func=AF.Exp), r=[K("bct")], w=[K("eq1")])
                for I in range(4):
                    if I == 0:
                        S.op("dve", lambda e: e.tensor_copy(out=aq[:, 0:32], in_=bct[:, 0:32]), r=[K("bct")], w=[K("aq")])
                        S.op("dve", lambda e: e.tensor_scalar(out=ak[:, 0, :], in0=bct[:], scalar1=-60.0, scalar2=None, op0=ALU.max),
                             r=[K("bct")], w=[K("ak")])
                    else:
                        rc = bct[:, 32 * I - 1:32 * I]
                        S.op("dve", lambda e: e.tensor_scalar(out=aq[:, 32 * I:32 * I + 32], in0=bct[:, 32 * I:32 * I + 32],
                                                               scalar1=rc, scalar2=None, op0=ALU.subtract),
                             r=[K("bct")], w=[K("aq")])
                        S.op("dve", lambda e: e.tensor_scalar(out=ak[:, I, :], in0=bct[:], scalar1=rc, scalar2=-60.0,
                                                               op0=ALU.subtract, op1=ALU.max),
                             r=[K("bct")], w=[K("ak")])
                S.op("act", lambda e: e.activation(out=aq[:], in_=aq[:], func=AF.Exp), r=[K("aq")], w=[K("aq")])
                S.op("act", lambda e: e.activation(out=ak[:], in_=ak[:], func=AF.Exp, scale=-1.0), r=[K("ak")], w=[K("ak")])
                S.op("dve", lambda e: e.tensor_tensor(out=qtl[:], in0=aq[:], in1=qT[gs][:, c0:c0 + 128], op=ALU.mult),
                     r=[K("aq"), K("qT%d" % gs)], w=[K("qtl")])
                S.op("dve", lambda e: e.tensor_tensor(out=q1[:], in0=eq1[:], in1=qT[gs][:, c0:c0 + 128], op=ALU.mult),
                     r=[K("eq1"), K("qT%d" % gs)], w=[K("q1")])
                for I in range(4):
                    S.op("pool" if I % 2 else "dve",
                         lambda e: e.tensor_tensor(out=ktl[:, I, :], in0=ak[:, I, :], in1=kTt[:, c0:c0 + 128], op=ALU.mult),
                         r=[K("ak"), K("kTt")], w=[K("ktl")])
                for I in range(4):
                    S.op("pe", lambda e: e.matmul(p_a[:, 32 * I:32 * I + 32], ktl[:, I, :], qtl[:, 32 * I:32 * I + 32],
                                                  start=True, stop=True),
                         r=[K("ktl"), K("qtl")], w=[K("pa")])
                S.op("dve", lambda e: e.tensor_tensor(out=atb[:], in0=p_a[:, 0:128], in1=triu[:], op=ALU.mult),
                     r=[K("pa"), K("triu")], w=[K("atb")])
                S.op("dve", lambda e: e.tensor_tensor(out=dlt[:], in0=p_bcl[:, 0:128], in1=bcs[:], op=ALU.subtract),
                     r=[K("pbcl"), K("bcs")], w=[K("dlt")])
                S.op("act", lambda e: e.activation(out=dlt[:], in_=dlt[:], func=AF.Exp), r=[K("dlt")], w=[K("dlt")])
                S.op("dve", lambda e: e.tensor_tensor(out=kd[:], in0=dlt[:], in1=ktm[:, s4, :], op=ALU.mult),
                     r=[K("dlt"), K("ktm")], w=[K("kd")])
                S.op("pe", lambda e: e.matmul(p_o[:, c0:c0 + 128], sbf[:], q1[:], start=True, stop=False),
                     r=[K("sbf"), K("q1")], w=[K("po")])
                S.op("pe", lambda e: e.matmul(p_o[:, c0:c0 + 128], vbf[:, s4, :], atb[:], start=False, stop=True),
                     r=[K("vbf"), K("atb")], w=[K("po")])
                S.op("pe", lambda e: e.matmul(p_s[:, 0:128], kd[:], vbf[:, s4, :], start=True, stop=True),
                     r=[K("kd"), K("vbf")], w=[K("pS")])
                S.op("dve", lambda e: e.scalar_tensor_tensor(out=s32[:], in0=s32[:], scalar=eq1[:, 127:128], in1=p_s[:, 0:128],
                                                             op0=ALU.mult, op1=ALU.add),
                     r=[K("s32"), K("eq1"), K("pS")], w=[K("s32")])
                S.op("act", lambda e: e.activation(out=sbf[:], in_=s32[:], func=AF.Copy), r=[K("s32")], w=[K("sbf")])
            S.op("dve", lambda e: e.tensor_copy(out=osb[:], in_=p_o[:]), r=[K("po")], w=[K("osb")])
            S.op("act", lambda e: e.activation(out=sqb[:], in_=osb[:], func=AF.Square), r=[K("osb")], w=[K("sqb")])
            S.op("pe", lambda e: e.matmul(p_q[:], ones_n[:], sqb[:], start=True, stop=True), r=[K("ones_n"), K("sqb")], w=[K("pq")])
            S.op("act", lambda e: e.activation(out=rstd[:], in_=p_q[:], func=AF.Sqrt, bias=1e-6, scale=1.0), r=[K("pq")], w=[K("rstd")])
            S.op("dve", lambda e: e.reciprocal(out=rstd[:], in_=rstd[:]), r=[K("rstd")], w=[K("rstd")])
            S.op("dve", lambda e: e.tensor_tensor(out=osb[:], in0=osb[:], in1=rstd[:], op=ALU.mult), r=[K("osb"), K("rstd")], w=[K("osb")])
            S.op("dve", lambda e: e.scalar_tensor_tensor(out=yb[gs][:], in0=osb[:], scalar=gcol[:, 0:1], in1=gt[gs][:],
                                                         op0=ALU.mult, op1=ALU.mult),
                 r=[K("osb"), K("gcol"), K("gt%d" % gs)], w=[K("yb%d" % gs)])
            S.dma("pool", yT[384:512, q0:q0 + 512], yb[gs][:], r=[K("yb%d" % gs)], w=[("ysrc", q0 // 1024)])
            if on_tile is not None:
                on_tile(G)
        S.barrier()


def phase_gdn(nc, S, T, scr_fm, scr_bd, consts, aux, yT):
    NB = T // 128
    NG = T // 512
    pf = "c_"
    with contextlib.ExitStack() as es:
        sb = lambda n, s, d: es.enter_context(nc.sbuf_tensor(uq(pf + n), s, d))
        ps = lambda n: (S.psum_keys.add(pf + n), es.enter_context(nc.psum_tensor(uq(pf + n), [128, 512], F32)))[1]
        K = lambda n: pf + n
        triu = sb("triu", [128, 128], F32)
        trius = sb("trius", [128, 128], F32)
        trils = sb("trils", [128, 128], F32)
        ident = sb("ident", [128, 128], F32)
        ones32 = sb("ones32", [128, 128], F32)
        ones_b = sb("ones_b", [128, 128], BF16)
        ones_n = sb("ones_n", [128, 128], BF16)
        cw = sb("cw", [128, 3, 4], F32)
        ab = sb("ab", [128, 2], F32)
        negA = sb("negA", [128, 1], F32)
        gcol = sb("gcol", [128, 1], F32)
        bd = sb("bd", [128, NB, 2], F32)
        beta = sb("beta", [128, NB], F32)
        nbeta = sb("nbeta", [128, NB], F32)
        gall = sb("gall", [128, NB], F32)
        Gc = sb("Gc", [128, NB], F32)
        bg = sb("bg", [128, NB], F32)
        xin = [sb("xin%d" % i, [128, 3, 515], F32) for i in range(2)]
        gt = [sb("gt%d" % i, [128, 512], F32) for i in range(2)]
        cv = sb("cv", [128, 3, 512], F32)
        sq = sb("sq", [128, 512], BF16)
        rs = sb("rs", [128, 512], F32)
        qTb = sb("qTb", [128, 512], BF16)
        kTb = sb("kTb", [128, 512], BF16)
        vTb = sb("vTb", [128, 512], BF16)
        identb = sb("identb", [128, 128], BF16)
        ktm = sb("ktm", [128, 128], F32)
        vtm = sb("vtm", [128, 128], F32)
        bv = sb("bv", [128, 128], F32)
        kbg = sb("kbg", [128, 128], F32)
        kdec = sb("kdec", [128, 128], BF16)
        gb = sb("gb", [128, 128], F32)
        grow = sb("grow", [128, 128], F32)
        x1 = sb("x1", [128, 128], F32)
        x2 = sb("x2", [128, 128], F32)
        gTi = sb("gTi", [128, 128], F32)
        gLs = sb("gLs", [128, 128], F32)
        eg = sb("eg", [128, 128], F32)
        Y = [sb("Y%d" % i, [128, 128], F32) for i in range(2)]
        YT = [sb("YT%d" % i, [128, 128], F32) for i in range(2)]
        XT = sb("XT", [128, 128], F32)
        yh = sb("yh", [128, 128], BF16)
        yl = sb("yl", [128, 128], BF16)
        yh32 = sb("yh32", [128, 128], F32)
        usb = sb("usb", [128, 128], F32)
        wT = sb("wT", [128, 128], BF16)
        aqk = sb("aqk", [128, 128], BF16)
        qdec = sb("qdec", [128, 128], BF16)
        cl = sb("cl", [128, 1], F32)
        vnew = sb("vnew", [128, 128], BF16)
        s32 = sb("s32", [128, 128], F32)
        sbf = sb("sbf", [128, 128], BF16)
        osb = sb("osb", [128, 512], F32)
        rstd = sb("rstd", [128, 512], F32)
        yb = [sb("yb%d" % i, [128, 512], BF16) for i in range(2)]
        p_ss = ps("pss"); p_tr = ps("ptr"); p_gk = ps("pgk"); p_yy = ps("pyy"); p_yt = ps("pyt"); p_yx = ps("pyx")
        p_o = ps("po"); p_s = ps("pS")

        S.dma("sp", triu[:], consts["triu"], w=[K("triu")])
        S.dma("sp", trius[:], consts["trius"], w=[K("trius")])
        S.dma("sp", trils[:], consts["trils"], w=[K("trils")])
        S.dma("sp", ident[:], consts["ident"], w=[K("ident")])
        S.op("dve", lambda e: e.memset(ones32[:], 1.0), w=[K("ones32")])
        S.op("dve", lambda e: e.tensor_copy(out=identb[:], in_=ident[:]), r=[K("ident")], w=[K("identb")])
        S.op("dve", lambda e: e.memset(ones_b[:], 1.0), w=[K("ones_b")])
        S.op("dve", lambda e: e.memset(ones_n[:], 1.0 / 128), w=[K("ones_n")])
        S.op("dve", lambda e: e.memset(s32[:], 0.0), w=[K("s32")])
        S.op("dve", lambda e: e.memset(sbf[:], 0.0), w=[K("sbf")])
        S.dma("sp", cw[:], aux["conv"], w=[K("cw")])
        S.dma("sp", ab[:], aux["a_dt"], w=[K("ab")])
        S.dma("sp", gcol[:], aux["dn_gain"], w=[K("gcol")])
        for n0 in range(0, NB, 16):
            n1 = min(NB, n0 + 16)
            S.dma("sp", bd[:, n0:n1, :], scr_bd[n0 * 128:n1 * 128, :].rearrange("(n p) c -> p n c", p=128),
                  r=["scr_bd"], w=[K("bd")])
        S.op("act", lambda e: e.activation(out=beta[:], in_=bd[:, :, 0], func=AF.Sigmoid), r=[K("bd")], w=[K("beta")])
        S.op("dve", lambda e: e.tensor_scalar(out=nbeta[:], in0=beta[:], scalar1=-1.0, scalar2=None, op0=ALU.mult),
             r=[K("beta")], w=[K("nbeta")])
        S.op("act", lambda e: e.activation(out=negA[:], in_=ab[:, 0:1], func=AF.Exp), r=[K("ab")], w=[K("negA")])
        S.op("dve", lambda e: e.tensor_scalar(out=negA[:], in0=negA[:], scalar1=-1.0, scalar2=None, op0=ALU.mult),
             r=[K("negA")], w=[K("negA")])
        S.op("act", lambda e: e.activation(out=gall[:], in_=bd[:, :, 1], func=AF.Exp, bias=ab[:, 1:2], scale=1.0),
             r=[K("bd"), K("ab")], w=[K("gall")])
        S.op("act", lambda e: e.activation(out=gall[:], in_=gall[:], func=AF.Ln, bias=1.0, scale=1.0),
             r=[K("gall")], w=[K("gall")])
        S.op("dve", lambda e: e.tensor_scalar(out=gall[:], in0=gall[:], scalar1=negA[:, 0:1], scalar2=None, op0=ALU.mult),
             r=[K("gall"), K("negA")], w=[K("gall")])
        S.op("pe", lambda e: e.matmul(p_ss[:, 0:NB], triu[:], gall[:], start=True, stop=True),
             r=[K("triu"), K("gall")], w=[K("pss")])
        S.op("dve", lambda e: e.tensor_copy(out=Gc[:], in_=p_ss[:, 0:NB]), r=[K("pss")], w=[K("Gc")])
        S.op("act", lambda e: e.activation(out=bg[:], in_=Gc[:], func=AF.Exp), r=[K("Gc")], w=[K("bg")])
        S.op("dve", lambda e: e.tensor_tensor(out=bg[:], in0=bg[:], in1=beta[:], op=ALU.mult), r=[K("bg"), K("beta")], w=[K("bg")])

        for G in range(NG):
            gs = G % 2
            q0 = G * 512
            kx = K("xin%d" % gs)
            for j, grp in enumerate((G_CQ, G_CK, G_CV)):
                if G == 0:
                    S.op("dve", lambda e: e.memset(xin[gs][:, j, 0:3], 0.0), w=[kx])
                    S.dma("sp", xin[gs][:, j, 3:515], scr_fm[grp][:, 0:512], r=["scr_fm"], w=[kx])
                else:
                    S.dma("sp", xin[gs][:, j, :], scr_fm[grp][:, q0 - 3:q0 + 512], r=["scr_fm"], w=[kx])
            S.dma("sp", gt[gs][:], scr_fm[G_CZ][:, q0:q0 + 512], r=["scr_fm"], w=[K("gt%d" % gs)])
            for j in range(3):
                en = "dve"
                S.op(en, lambda e: e.tensor_scalar(out=cv[:, j, :], in0=xin[gs][:, j, 3:515], scalar1=cw[:, j, 3:4],
                                                   scalar2=None, op0=ALU.mult), r=[kx, K("cw")], w=[K("cv%d" % j)])
                for tap in (2, 1, 0):
                    S.op(en, lambda e: e.scalar_tensor_tensor(out=cv[:, j, :], in0=xin[gs][:, j, tap:tap + 512],
                                                              scalar=cw[:, j, tap:tap + 1], in1=cv[:, j, :],
                                                              op0=ALU.mult, op1=ALU.add),
                         r=[kx, K("cw"), K("cv%d" % j)], w=[K("cv%d" % j)])
                S.op("act", lambda e: e.activation(out=cv[:, j, :], in_=cv[:, j, :], func=AF.Silu),
                     r=[K("cv%d" % j)], w=[K("cv%d" % j)])
            S.op("pool", lambda e: e.tensor_copy(out=vTb[:], in_=cv[:, 2, :]), r=[K("cv2")], w=[K("vTb")])
            for j in range(2):
                S.op("act", lambda e: e.activation(out=sq[:], in_=cv[:, j, :], func=AF.Square), r=[K("cv%d" % j)], w=[K("sq")])
                S.op("pe", lambda e: e.matmul(p_ss[:], ones_b[:], sq[:], start=True, stop=True), r=[K("ones_b"), K("sq")], w=[K("pss")])
                S.op("act", lambda e: e.activation(out=rs[:], in_=p_ss[:], func=AF.Sqrt, bias=1e-6, scale=1.0), r=[K("pss")], w=[K("rs")])
                S.op("dve", lambda e: e.reciprocal(out=rs[:], in_=rs[:]), r=[K("rs")], w=[K("rs")])
                if j == 0:
                    S.op("dve", lambda e: e.scalar_tensor_tensor(out=cv[:, 0, :], in0=cv[:, 0, :], scalar=128.0 ** -0.5, in1=rs[:],
                                                                 op0=ALU.mult, op1=ALU.mult),
                         r=[K("cv0"), K("rs")], w=[K("cv0")])
                    S.op("act", lambda e: e.activation(out=qTb[:], in_=cv[:, 0, :], func=AF.Copy), r=[K("cv0")], w=[K("qTb")])
                else:
                    S.op("dve", lambda e: e.tensor_tensor(out=cv[:, 1, :], in0=cv[:, 1, :], in1=rs[:], op=ALU.mult),
                         r=[K("cv1"), K("rs")], w=[K("cv1")])
                    S.op("act", lambda e: e.activation(out=kTb[:], in_=cv[:, 1, :], func=AF.Copy), r=[K("cv1")], w=[K("kTb")])
            for s4 in range(4):
                n = 4 * G + s4
                c0 = s4 * 128
                gcn = Gc[:, n:n + 1]
                S.op("pe", lambda e: e.matmul(p_tr[:, 0:128], kTb[:, c0:c0 + 128], identb[:], start=True, stop=True),
                     r=[K("kTb"), K("identb")], w=[K("ptr")])
                S.op("pe", lambda e: e.matmul(p_tr[:, 128:256], vTb[:, c0:c0 + 128], identb[:], start=True, stop=True),
                     r=[K("vTb"), K("identb")], w=[K("ptr")])
                S.op("dve", lambda e: e.tensor_copy(out=ktm[:], in_=p_tr[:, 0:128]), r=[K("ptr")], w=[K("ktm")])
                S.op("dve", lambda e: e.tensor_scalar(out=bv[:], in0=p_tr[:, 128:256], scalar1=beta[:, n:n + 1], scalar2=None, op0=ALU.mult),
                     r=[K("ptr"), K("beta")], w=[K("bv")])
                S.op("dve", lambda e: e.tensor_scalar(out=kbg[:], in0=ktm[:], scalar1=bg[:, n:n + 1], scalar2=None, op0=ALU.mult),
                     r=[K("ktm"), K("bg")], w=[K("kbg")])
                S.op("dve", lambda e: e.tensor_scalar(out=gb[:], in0=ones32[:], scalar1=gall[:, n:n + 1], scalar2=None, op0=ALU.mult),
                     r=[K("ones32"), K("gall")], w=[K("gb")])
                S.op("pe", lambda e: e.matmul(p_gk[:, 0:128], gb[:], triu[:], start=True, stop=True), r=[K("gb"), K("triu")], w=[K("pgk")])
                S.op("act", lambda e: e.activation(out=grow[:], in_=p_gk[:, 0:128], func=AF.Copy), r=[K("pgk")], w=[K("grow")])
                S.op("dve", lambda e: e.tensor_scalar(out=x1[:], in0=grow[:], scalar1=gcn, scalar2=0.0, op0=ALU.subtract, op1=ALU.min),
                     r=[K("grow"), K("Gc")], w=[K("x1")])
                S.op("dve", lambda e: e.tensor_scalar(out=x2[:], in0=grow[:], scalar1=gcn, scalar2=0.0, op0=ALU.subtract, op1=ALU.max),
                     r=[K("grow"), K("Gc")], w=[K("x2")])
                S.op("act", lambda e: e.activation(out=x1[:], in_=x1[:], func=AF.Exp), r=[K("x1")], w=[K("x1")])
                S.op("act", lambda e: e.activation(out=x2[:], in_=x2[:], func=AF.Exp, scale=-1.0), r=[K("x2")], w=[K("x2")])
                S.op("act", lambda e: e.activation(out=eg[:], in_=grow[:], func=AF.Exp), r=[K("grow")], w=[K("eg")])
                S.op("pool", lambda e: e.tensor_tensor(out=gTi[:], in0=x1[:], in1=triu[:], op=ALU.mult), r=[K("x1"), K("triu")], w=[K("gTi")])
                S.op("pool", lambda e: e.tensor_tensor(out=gLs[:], in0=x2[:], in1=trils[:], op=ALU.mult), r=[K("x2"), K("trils")], w=[K("gLs")])
                S.op("pe", lambda e: e.matmul(p_tr[:, 256:384], kTb[:, c0:c0 + 128], kTb[:, c0:c0 + 128], start=True, stop=True),
                     r=[K("kTb")], w=[K("ptr")])
                S.op("dve", lambda e: e.scalar_tensor_tensor(out=Y[0][:], in0=p_tr[:, 256:384], scalar=nbeta[:, n:n + 1], in1=gLs[:],
                                                             op0=ALU.mult, op1=ALU.mult),
                     r=[K("ptr"), K("nbeta"), K("gLs")], w=[K("Y0")])
                S.op("act", lambda e: e.activation(out=yh[:], in_=Y[0][:], func=AF.Copy), r=[K("Y0")], w=[K("yh")])
                S.op("dve", lambda e: e.tensor_copy(out=yh32[:], in_=yh[:]), r=[K("yh")], w=[K("yh32")])
                S.op("dve", lambda e: e.tensor_tensor(out=yl[:], in0=Y[0][:], in1=yh32[:], op=ALU.subtract), r=[K("Y0"), K("yh32")], w=[K("yl")])
                S.op("pe", lambda e: e.matmul(p_tr[:, 384:512], yh[:], identb[:], start=True, stop=False), r=[K("yh"), K("identb")], w=[K("ptr")])
                S.op("pe", lambda e: e.matmul(p_tr[:, 384:512], yl[:], identb[:], start=False, stop=True), r=[K("yl"), K("identb")], w=[K("ptr")])
                S.op("dve", lambda e: e.tensor_copy(out=YT[0][:], in_=p_tr[:, 384:512]), r=[K("ptr")], w=[K("YT0")])
                S.op("dve", lambda e: e.tensor_tensor(out=XT[:], in0=p_tr[:, 384:512], in1=ident[:], op=ALU.add),
                     r=[K("ptr"), K("ident")], w=[K("XT")])
                cur = 0
                for lvl in range(1, 7):
                    nxt = 1 - cur
                    S.op("pe", lambda e: e.matmul(p_yy[:, 0:128], YT[cur][:], Y[cur][:], start=True, stop=True),
                         r=[K("Y%d" % cur), K("YT%d" % cur)], w=[K("pyy")])
                    if lvl < 6:
                        S.op("pe", lambda e: e.matmul(p_yt[:, 0:128], Y[cur][:], YT[cur][:], start=True, stop=True),
                             r=[K("Y%d" % cur), K("YT%d" % cur)], w=[K("pyt")])
                    S.op("act", lambda e: e.activation(out=Y[nxt][:], in_=p_yy[:, 0:128], func=AF.Copy), r=[K("pyy")], w=[K("Y%d" % nxt)])
                    if lvl < 6:
                        S.op("dve", lambda e: e.tensor_copy(out=YT[nxt][:], in_=p_yt[:, 0:128]), r=[K("pyt")], w=[K("YT%d" % nxt)])
                    S.op("pe", lambda e: e.matmul(p_yx[:, 0:128], Y[nxt][:], XT[:], start=True, stop=True),
                         r=[K("Y%d" % nxt), K("XT")], w=[K("pyx")])
                    S.op("dve", lambda e: e.tensor_tensor(out=XT[:], in0=p_yx[:, 0:128], in1=XT[:], op=ALU.add),
                         r=[K("pyx"), K("XT")], w=[K("XT")])
                    cur = nxt
                S.op("pe", lambda e: e.matmul(p_yy[:, 0:128], XT[:], bv[:], start=True, stop=True), r=[K("XT"), K("bv")], w=[K("pyy")])
                S.op("pe", lambda e: e.matmul(p_yt[:, 0:128], kbg[:], XT[:], start=True, stop=True), r=[K("XT"), K("kbg")], w=[K("pyt")])
                S.op("pe", lambda e: e.matmul(p_gk[:, 256:384], kTb[:, c0:c0 + 128], qTb[:, c0:c0 + 128], start=True, stop=True),
                     r=[K("kTb"), K("qTb")], w=[K("pgk")])
                S.op("act", lambda e: e.activation(out=usb[:], in_=p_yy[:, 0:128], func=AF.Copy), r=[K("pyy")], w=[K("usb")])
                S.op("dve", lambda e: e.tensor_copy(out=wT[:], in_=p_yt[:, 0:128]), r=[K("pyt")], w=[K("wT")])
                S.op("dve", lambda e: e.tensor_tensor(out=aqk[:], in0=p_gk[:, 256:384], in1=gTi[:], op=ALU.mult),
                     r=[K("pgk"), K("gTi")], w=[K("aqk")])
                S.op("dve", lambda e: e.tensor_tensor(out=qdec[:], in0=cv[:, 0, c0:c0 + 128], in1=eg[:], op=ALU.mult),
                     r=[K("cv0"), K("eg")], w=[K("qdec")])
                S.op("dve", lambda e: e.tensor_tensor(out=cl[:], in0=grow[:, 127:128], in1=gcn, op=ALU.subtract),
                     r=[K("grow"), K("Gc")], w=[K("cl")])
                S.op("act", lambda e: e.activation(out=cl[:], in_=cl[:], func=AF.Exp), r=[K("cl")], w=[K("cl")])
                S.op("dve", lambda e: e.tensor_scalar(out=kdec[:], in0=ktm[:], scalar1=cl[:, 0:1], scalar2=None, op0=ALU.mult),
                     r=[K("ktm"), K("cl")], w=[K("kdec")])
                S.op("pe", lambda e: e.matmul(p_gk[:, 128:256], wT[:], sbf[:], start=True, stop=True), r=[K("wT"), K("sbf")], w=[K("pgk")])
                S.op("dve", lambda e: e.tensor_tensor(out=vnew[:], in0=usb[:], in1=p_gk[:, 128:256], op=ALU.subtract),
                     r=[K("usb"), K("pgk")], w=[K("vnew")])
                S.op("pe", lambda e: e.matmul(p_o[:, c0:c0 + 128], sbf[:], qdec[:], start=True, stop=False), r=[K("sbf"), K("qdec")], w=[K("po")])
                S.op("pe", lambda e: e.matmul(p_o[:, c0:c0 + 128], vnew[:], aqk[:], start=False, stop=True), r=[K("vnew"), K("aqk")], w=[K("po")])
                S.op("pe", lambda e: e.matmul(p_s[:, 0:128], kdec[:], vnew[:], start=True, stop=True), r=[K("kdec"), K("vnew")], w=[K("pS")])
                S.op("dve", lambda e: e.scalar_tensor_tensor(out=s32[:], in0=s32[:], scalar=eg[:, 127:128], in1=p_s[:, 0:128],
                                                             op0=ALU.mult, op1=ALU.add),
                     r=[K("s32"), K("eg"), K("pS")], w=[K("s32")])
                S.op("act", lambda e: e.activation(out=sbf[:], in_=s32[:], func=AF.Copy), r=[K("s32")], w=[K("sbf")])
            S.op("dve", lambda e: e.tensor_copy(out=osb[:], in_=p_o[:]), r=[K("po")], w=[K("osb")])
            S.op("act", lambda e: e.activation(out=sq[:], in_=osb[:], func=AF.Square), r=[K("osb")], w=[K("sq")])
            S.op("pe", lambda e: e.matmul(p_ss[:], ones_n[:], sq[:], start=True, stop=True), r=[K("ones_n"), K("sq")], w=[K("pss")])
            S.op("act", lambda e: e.activation(out=rstd[:], in_=p_ss[:], func=AF.Sqrt, bias=1e-6, scale=1.0), r=[K("pss")], w=[K("rstd")])
            S.op("dve", lambda e: e.reciprocal(out=rstd[:], in_=rstd[:]), r=[K("rstd")], w=[K("rstd")])
            S.op("dve", lambda e: e.tensor_tensor(out=osb[:], in0=osb[:], in1=rstd[:], op=ALU.mult), r=[K("osb"), K("rstd")], w=[K("osb")])
            S.op("dve", lambda e: e.scalar_tensor_tensor(out=yb[gs][:], in0=osb[:], scalar=gcol[:, 0:1], in1=gt[gs][:],
                                                         op0=ALU.mult, op1=ALU.mult),
                 r=[K("osb"), K("gcol"), K("gt%d" % gs)], w=[K("yb%d" % gs)])
            S.dma("pool", yT[256:384, q0:q0 + 512], yb[gs][:], r=[K("yb%d" % gs)], w=[("ysrc", q0 // 1024)])
        S.barrier()


class YOut:
    def __init__(self, ysrc):
        self.y = ysrc

    def __getitem__(self, idx):
        rs, cs = idx
        k, off = cs.start // 1024, cs.start % 1024
        return self.y[k][rs.start:rs.stop, off:off + (cs.stop - cs.start)]


def exchange(S, src, dst, n, rkey, wkey):
    for k in range(n):
        S.coll(src[k], dst[k], r=[rkey], w=[wkey])
    S.barrier()


def phase_m1(nc, S, T, wm_g, wbr_g, scr_h, ydst, msrc, mdst):
    NT = T // 512
    with contextlib.ExitStack() as es:
        sb = lambda n, s, d: es.enter_context(nc.sbuf_tensor(uq("m_" + n), s, d))
        ps = lambda n: (S.psum_keys.add("m_" + n), es.enter_context(nc.psum_tensor(uq("m_" + n), [128, 512], F32)))[1]
        K = lambda n: "m_" + n
        wm = sb("wm", [128, 16, 4, 512], BF16)
        wb = sb("wb", [128, 4, 4, 512], BF16)
        st = [sb("st%d" % i, [128, 2048], F32) for i in range(2)]
        hT = [sb("hT%d" % i, [128, 16, 512], BF16) for i in range(2)]
        yt = [sb("yt%d" % i, [128, 16, 512], BF16) for i in range(2)]
        mx = [sb("mx%d" % i, [128, 4, 512], BF16) for i in range(2)]
        gsb = [sb("g%d" % i, [128, 512], F32) for i in range(2)]
        acc = sb("acc", [128, 512], F32)
        tmp = sb("tmp", [128, 512], F32)
        p_l = [ps("pl%d" % i) for i in range(3)]
        p_z = [ps("pz%d" % i) for i in range(3)]
        wm_v = wm_g.rearrange("(c p) b j -> c p (b j)", p=128)
        for c in range(16):
            sl = c % 2
            S.dma("sp", st[sl][:, :], wm_v[c], w=[K("st%d" % sl)])
            if c % 2:
                S.op("act", lambda e: e.activation(out=wm[:, c, :, :].rearrange("p b j -> p (b j)"), in_=st[sl][:, :], func=AF.Copy),
                     r=[K("st%d" % sl)], w=[K("wm")])
            else:
                S.op("dve", lambda e: e.tensor_copy(out=wm[:, c, :, :].rearrange("p b j -> p (b j)"), in_=st[sl][:, :]),
                     r=[K("st%d" % sl)], w=[K("wm")])
        for br in range(4):
            sl = br % 2
            S.dma("sp", st[sl][:, :].rearrange("p (c j) -> p c j", c=4), wbr_g[br].rearrange("(c p) j -> p c j", p=128), w=[K("st%d" % sl)])
            S.op("dve", lambda e: e.tensor_copy(out=wb[:, :, br, :], in_=st[sl][:, :].rearrange("p (c j) -> p c j", c=4)),
                 r=[K("st%d" % sl)], w=[K("wb")])
        li = zi = 0
        for t in range(NT):
            sl = t % 2
            t0 = t * 512
            k, off = t0 // 1024, t0 % 1024
            S.dma("sp", hT[sl][:], scr_h[t], r=["scr_h"], w=[K("hT%d" % sl)])
            yv = ydst[k].rearrange("(g b p) t -> p b g t", g=4, b=4, p=128)
            for br in range(4):
                S.dma("sp", yt[sl][:, 4 * br:4 * br + 4, :], yv[:, br, :, off:off + 512], r=[("ydst", k)], w=[K("yt%d" % sl)])
            for fo in range(4):
                for br in range(4):
                    pl = p_l[li % 3]; kl = K("pl%d" % (li % 3)); li += 1
                    pz = p_z[zi % 3]; kz = K("pz%d" % (zi % 3)); zi += 1
                    g = gsb[br % 2]; kg = K("g%d" % (br % 2))
                    for c in range(16):
                        S.op("pe", lambda e: e.matmul(pl[:], wm[:, c, br, fo * 128:(fo + 1) * 128], hT[sl][:, c, :], start=(c == 0), stop=(c == 15)),
                             r=[K("wm"), K("hT%d" % sl)], w=[kl])
                    S.op("act", lambda e: e.activation(out=g[:], in_=pl[:], func=AF.Sigmoid), r=[kl], w=[kg])
                    for c4 in range(4):
                        S.op("pe", lambda e: e.matmul(pz[:], wb[:, c4, br, fo * 128:(fo + 1) * 128], yt[sl][:, 4 * br + c4, :],
                                                      start=(c4 == 0), stop=(c4 == 3)),
                             r=[K("wb"), K("yt%d" % sl)], w=[kz])
                    if br == 0:
                        S.op("dve", lambda e: e.tensor_tensor(out=acc[:], in0=pz[:], in1=g[:], op=ALU.mult), r=[kz, kg], w=[K("acc")])
                    else:
                        S.op("dve", lambda e: e.tensor_tensor(out=tmp[:], in0=pz[:], in1=g[:], op=ALU.mult), r=[kz, kg], w=[K("tmp")])
                        if br < 3:
                            S.op("dve", lambda e: e.tensor_tensor(out=acc[:], in0=acc[:], in1=tmp[:], op=ALU.add),
                                 r=[K("acc"), K("tmp")], w=[K("acc")])
                        else:
                            S.op("dve", lambda e: e.tensor_tensor(out=mx[sl][:, fo, :], in0=acc[:], in1=tmp[:], op=ALU.add),
                                 r=[K("acc"), K("tmp")], w=[K("mx%d" % sl)])
            S.dma("act", msrc[k].rearrange("(f p) t -> p f t", p=128)[:, :, off:off + 512], mx[sl][:],
                  r=[K("mx%d" % sl)], w=[("msrc", k)])
            if t % 2 == 1:
                S.coll(msrc[k], mdst[k], r=[("msrc", k)], w=[("mdst", k)])
        S.barrier()


def phase_m2(nc, S, T, wout_g, mdst, xmine_tile, xsrc_out, xdst_out, lay):
    NT = T // 512
    with contextlib.ExitStack() as es:
        sb = lambda n, s, d: es.enter_context(nc.sbuf_tensor(uq("o_" + n), s, d))
        ps = lambda n: (S.psum_keys.add("o_" + n), es.enter_context(nc.psum_tensor(uq("o_" + n), [128, 512], F32)))[1]
        K = lambda n: "o_" + n
        wo = sb("wo", [128, 16, 512], BF16)
        st = [sb("st%d" % i, [128, 2048], F32) for i in range(2)]
        mt = [sb("mt%d" % i, [128, 16, 512], BF16) for i in range(2)]
        xr = [sb("xr%d" % i, [128, 4, 512], F32) for i in range(2)]
        xn = [sb("xn%d" % i, [128, 4, 512], F32) for i in range(2)]
        p_o = [ps("po%d" % i) for i in range(4)]
        wv = wout_g.rearrange("(c p) j -> p c j", p=128)
        for c4 in range(4):
            sl = c4 % 2
            S.dma("sp", st[sl][:, :].rearrange("p (c j) -> p c j", c=4), wv[:, 4 * c4:4 * c4 + 4, :], w=[K("st%d" % sl)])
            S.op("dve", lambda e: e.tensor_copy(out=wo[:, 4 * c4:4 * c4 + 4, :], in_=st[sl][:, :].rearrange("p (c j) -> p c j", c=4)),
                 r=[K("st%d" % sl)], w=[K("wo")])
        oi = 0
        for t in range(NT):
            sl = t % 2
            t0 = t * 512
            k, off = t0 // 1024, t0 % 1024
            S.dma("sp", mt[sl][:], mdst[k].rearrange("(c p) t -> p c t", p=128)[:, :, off:off + 512], r=[("mdst", k)], w=[K("mt%d" % sl)])
            S.dma("sp", xr[sl][:], xmine_tile(t), r=[("xsrc", lay - 1, t)], w=[K("xr%d" % sl)])
            for fo in range(4):
                po = p_o[oi % 4]; kpo = K("po%d" % (oi % 4)); oi += 1
                for c in range(16):
                    S.op("pe", lambda e: e.matmul(po[:], wo[:, c, fo * 128:(fo + 1) * 128], mt[sl][:, c, :], start=(c == 0), stop=(c == 15)),
                         r=[K("wo"), K("mt%d" % sl)], w=[kpo])
                S.op("dve", lambda e: e.tensor_tensor(out=xn[sl][:, fo, :], in0=po[:], in1=xr[sl][:, fo, :], op=ALU.add),
                     r=[kpo, K("xr%d" % sl)], w=[K("xn%d" % sl)])
            S.dma("pool", xsrc_out[t].rearrange("(f p) t -> p f t", p=128), xn[sl][:], r=[K("xn%d" % sl)], w=[("xsrc", lay, t)])
            S.coll(xsrc_out[t], xdst_out[t], r=[("xsrc", lay, t)], w=[("xdst", lay, t)])
        S.barrier()


def phase_fnorm(nc, S, T, xdst, xmine, fgain4, outT):
    NT = T // 512
    with contextlib.ExitStack() as es:
        sb = lambda n, s, d: es.enter_context(nc.sbuf_tensor(uq("f_" + n), s, d))
        ps = lambda n: (S.psum_keys.add("f_" + n), es.enter_context(nc.psum_tensor(uq("f_" + n), [128, 512], F32)))[1]
        K = lambda n: "f_" + n
        fg = sb("fg", [128, 4], F32)
        ones = sb("ones", [128, 128], BF16)
        xs = [sb("xs%d" % i, [128, 16, 512], F32) for i in range(2)]
        xm = [sb("xm%d" % i, [128, 4, 512], F32) for i in range(2)]
        sq = sb("sq", [128, 16, 512], BF16)
        rstd = sb("rstd", [128, 512], F32)
        ot = [sb("ot%d" % i, [128, 4, 512], F32) for i in range(2)]
        p_q = ps("pq")
        S.op("dve", lambda e: e.memset(ones[:], 1.0 / D), w=[K("ones")])
        S.dma("sp", fg[:], fgain4, w=[K("fg")])
        for t in range(NT):
            sl = t % 2
            t0 = t * 512
            for c4 in range(4):
                S.dma("sp", xs[sl][:, 4 * c4:4 * c4 + 4, :], xdst[t].rearrange("(c p) t -> p c t", p=128)[:, 4 * c4:4 * c4 + 4, :],
                      r=[("xdst", DEPTH - 1, t)], w=[K("xs%d" % sl)])
            S.dma("sp", xm[sl][:], xmine[t].rearrange("(f p) t -> p f t", p=128), r=[("xsrc", DEPTH - 1, t)], w=[K("xm%d" % sl)])
            S.op("act", lambda e: e.activation(out=sq[:], in_=xs[sl][:], func=AF.Square), r=[K("xs%d" % sl)], w=[K("sq")])
            for c in range(16):
                S.op("pe", lambda e: e.matmul(p_q[:], ones[:], sq[:, c, :], start=(c == 0), stop=(c == 15)),
                     r=[K("sq"), K("ones")], w=[K("pq")])
            S.op("act", lambda e: e.activation(out=rstd[:], in_=p_q[:], func=AF.Sqrt, bias=1e-6, scale=1.0), r=[K("pq")], w=[K("rstd")])
            S.op("dve", lambda e: e.reciprocal(out=rstd[:], in_=rstd[:]), r=[K("rstd")], w=[K("rstd")])
            for f in range(4):
                S.op("dve", lambda e: e.scalar_tensor_tensor(out=ot[sl][:, f, :], in0=xm[sl][:, f, :], scalar=fg[:, f:f + 1], in1=rstd[:],
                                                             op0=ALU.mult, op1=ALU.mult),
                     r=[K("xm%d" % sl), K("fg"), K("rstd")], w=[K("ot%d" % sl)])
            S.dma("pool", outT.rearrange("(f p) t -> p f t", p=128)[:, :, t0:t0 + 512], ot[sl][:], r=[K("ot%d" % sl)], w=["outT"])
        S.barrier()


OFF = {"a_q": 0, "a_k": 512, "a_v": 1024, "a_gate": 1536, "b_q": 2048, "b_k": 2560, "b_v": 3072, "b_gate": 3584,
       "c_q": 4096, "c_k": 4608, "c_v": 5120, "c_z": 5632, "c_beta": 6144, "c_a": 6148,
       "d_q": 6152, "d_f": 6664, "d_i": 7176, "d_gate": 7688, "merge": 8200}
FM_ORDER = ("a_q", "a_k", "b_q", "b_k", "a_gate", "b_gate", "c_q", "c_k", "c_v", "c_z", "d_q", "d_f", "d_gate")
TM_ORDER = ("a_v", "b_v", "d_f", "d_i")
DEPTH = 2


def build_fused(T):
    nc = bass.Bass("TRN2", target_bir_lowering=False)
    inp = lambda n, s, d=F32: nc.dram_tensor(n, s, d, kind="ExternalInput").ap()
    NT = T // 512
    NK = T // 1024
    xT = inp("xT", [D, T])
    xmine0 = inp("xmine0", [512, T])
    btA = inp("btA", [2, 128, 2944]); mtA = inp("mtA", [2, 128, 2944])
    btB = inp("btB", [1, 128, 2432]); mtB = inp("mtB", [1, 128, 2432])
    b31 = inp("b31", [128, 1])
    consts = {k: inp(k, [128, 128]) for k in ("triu", "trius", "trils", "ident")}
    lbz_col = inp("lbz_col", [128, 2]); lbz_row = inp("lbz_row", [128, 2, 128])
    fgain4 = inp("fgain4", [128, 4])
    L = []
    for l in range(DEPTH):
        L.append({
            "wfm": inp("wfm%d" % l, [D, NFM * 128]), "wtm": inp("wtm%d" % l, [D, NTM + 2]), "gain16": inp("gain16_%d" % l, [128, 16]),
            "auxB": {"b31": b31, "dl": inp("dl%d" % l, [128, 256]), "subln": inp("subln%d" % l, [128, 1]),
                     "lam_init": 0.8 - 0.6 * math.exp(-0.3 * l)},
            "auxC": {"conv": inp("conv%d" % l, [128, 3, 4]), "a_dt": inp("a_dt%d" % l, [128, 2]), "dn_gain": inp("dn_gain%d" % l, [128, 1])},
            "auxD": {"hg_gain": inp("hg_gain%d" % l, [128, 1]), "lbz_col": lbz_col, "lbz_row": lbz_row},
            "wm_g": inp("wm_g%d" % l, [D, 4, 512]), "wbr_g": inp("wbr_g%d" % l, [4, 512, 512]), "wout_g": inp("wout_g%d" % l, [D, 512]),
        })
    scr_qk = nc.dram_tensor("scr_qk", [NQK, 128, T], BF16).ap()
    scr_fm = nc.dram_tensor("scr_fm", [NFM32, 128, T], F32).ap()
    scr_v = nc.dram_tensor("scr_v", [T, 256], BF16).ap()
    scr_tm = nc.dram_tensor("scr_tm", [T, 256], F32).ap()
    scr_bd = nc.dram_tensor("scr_bd", [T, 2], F32).ap()
    scr_h = nc.dram_tensor("scr_h", [NT, 128, 16, 512], BF16).ap()
    ysrc = nc.dram_tensor("ysrc", [NK, 512, 1024], BF16).ap()
    ydst = nc.dram_tensor("ydst", [NK, 2048, 1024], BF16).ap()
    msrc = nc.dram_tensor("msrc", [NK, 512, 1024], BF16).ap()
    mdst = nc.dram_tensor("mdst", [NK, 2048, 1024], BF16).ap()
    xsrc = [nc.dram_tensor("xsrc%d" % l, [NT, 512, 512], F32).ap() for l in range(DEPTH)]
    xdst = [nc.dram_tensor("xdst%d" % l, [NT, 2048, 512], F32).ap() for l in range(DEPTH)]
    outT = nc.dram_tensor("outT", [512, T], F32, kind="ExternalOutput").ap()
    yT = YOut(ysrc)
    xT_v = xT.rearrange("(c p) t -> p c t", p=128)
    xm0_v = xmine0.rearrange("(f p) t -> p f t", p=128)
    with contextlib.ExitStack() as es:
        S = Sched(nc, es)
        for l in range(DEPTH):
            P = L[l]
            UNIQ[0] = l
            if l == 0:
                xtile = lambda t, c4: xT_v[:, 4 * c4:4 * c4 + 4, t * 512:(t + 1) * 512]
                xmine_tile = lambda t: xm0_v[:, :, t * 512:(t + 1) * 512]
            else:
                xd = xdst[l - 1]
                xp = xsrc[l - 1]
                xtile = lambda t, c4: xd[t].rearrange("(c p) t -> p c t", p=128)[:, 4 * c4:4 * c4 + 4, :]
                xmine_tile = lambda t: xp[t].rearrange("(f p) t -> p f t", p=128)
            phase_proj(nc, S, T, xtile, P["wfm"], P["wtm"], P["gain16"], scr_qk, scr_fm, scr_v, scr_tm, scr_bd, scr_h, lay=l)
            phase_attn(nc, S, T, "A", scr_qk, scr_fm, scr_v, btA, mtA, None, yT)
            phase_attn(nc, S, T, "B", scr_qk, scr_fm, scr_v, btB, mtB, P["auxB"], yT)
            phase_gdn(nc, S, T, scr_fm, scr_bd, consts, P["auxC"], yT)

            def ytile_done(G):
                if G % 2 == 1:
                    k = G // 2
                    S.coll(ysrc[k], ydst[k], r=[("ysrc", k)], w=[("ydst", k)])
            phase_hgrn(nc, S, T, l, scr_fm, scr_tm, consts, P["auxD"], yT, on_tile=ytile_done)
            phase_m1(nc, S, T, P["wm_g"], P["wbr_g"], scr_h, ydst, msrc, mdst)
            phase_m2(nc, S, T, P["wout_g"], mdst, xmine_tile, xsrc[l], xdst[l], l)
        phase_fnorm(nc, S, T, xdst[DEPTH - 1], xsrc[DEPTH - 1], fgain4, outT)
        S.finish()
    return nc


def core_inputs(inputs, xT_b, g):
    l16 = lambda v: np.ascontiguousarray(v.reshape(16, 128).T)
    col = lambda v: np.ascontiguousarray(v.reshape(128, 1))
    rel_bias = inputs["rel_bias"]
    btA, mtA = attn_tables(rel_bias, "A", g)
    btB, mtB = attn_tables(rel_bias, "B", g)
    lbz = inputs["hg_lb_logits"][:, 128 * g:128 * (g + 1)]
    one = np.ones((128, 128), np.float32)
    m = {
        "xT": xT_b, "xmine0": np.ascontiguousarray(xT_b[512 * g:512 * (g + 1)]),
        "btA": btA, "mtA": mtA, "btB": btB, "mtB": mtB,
        "b31": np.full((128, 1), rel_bias[31, 8 + g], np.float32),
        "triu": np.triu(one), "trius": np.triu(one, 1), "trils": np.tril(one, -1), "ident": np.eye(128, dtype=np.float32),
        "lbz_col": np.ascontiguousarray(lbz.T),
        "lbz_row": np.ascontiguousarray(np.broadcast_to(lbz[None], (128, 2, 128))),
        "fgain4": np.ascontiguousarray(inputs["final_gain"][512 * g:512 * (g + 1)].reshape(4, 128).T),
    }
    for l in range(DEPTH):
        w = inputs["w_in"][l]
        sl = lambda name, wd=128: w[:, OFF[name] + wd * g:OFF[name] + wd * (g + 1)]
        conv = inputs["dn_conv"][l]
        convl = np.stack([conv[:, j * 512 + 128 * g:j * 512 + 128 * (g + 1)] for j in range(3)])
        wmg = w[:, OFF["merge"]:].reshape(D, 4, D)[:, :, 512 * g:512 * (g + 1)]
        m.update({
            "wfm%d" % l: np.ascontiguousarray(np.concatenate([sl(n) for n in FM_ORDER], axis=1)),
            "wtm%d" % l: np.ascontiguousarray(np.concatenate([sl(n) for n in TM_ORDER] + [sl("c_beta", 1), sl("c_a", 1)], axis=1)),
            "gain16_%d" % l: l16(inputs["norm_gain"][l]),
            "dl%d" % l: np.ascontiguousarray(np.broadcast_to(inputs["diff_lambda"][l].reshape(1, 256), (128, 256))),
            "subln%d" % l: col(inputs["diff_subln_gain"][l]),
            "conv%d" % l: np.ascontiguousarray(convl.transpose(2, 0, 1)),
            "a_dt%d" % l: np.ascontiguousarray(np.broadcast_to(
                np.array([inputs["dn_a_log"][l, g], inputs["dn_dt_bias"][l, g]], np.float32), (128, 2))),
            "dn_gain%d" % l: col(inputs["dn_norm_gain"][l]),
            "hg_gain%d" % l: col(inputs["hg_norm_gain"][l]),
            "wm_g%d" % l: np.ascontiguousarray(wmg),
            "wbr_g%d" % l: np.ascontiguousarray(inputs["w_branch"][l][:, :, 512 * g:512 * (g + 1)]),
            "wout_g%d" % l: np.ascontiguousarray(inputs["w_out"][l][:, 512 * g:512 * (g + 1)]),
        })
    return m


def kernel_impl(inputs, n_cores=8):
    inputs = {k: np.asarray(v, dtype=np.float32) for k, v in inputs.items()}
    x = inputs["x"]
    Bsz, T, _ = x.shape
    assert Bsz * 4 == n_cores
    nc = build_fused(T)
    in_maps = []
    for b in range(Bsz):
        xT_b = np.ascontiguousarray(x[b].T)
        for g in range(4):
            in_maps.append(core_inputs(inputs, xT_b, g))
    res = run_bass_kernel_spmd(nc, in_maps, core_ids=list(range(n_cores)))
    out = np.empty((Bsz, T, D), np.float32)
    for b in range(Bsz):
        for g in range(4):
            out[b, :, 512 * g:512 * (g + 1)] = np.asarray(res.results[b * 4 + g]["outT"]).T
    return out


def kernel(**inputs):
    return kernel_impl(inputs)
```

```python
import contextlib
import math
import numpy as np
import concourse.bass as bass
import concourse.mybir as mybir
from concourse.bass_utils import run_bass_kernel_spmd

F32 = mybir.dt.float32
BF16 = mybir.dt.bfloat16
AF = mybir.ActivationFunctionType
ALU = mybir.AluOpType

D = 2048
NDS = 24
UNIQ = [0]


def uq(n):
    return "%s_L%d" % (n, UNIQ[0])


class Sched:
    def __init__(self, nc, es, selfdeps=True):
        self.nc = nc
        self.selfdeps = selfdeps
        self.E = {}
        self.sem = {}
        for name, eng in (("pe", nc.tensor), ("act", nc.scalar), ("dve", nc.vector),
                          ("pool", nc.gpsimd), ("sp", nc.sync)):
            sem = es.enter_context(nc.semaphore("s_" + name))
            self.E[name] = {"eng": eng, "cnt": 0, "seen": {}}
            self.sem[name] = sem
        self.dq = {}
        for q in ("sp", "pool", "act"):
            names = []
            for i in range(NDS):
                nm = "d_%s_%d" % (q, i)
                self.sem[nm] = es.enter_context(nc.semaphore(nm))
                names.append(nm)
            self.dq[q] = {"sems": names, "n": 0}
        self.sem["cc"] = es.enter_context(nc.semaphore("s_cc"))
        self.ncc = 0
        self.lw = {}
        self.rd = {}
        self.psum_keys = set()

    def _deps(self, r, w):
        deps = {}

        def add(tok):
            if tok is None:
                return
            s, v = tok
            if deps.get(s, 0) < v:
                deps[s] = v
        for k in r:
            add(self.lw.get(k))
        for k in w:
            add(self.lw.get(k))
            for s, v in self.rd.get(k, {}).items():
                add((s, v))
        return deps

    def _wait(self, ename, deps):
        E = self.E[ename]
        for s, v in deps.items():
            if s == ename and (ename == "pe" or not self.selfdeps):
                continue
            if E["seen"].get(s, 0) >= v:
                continue
            E["eng"].wait_ge(self.sem[s], v)
            E["seen"][s] = v

    def _commit(self, tok, r, w):
        s, v = tok
        for k in w:
            self.lw[k] = tok
            self.rd[k] = {}
        for k in r:
            d = self.rd.setdefault(k, {})
            if d.get(s, 0) < v:
                d[s] = v

    def op(self, ename, fn, r=(), w=()):
        xs = [k for k in r if k in self.psum_keys and k not in w]
        if xs:
            w = list(w) + xs
        deps = self._deps(r, w)
        self._wait(ename, deps)
        E = self.E[ename]
        ins = fn(E["eng"])
        E["cnt"] += 1
        ins.then_inc(self.sem[ename], 1)
        self._commit((ename, E["cnt"]), r, w)

    def dma(self, q, out, in_, r=(), w=()):
        deps = self._deps(r, w)
        Q = self.dq[q]
        n = Q["n"]
        nm = Q["sems"][n % NDS]
        use = n // NDS
        if use > 0:
            deps[nm] = max(deps.get(nm, 0), 16 * use)
        self._wait(q, deps)
        ins = self.E[q]["eng"].dma_start(out=out, in_=in_)
        ins.then_inc(self.sem[nm], 16)
        Q["n"] += 1
        self._commit((nm, 16 * (use + 1)), r, w)

    def coll(self, src, dst, r=(), w=()):
        deps = self._deps(r, w)
        self._wait("pool", deps)
        ins = self.nc.gpsimd.collective_compute("AllGather", ALU.bypass, replica_groups=[[0, 1, 2, 3], [4, 5, 6, 7]],
                                                ins=[src], outs=[dst])
        ins.then_inc(self.sem["cc"])
        self.ncc += 1
        self._commit(("cc", self.ncc), r, w)

    def barrier(self):
        deps = {}
        for q, Q in self.dq.items():
            for i, nm in enumerate(Q["sems"]):
                if Q["n"] > i:
                    deps[nm] = 16 * ((Q["n"] - i + NDS - 1) // NDS)
        for name, E in self.E.items():
            if E["cnt"]:
                deps[name] = E["cnt"]
        for name in self.E:
            d = {k: v for k, v in deps.items() if k != name}
            E = self.E[name]
            for sname, v in d.items():
                if E["seen"].get(sname, 0) >= v:
                    continue
                E["eng"].wait_ge(self.sem[sname], v)
                E["seen"][sname] = v

    def finish(self):
        deps = {}
        for q, Q in self.dq.items():
            for i, nm in enumerate(Q["sems"]):
                if Q["n"] > i:
                    deps[nm] = 16 * ((Q["n"] - i + NDS - 1) // NDS)
        for name, E in self.E.items():
            if name != "sp" and E["cnt"]:
                deps[name] = E["cnt"]
        if self.ncc:
            deps["cc"] = self.ncc
        self._wait("sp", deps)


FM_AQ, FM_AK, FM_BQ, FM_BK = 0, 1, 2, 3
G_AG, G_BG, G_CQ, G_CK, G_CV, G_CZ, G_DQ, G_DF, G_DG = range(9)
NQK = 4
NFM32 = 9
NFM = NQK + NFM32
NTM = 512
SILU_GROUPS = (G_AG, G_BG, G_CZ, G_DG)


def phase_proj(nc, S, T, xtile, wfm, wtm, gain16, scr_qk, scr_fm, scr_v, scr_tm, scr_bd, scr_h, lay=0):
    TT = 512
    NT = T // TT
    with contextlib.ExitStack() as es:
        sb = lambda n, s, d: es.enter_context(nc.sbuf_tensor(uq(n), s, d))
        ps = lambda n: (S.psum_keys.add(n), es.enter_context(nc.psum_tensor(uq(n), [128, 512], F32)))[1]
        w_fm = sb("p_wfm", [128, 16, NFM * 128], BF16)
        w_tm = sb("p_wtm", [128, 16, NTM + 2], BF16)
        wst = [sb("p_wst%d" % i, [128, NFM * 128], F32) for i in range(2)]
        gain = sb("p_gain", [128, 16], F32)
        ones = sb("p_ones", [128, 128], BF16)
        xs = [sb("p_xs0", [128, 16, TT], F32)] * 2
        sq = sb("p_sq", [128, 16, TT], BF16)
        rstd = sb("p_rstd", [128, TT], F32)
        hT = sb("p_hT", [128, 16, TT], BF16)
        o_qk = [sb("p_oqk%d" % i, [128, NQK, TT], BF16) for i in range(2)]
        o_fm = [sb("p_ofm%d" % i, [128, NFM32, TT], F32) for i in range(2)]
        o_tm = [sb("p_otm%d" % i, [128, 4, 256], F32) for i in range(2)]
        o_v = [sb("p_ov%d" % i, [128, 4, 256], BF16) for i in range(2)]
        o_bd = [sb("p_obd%d" % i, [128, 4, 2], F32) for i in range(2)]
        p_ssq = ps("p_pssq")
        p_mm = [ps("p_pmm%d" % i) for i in range(4)]
        p_bd = ps("p_pbd")

        S.op("dve", lambda e: e.memset(ones[:], 1.0 / D), w=["p_ones"])
        S.dma("sp", gain[:], gain16, w=["p_gain"])
        wfm_v = wfm.rearrange("(c p) n -> c p n", p=128)
        wtm_v = wtm.rearrange("(c p) n -> c p n", p=128)
        for c in range(16):
            st = wst[c % 2]
            k = "p_wst%d" % (c % 2)
            S.dma("sp", st[:, :], wfm_v[c], w=[k])
            S.op("act" if c % 2 else "dve",
                 (lambda e, st=st, c=c: e.activation(out=w_fm[:, c, :], in_=st[:, :], func=AF.Copy))
                 if c % 2 else
                 (lambda e, st=st, c=c: e.tensor_copy(out=w_fm[:, c, :], in_=st[:, :])),
                 r=[k], w=["p_wfm"])
        for c in range(16):
            st = wst[c % 2]
            k = "p_wst%d" % (c % 2)
            S.dma("sp", st[:, 0:NTM + 2], wtm_v[c], w=[k])
            S.op("dve", lambda e, st=st, c=c: e.tensor_copy(out=w_tm[:, c, :], in_=st[:, 0:NTM + 2]),
                 r=[k], w=["p_wtm"])

        mmi = 0
        for t in range(NT):
            sl = t % 2
            t0 = t * TT
            kx = "p_xs0"
            for c4 in range(4):
                S.dma("sp", xs[sl][:, 4 * c4:4 * c4 + 4, :], xtile(t, c4), r=[("xdst", lay - 1, t)], w=[kx])
            S.op("act", lambda e: e.activation(out=sq[:], in_=xs[sl][:], func=AF.Square),
                 r=[kx], w=["p_sq"])
            for c in range(16):
                S.op("pe", lambda e, c=c: e.matmul(p_ssq[:], ones[:], sq[:, c, :], start=(c == 0), stop=(c == 15)),
                     r=["p_sq", "p_ones"], w=["p_pssq"])
            S.op("act", lambda e: e.activation(out=rstd[:], in_=p_ssq[:], func=AF.Sqrt, bias=1e-6, scale=1.0),
                 r=["p_pssq"], w=["p_rstd"])
            S.op("dve", lambda e: e.reciprocal(out=rstd[:], in_=rstd[:]), r=["p_rstd"], w=["p_rstd"])
            for c in range(16):
                S.op("dve", lambda e, c=c: e.scalar_tensor_tensor(
                    out=hT[:, c, :], in0=xs[sl][:, c, :], scalar=gain[:, c:c + 1], in1=rstd[:],
                    op0=ALU.mult, op1=ALU.mult),
                    r=[kx, "p_gain", "p_rstd"], w=["p_hT"])
            S.dma("pool", scr_h[t], hT[:], r=["p_hT"], w=["scr_h"])
            for g in range(NFM):
                pm = p_mm[mmi % 4]
                kp = "p_pmm%d" % (mmi % 4)
                mmi += 1
                for c in range(16):
                    S.op("pe", lambda e, c=c, g=g, pm=pm: e.matmul(
                        pm[:], w_fm[:, c, g * 128:(g + 1) * 128], hT[:, c, :], start=(c == 0), stop=(c == 15)),
                        r=["p_wfm", "p_hT"], w=[kp])
                if g < NQK:
                    S.op("dve", lambda e, g=g, pm=pm: e.tensor_copy(out=o_qk[sl][:, g, :], in_=pm[:]),
                         r=[kp], w=["p_oqk%d" % sl])
                else:
                    gg = g - NQK
                    fn = AF.Silu if gg in SILU_GROUPS else AF.Copy
                    S.op("act", lambda e, gg=gg, pm=pm, fn=fn: e.activation(out=o_fm[sl][:, gg, :], in_=pm[:], func=fn),
                         r=[kp], w=["p_ofm%d" % sl])
            S.dma("pool", scr_qk.rearrange("g p t -> p g t")[:, :, t0:t0 + TT], o_qk[sl][:],
                  r=["p_oqk%d" % sl], w=["scr_qk"])
            S.dma("pool", scr_fm.rearrange("g p t -> p g t")[:, :, t0:t0 + TT], o_fm[sl][:],
                  r=["p_ofm%d" % sl], w=["scr_fm"])
            for s4 in range(4):
                pm = p_mm[mmi % 4]
                kp = "p_pmm%d" % (mmi % 4)
                mmi += 1
                for c in range(16):
                    S.op("pe", lambda e, c=c, pm=pm, s4=s4: e.matmul(
                        pm[:], hT[:, c, s4 * 128:(s4 + 1) * 128], w_tm[:, c, 0:NTM], start=(c == 0), stop=(c == 15)),
                        r=["p_wtm", "p_hT"], w=[kp])
                S.op("dve", lambda e, pm=pm, s4=s4: e.tensor_copy(out=o_v[sl][:, s4, :], in_=pm[:, 0:256]),
                     r=[kp], w=["p_ov%d" % sl])
                S.op("act", lambda e, pm=pm, s4=s4: e.activation(out=o_tm[sl][:, s4, :], in_=pm[:, 256:512], func=AF.Copy),
                     r=[kp], w=["p_otm%d" % sl])
                for c in range(16):
                    S.op("pe", lambda e, c=c, s4=s4: e.matmul(
                        p_bd[:, 0:2], hT[:, c, s4 * 128:(s4 + 1) * 128], w_tm[:, c, NTM:NTM + 2],
                        start=(c == 0), stop=(c == 15)),
                        r=["p_wtm", "p_hT"], w=["p_pbd"])
                S.op("dve", lambda e, s4=s4: e.tensor_copy(out=o_bd[sl][:, s4, :], in_=p_bd[:, 0:2]),
                     r=["p_pbd"], w=["p_obd%d" % sl])
            S.dma("pool", scr_tm[t0:t0 + TT, :].rearrange("(s p) c -> p s c", p=128), o_tm[sl][:],
                  r=["p_otm%d" % sl], w=["scr_tm"])
            S.dma("pool", scr_v[t0:t0 + TT, :].rearrange("(s p) c -> p s c", p=128), o_v[sl][:],
                  r=["p_ov%d" % sl], w=["scr_v"])
            S.dma("pool", scr_bd[t0:t0 + TT, :].rearrange("(s p) c -> p s c", p=128), o_bd[sl][:],
                  r=["p_obd%d" % sl], w=["scr_bd"])
        S.barrier()


def phase_attn(nc, S, T, kind, scr_qk, scr_fm, scr_v, btab, mtab, aux, yT):
    isA = kind == "A"
    NB = T // 128
    NG = T // 512
    qg, kg = (FM_AQ, FM_AK) if isA else (FM_BQ, FM_BK)
    vcol = 0 if isA else 128
    dv = 64 if isA else 128
    ntab = 2 if isA else 1
    W = btab.shape[2]
    dmax = 16 if isA else 12
    pf = "a%s_" % kind
    with contextlib.ExitStack() as es:
        sb = lambda n, s, d: es.enter_context(nc.sbuf_tensor(uq(pf + n), s, d))
        ps = lambda n, m=128: (S.psum_keys.add(pf + n), es.enter_context(nc.psum_tensor(uq(pf + n), [m, 512], F32)))[1]
        K = lambda n: pf + n
        qT = sb("qT", [128, T], BF16)
        kT = sb("kT", [128, T], BF16)
        v = sb("v", [128, NB, 128], BF16)
        wgt = sb("wgt", [128, ntab, W], F32)
        mst = sb("mst", [128, W], F32)
        ones = sb("ones", [128, 128], BF16)
        e32 = [sb("e32_%d" % i, [128, 512], F32) for i in range(4)]
        pT = [sb("pT%d" % i, [128, 512], BF16) for i in range(8)]
        rr = sb("rr", [128, 512], F32)
        dacc = [sb("dacc%d" % i, [128, 512], F32) for i in range(2)]
        dhi = sb("dhi", [128, 512], BF16)
        dlo = sb("dlo", [128, 512], BF16)
        o1 = sb("o1", [128, 512], F32)
        o2 = sb("o2", [128, 512], F32)
        gt = [sb("gt%d" % i, [128, 512], F32) for i in range(2)]
        yb = [sb("yb%d" % i, [128, 512], BF16) for i in range(2)]
        gtA = [sb("gtA%d" % i, [64, 2, 512], F32) for i in range(2)] if isA else None
        ybA = [sb("ybA%d" % i, [64, 2, 512], BF16) for i in range(2)] if isA else None
        p_s = [ps("ps%d" % i) for i in range(4)]
        p_o = [ps("po%d" % i, dv) for i in range(2)]
        p_d = [ps("pd%d" % i, dv) for i in range(2)]
        S.op("dve", lambda e: e.memset(ones[:], 1.0), w=[K("ones")])
        for c in range(0, T, 4096):
            c1 = min(T, c + 4096)
            S.dma("sp", qT[:, c:c1], scr_qk[qg][:, c:c1], r=["scr_qk"], w=[K("qT")])
            S.dma("sp", kT[:, c:c1], scr_qk[kg][:, c:c1], r=["scr_qk"], w=[K("kT")])
        for n0 in range(0, NB, 8):
            S.dma("sp", v[:, n0:n0 + 8, :],
                  scr_v[n0 * 128:(n0 + 8) * 128, vcol:vcol + 128].rearrange("(n p) c -> p n c", p=128),
                  r=["scr_v"], w=[K("v")])
        for i in range(ntab):
            S.dma("sp", wgt[:, i, :], btab[i], w=[K("wgt")])
            S.dma("sp", mst[:], mtab[i], w=[K("mst")])
            S.op("act", lambda e, i=i: e.activation(out=wgt[:, i, :], in_=wgt[:, i, :], func=AF.Exp),
                 r=[K("wgt")], w=[K("wgt")])
            S.op("dve", lambda e, i=i: e.tensor_tensor(out=wgt[:, i, :], in0=wgt[:, i, :], in1=mst[:], op=ALU.mult),
                 r=[K("wgt"), K("mst")], w=[K("wgt")])
        if not isA:
            b31 = sb("b31", [128, 1], F32)
            dl = sb("dl", [128, 256], F32)
            tmp = sb("tmp", [128, 128], F32)
            ss = sb("ss", [128, 2], F32)
            neglam = sb("neglam", [128, 1], F32)
            gcol = sb("gcol", [128, 1], F32)
            ones_n = sb("ones_n", [128, 128], BF16)
            sqb = sb("sqb", [128, 512], BF16)
            rstd = sb("rstd", [128, 512], F32)
            p_q = p_s[3]
            lam_init = aux["lam_init"]
            S.op("dve", lambda e: e.memset(ones_n[:], 1.0 / 128), w=[K("ones_n")])
            S.dma("sp", b31[:], aux["b31"], w=[K("b31")])
            S.dma("sp", dl[:], aux["dl"], w=[K("dl")])
            S.dma("sp", gcol[:], aux["subln"], w=[K("gcol")])
            S.op("dve", lambda e: e.tensor_tensor(out=tmp[:, 0:64], in0=dl[:, 0:64], in1=dl[:, 64:128], op=ALU.mult),
                 r=[K("dl")], w=[K("tmp")])
            S.op("dve", lambda e: e.tensor_tensor(out=tmp[:, 64:128], in0=dl[:, 128:192], in1=dl[:, 192:256], op=ALU.mult),
                 r=[K("dl")], w=[K("tmp")])
            S.op("dve", lambda e: e.reduce_sum(out=ss[:, 0:1], in_=tmp[:, 0:64], axis=mybir.AxisListType.X),
                 r=[K("tmp")], w=[K("ss")])
            S.op("dve", lambda e: e.reduce_sum(out=ss[:, 1:2], in_=tmp[:, 64:128], axis=mybir.AxisListType.X),
                 r=[K("tmp")], w=[K("ss")])
            S.op("act", lambda e: e.activation(out=ss[:], in_=ss[:], func=AF.Exp), r=[K("ss")], w=[K("ss")])
            S.op("dve", lambda e: e.tensor_tensor(out=neglam[:], in0=ss[:, 1:2], in1=ss[:, 0:1], op=ALU.subtract),
                 r=[K("ss")], w=[K("neglam")])
            S.op("dve", lambda e: e.tensor_scalar(out=neglam[:], in0=neglam[:], scalar1=-lam_init, scalar2=None, op0=ALU.add),
                 r=[K("neglam")], w=[K("neglam")])
            S.op("dve", lambda e: e.tensor_scalar(out=gcol[:], in0=gcol[:], scalar1=1.0 - lam_init, scalar2=None, op0=ALU.mult),
                 r=[K("gcol")], w=[K("gcol")])

        step = 0
        for G in range(NG):
            q0 = G * 512
            gs = G % 2
            ggrp = G_AG if isA else G_BG
            if isA:
                for h in range(2):
                    S.dma("sp", gtA[gs][:, h, :],
                          scr_fm[ggrp][64 * h:64 * h + 64, q0:q0 + 512], r=["scr_fm"], w=[K("gt%d" % gs)])
            else:
                S.dma("sp", gt[gs][:], scr_fm[ggrp][:, q0:q0 + 512], r=["scr_fm"], w=[K("gt%d" % gs)])
            kb_lo = max(0, 4 * G - 16) if isA else 0
            kb_hi = 4 * G + 3
            steps = [(kb, sub) for kb in range(kb_lo, kb_hi + 1) for sub in range(2)]

            def emit_qk(i):
                kb, sub = steps[i]
                st_ = step0 + i
                psl, ptl = st_ % 4, st_ % 8
                delta0 = 4 * G - kb
                near = delta0 <= dmax
                j0 = 128 * delta0 + 384
                S.op("pe", lambda e: e.matmul(p_s[psl][:], kT[64 * sub:64 * sub + 64, kb * 128:(kb + 1) * 128],
                                              qT[64 * sub:64 * sub + 64, q0:q0 + 512], start=True, stop=True),
                     r=[K("kT"), K("qT")], w=[K("ps%d" % psl)])
                if near:
                    S.op("act", lambda e: e.activation(out=e32[psl][:], in_=p_s[psl][:], func=AF.Exp, scale=0.125),
                         r=[K("ps%d" % psl)], w=[K("e32_%d" % psl)])
                    ti = sub if isA else 0
                    S.op("dve" if (st_ % 2) else "pool",
                         lambda e: e.tensor_tensor(out=pT[ptl][:], in0=e32[psl][:], in1=wgt[:, ti, j0:j0 + 512], op=ALU.mult),
                         r=[K("e32_%d" % psl), K("wgt")], w=[K("pT%d" % ptl)])
                else:
                    S.op("act", lambda e: e.activation(out=pT[ptl][:], in_=p_s[psl][:], func=AF.Exp,
                                                       bias=b31[:, 0:1], scale=0.125),
                         r=[K("ps%d" % psl), K("b31")], w=[K("pT%d" % ptl)])

            def emit_pv(i):
                kb, sub = steps[i]
                ptl = (step0 + i) % 8
                first = kb == kb_lo
                last = kb == kb_hi
                vs = v[:, kb, 64 * sub:64 * sub + 64] if isA else v[:, kb, :]
                S.op("pe", lambda e: e.matmul(p_o[sub][:], vs, pT[ptl][:], start=first, stop=last),
                     r=[K("v"), K("pT%d" % ptl)], w=[K("po%d" % sub)])
                if True:
                    S.op("pe", lambda e: e.matmul(p_d[sub][:], ones[:, 0:dv], pT[ptl][:], start=first, stop=last),
                         r=[K("ones"), K("pT%d" % ptl)], w=[K("pd%d" % sub)])
                    return
                en = "pool" if ((step0 + i) % 2) else "dve"
                if first:
                    S.op(en, lambda e: e.tensor_copy(out=dacc[sub][:], in_=pT[ptl][:]),
                         r=[K("pT%d" % ptl)], w=[K("dacc%d" % sub)])
                else:
                    S.op(en, lambda e: e.tensor_tensor(out=dacc[sub][:], in0=dacc[sub][:], in1=pT[ptl][:], op=ALU.add),
                         r=[K("pT%d" % ptl), K("dacc%d" % sub)], w=[K("dacc%d" % sub)])

            step0 = step
            nbu = len(steps) // 4
            for j in (0, 2, 1, 3):
                emit_qk(j)
            for bi in range(nbu):
                nx = bi + 1 < nbu
                if nx:
                    emit_qk(4 * bi + 4)
                    emit_qk(4 * bi + 6)
                emit_pv(4 * bi + 0)
                emit_pv(4 * bi + 2)
                if nx:
                    emit_qk(4 * bi + 5)
                    emit_qk(4 * bi + 7)
                emit_pv(4 * bi + 1)
                emit_pv(4 * bi + 3)
            step += len(steps)
            for sub in []:
                S.op("act", lambda e: e.activation(out=dhi[:], in_=dacc[sub][:], func=AF.Copy), r=[K("dacc%d" % sub)], w=[K("dhi")])
                S.op("dve", lambda e: e.tensor_tensor(out=dlo[:], in0=dacc[sub][:], in1=dhi[:], op=ALU.subtract),
                     r=[K("dacc%d" % sub), K("dhi")], w=[K("dlo")])
                S.op("pe", lambda e: e.matmul(p_d[sub][:], ones[:, 0:dv], dhi[:], start=True, stop=False),
                     r=[K("ones"), K("dhi")], w=[K("pd%d" % sub)])
                S.op("pe", lambda e: e.matmul(p_d[sub][:], ones[:, 0:dv], dlo[:], start=False, stop=True),
                     r=[K("ones"), K("dlo")], w=[K("pd%d" % sub)])
            if isA:
                for h in range(2):
                    S.op("dve", lambda e: e.reciprocal(out=rr[0:64, :], in_=p_d[h][:]), r=[K("pd%d" % h)], w=[K("rr")])
                    S.op("dve", lambda e: e.tensor_tensor(out=o1[0:64, :], in0=p_o[h][:], in1=rr[0:64, :], op=ALU.mult),
                         r=[K("po%d" % h), K("rr")], w=[K("o1")])
                    S.op("dve", lambda e: e.tensor_tensor(out=ybA[gs][:, h, :], in0=o1[0:64, :], in1=gtA[gs][:, h, :], op=ALU.mult),
                         r=[K("o1"), K("gt%d" % gs)], w=[K("yb%d" % gs)])
                S.dma("pool", yT[0:128, q0:q0 + 512].rearrange("(h p) t -> p h t", p=64), ybA[gs][:],
                      r=[K("yb%d" % gs)], w=[("ysrc", q0 // 1024)])
            else:
                S.op("dve", lambda e: e.reciprocal(out=rr[:], in_=p_d[0][:]), r=[K("pd0")], w=[K("rr")])
                S.op("dve", lambda e: e.tensor_tensor(out=o1[:], in0=p_o[0][:], in1=rr[:], op=ALU.mult),
                     r=[K("po0"), K("rr")], w=[K("o1")])
                S.op("dve", lambda e: e.reciprocal(out=rr[:], in_=p_d[1][:]), r=[K("pd1")], w=[K("rr")])
                S.op("dve", lambda e: e.tensor_tensor(out=o2[:], in0=p_o[1][:], in1=rr[:], op=ALU.mult),
                     r=[K("po1"), K("rr")], w=[K("o2")])
                S.op("dve", lambda e: e.scalar_tensor_tensor(out=o1[:], in0=o2[:], scalar=neglam[:, 0:1], in1=o1[:],
                                                             op0=ALU.mult, op1=ALU.add),
                     r=[K("o1"), K("o2"), K("neglam")], w=[K("o1")])
                S.op("act", lambda e: e.activation(out=sqb[:], in_=o1[:], func=AF.Square), r=[K("o1")], w=[K("sqb")])
                S.op("pe", lambda e: e.matmul(p_q[:], ones_n[:], sqb[:], start=True, stop=True),
                     r=[K("ones_n"), K("sqb")], w=[K("ps3")])
                S.op("act", lambda e: e.activation(out=rstd[:], in_=p_q[:], func=AF.Sqrt, bias=1e-6, scale=1.0),
                     r=[K("ps3")], w=[K("rstd")])
                S.op("dve", lambda e: e.reciprocal(out=rstd[:], in_=rstd[:]), r=[K("rstd")], w=[K("rstd")])
                S.op("dve", lambda e: e.tensor_tensor(out=o1[:], in0=o1[:], in1=rstd[:], op=ALU.mult),
                     r=[K("o1"), K("rstd")], w=[K("o1")])
                S.op("dve", lambda e: e.scalar_tensor_tensor(out=yb[gs][:], in0=o1[:], scalar=gcol[:, 0:1], in1=gt[gs][:],
                                                             op0=ALU.mult, op1=ALU.mult),
                     r=[K("o1"), K("gcol"), K("gt%d" % gs)], w=[K("yb%d" % gs)])
                S.dma("pool", yT[128:256, q0:q0 + 512], yb[gs][:], r=[K("yb%d" % gs)], w=[("ysrc", q0 // 1024)])
        S.barrier()


def t5_bucket_np(dist):
    n = np.maximum(dist, 0)
    max_exact = 16
    nf = np.maximum(n, max_exact).astype(np.float32)
    large = max_exact + (np.log(nf / max_exact) / math.log(2048 / max_exact) * (32 - max_exact)).astype(np.int32)
    large = np.minimum(large, 31)
    return np.where(n < max_exact, n, large)


def attn_tables(rel_bias, kind, g):
    W = 2944 if kind == "A" else 2432
    ki = np.arange(128)[:, None]
    d = (np.arange(W)[None, :] - 384) - ki
    bk = t5_bucket_np(d)
    if kind == "A":
        mult = ((d >= 0) & (d <= 128)).astype(np.float32) \
            + ((d >= 0) & (d <= 512) & (d % 4 == 0)).astype(np.float32) \
            + ((d >= 0) & (d <= 2048) & (d % 16 == 0)).astype(np.float32)
        heads = [2 * g, 2 * g + 1]
    else:
        mult = (d >= 0).astype(np.float32)
        heads = [8 + g]
    bt = np.stack([np.where(mult > 0, rel_bias[:, h][bk], np.float32(0)) for h in heads]).astype(np.float32)
    mt = np.stack([mult for _ in heads]).astype(np.float32)
    return np.ascontiguousarray(bt), np.ascontiguousarray(mt)


def lb_compute(S, K, layer, z, e, w1, lb, oml, negoml, sl):
    S.op("act", lambda en: en.activation(out=e[:], in_=z[:], func=AF.Exp), r=[K("lbz")], w=[K("lbe")])
    S.op("dve", lambda en: en.tensor_tensor(out=w1[:], in0=e[:, 0, :], in1=e[:, 1, :], op=ALU.add), r=[K("lbe")], w=[K("lbw")])
    S.op("dve", lambda en: en.reciprocal(out=w1[:], in_=w1[:]), r=[K("lbw")], w=[K("lbw")])
    S.op("dve", lambda en: en.tensor_tensor(out=e[:, 0, :], in0=e[:, 0, :], in1=w1[:], op=ALU.mult), r=[K("lbe"), K("lbw")], w=[K("lbe")])
    S.op("dve", lambda en: en.tensor_tensor(out=e[:, 1, :], in0=e[:, 1, :], in1=w1[:], op=ALU.mult), r=[K("lbe"), K("lbw")], w=[K("lbe")])
    if layer == 0:
        S.op("dve", lambda en: en.tensor_tensor(out=lb[:], in0=e[:, 0, :], in1=e[:, 0, :], op=ALU.subtract), r=[K("lbe")], w=[K("lb")])
    else:
        S.op("dve", lambda en: en.tensor_tensor(out=lb[:], in0=e[:, 0, :], in1=e[:, 1, :], op=ALU.add), r=[K("lbe")], w=[K("lb")])
        S.op("dve", lambda en: en.tensor_tensor(out=lb[:], in0=lb[:], in1=e[:, 0, :], op=ALU.subtract), r=[K("lbe"), K("lb")], w=[K("lb")])
    S.op("dve", lambda en: en.tensor_scalar(out=lb[:], in0=lb[:], scalar1=0.0, scalar2=1.0, op0=ALU.max, op1=ALU.min),
         r=[K("lb")], w=[K("lb")])
    S.op("dve", lambda en: en.tensor_scalar(out=oml[:], in0=lb[:], scalar1=-1.0, scalar2=1.0, op0=ALU.mult, op1=ALU.add),
         r=[K("lb")], w=[K("oml")])
    S.op("dve", lambda en: en.tensor_scalar(out=negoml[:], in0=lb[:], scalar1=1.0, scalar2=-1.0, op0=ALU.mult, op1=ALU.add),
         r=[K("lb")], w=[K("oml")])


def phase_hgrn(nc, S, T, layer, scr_fm, scr_tm, consts, aux, yT, on_tile=None):
    NB = T // 128
    NG = T // 512
    pf = "d_"
    with contextlib.ExitStack() as es:
        sb = lambda n, s, d: es.enter_context(nc.sbuf_tensor(uq(pf + n), s, d))
        ps = lambda n: (S.psum_keys.add(pf + n), es.enter_context(nc.psum_tensor(uq(pf + n), [128, 512], F32)))[1]
        K = lambda n: pf + n
        triu = sb("triu", [128, 128], F32)
        ones32 = sb("ones32", [128, 128], F32)
        ones_n = sb("ones_n", [128, 128], BF16)
        zc = sb("zc", [128, 2, 1], F32); ec = sb("ec", [128, 2, 1], F32); wc = sb("wc", [128, 1], F32)
        lbc = sb("lbc", [128, 1], F32); omlc = sb("omlc", [128, 1], F32); nomlc = sb("nomlc", [128, 1], F32)
        zr = sb("zr", [128, 2, 128], F32); er = sb("er", [128, 2, 128], F32); wr = sb("wr", [128, 128], F32)
        lbr = sb("lbr", [128, 128], F32); omlr = sb("omlr", [128, 128], F32); nomlr = sb("nomlr", [128, 128], F32)
        gcol = sb("gcol", [128, 1], F32)
        tm = [sb("tm%d" % i, [128, 4, 256], F32) for i in range(2)]
        fT = [sb("fT%d" % i, [128, 512], F32) for i in range(2)]
        qT = [sb("qT%d" % i, [128, 512], F32) for i in range(2)]
        gt = [sb("gt%d" % i, [128, 512], F32) for i in range(2)]
        sg = sb("sg", [128, 4, 128], F32)
        logf = sb("logf", [128, 4, 128], F32)
        ktm = sb("ktm", [128, 4, 128], F32)
        vbf = sb("vbf", [128, 4, 128], BF16)
        kTt = sb("kTt", [128, 512], F32)
        bct = sb("bct", [128, 128], F32)
        bcs = sb("bcs", [128, 128], F32)
        aq = sb("aq", [128, 128], F32)
        eq1 = sb("eq1", [128, 128], F32)
        qtl = sb("qtl", [128, 128], BF16)
        q1 = sb("q1", [128, 128], BF16)
        ak = sb("ak", [128, 4, 128], F32)
        ktl = sb("ktl", [128, 4, 128], BF16)
        atb = sb("atb", [128, 128], BF16)
        dlt = sb("dlt", [128, 128], F32)
        kd = sb("kd", [128, 128], BF16)
        s32 = sb("s32", [128, 128], F32)
        sbf = sb("sbf", [128, 128], BF16)
        osb = sb("osb", [128, 512], F32)
        sqb = sb("sqb", [128, 512], BF16)
        rstd = sb("rstd", [128, 512], F32)
        yb = [sb("yb%d" % i, [128, 512], BF16) for i in range(2)]
        p_bc = ps("pbc"); p_bct = ps("pbct"); p_bcl = ps("pbcl"); p_a = ps("pa"); p_o = ps("po"); p_s = ps("pS"); p_q = ps("pq")

        S.dma("sp", triu[:], consts["triu"], w=[K("triu")])
        S.op("dve", lambda e: e.memset(ones32[:], 1.0), w=[K("ones32")])
        S.op("dve", lambda e: e.memset(ones_n[:], 1.0 / 128), w=[K("ones_n")])
        S.op("dve", lambda e: e.memset(s32[:], 0.0), w=[K("s32")])
        S.op("dve", lambda e: e.memset(sbf[:], 0.0), w=[K("sbf")])
        S.dma("sp", gcol[:], aux["hg_gain"], w=[K("gcol")])
        S.dma("sp", zc[:, :, 0], aux["lbz_col"], w=[K("c_lbz")])
        S.dma("sp", zr[:], aux["lbz_row"], w=[K("r_lbz")])
        lb_compute(S, lambda n: K("c_" + n), layer, zc, ec, wc, lbc, omlc, nomlc, None)
        lb_compute(S, lambda n: K("r_" + n), layer, zr, er, wr, lbr, omlr, nomlr, None)
        KC = lambda n: K("c_" + n)
        KR = lambda n: K("r_" + n)

        for G in range(NG):
            gs = G % 2
            q0 = G * 512
            S.dma("sp", tm[gs][:], scr_tm[q0:q0 + 512, :].rearrange("(s p) c -> p s c", p=128), r=["scr_tm"], w=[K("tm%d" % gs)])
            S.dma("sp", fT[gs][:], scr_fm[G_DF][:, q0:q0 + 512], r=["scr_fm"], w=[K("fT%d" % gs)])
            S.dma("sp", qT[gs][:], scr_fm[G_DQ][:, q0:q0 + 512], r=["scr_fm"], w=[K("qT%d" % gs)])
            S.dma("sp", gt[gs][:], scr_fm[G_DG][:, q0:q0 + 512], r=["scr_fm"], w=[K("gt%d" % gs)])
            S.op("act", lambda e: e.activation(out=sg[:], in_=tm[gs][:, :, 0:128], func=AF.Sigmoid), r=[K("tm%d" % gs)], w=[K("sg")])
            for s4 in range(4):
                S.op("dve", lambda e: e.tensor_tensor(out=sg[:, s4, :], in0=sg[:, s4, :], in1=omlr[:], op=ALU.mult),
                     r=[K("sg"), KR("oml")], w=[K("sg")])
                S.op("dve", lambda e: e.tensor_tensor(out=ktm[:, s4, :], in0=omlr[:], in1=sg[:, s4, :], op=ALU.subtract),
                     r=[K("sg"), KR("oml")], w=[K("ktm")])
                S.op("dve", lambda e: e.tensor_tensor(out=sg[:, s4, :], in0=sg[:, s4, :], in1=lbr[:], op=ALU.add),
                     r=[K("sg"), KR("lb")], w=[K("sg")])
            S.op("act", lambda e: e.activation(out=logf[:], in_=sg[:], func=AF.Ln), r=[K("sg")], w=[K("logf")])
            S.op("pool", lambda e: e.tensor_copy(out=vbf[:], in_=tm[gs][:, :, 128:256]), r=[K("tm%d" % gs)], w=[K("vbf")])
            S.op("act", lambda e: e.activation(out=kTt[:], in_=fT[gs][:], func=AF.Sigmoid), r=[K("fT%d" % gs)], w=[K("kTt")])
            S.op("dve", lambda e: e.tensor_scalar(out=kTt[:], in0=kTt[:], scalar1=nomlc[:, 0:1], scalar2=omlc[:, 0:1],
                                                   op0=ALU.mult, op1=ALU.add),
                 r=[K("kTt"), KC("oml")], w=[K("kTt")])
            for s4 in range(4):
                c0 = s4 * 128
                lf = logf[:, s4, :]
                S.op("pe", lambda e: e.matmul(p_bc[:, 0:128], triu[:], lf, start=True, stop=True), r=[K("triu"), K("logf")], w=[K("pbc")])
                S.op("pe", lambda e: e.matmul(p_bct[:, 0:128], lf, triu[:], start=True, stop=True), r=[K("triu"), K("logf")], w=[K("pbct")])
                S.op("pe", lambda e: e.matmul(p_bcl[:, 0:128], ones32[:], lf, start=True, stop=True), r=[K("ones32"), K("logf")], w=[K("pbcl")])
                S.op("dve", lambda e: e.tensor_copy(out=bct[:], in_=p_bct[:, 0:128]), r=[K("pbct")], w=[K("bct")])
                S.op("act", lambda e: e.activation(out=bcs[:], in_=p_bc[:, 0:128], func=AF.Copy), r=[K("pbc")], w=[K("bcs")])
                S.op("act", lambda e: e.activation(out=eq1[:], in_=bct[:], func=AF.Exp), r=[K("bct")], w=[K("eq1")])
                for I in range(4):
                    if I == 0:
                        S.op("dve", lambda e: e.tensor_copy(out=aq[:, 0:32], in_=bct[:, 0:32]), r=[K("bct")], w=[K("aq")])
                        S.op("dve", lambda e: e.tensor_scalar(out=ak[:, 0, :], in0=bct[:], scalar1=-60.0, scalar2=None, op0=ALU.max),
                             r=[K("bct")], w=[K("ak")])
                    else:
                        rc = bct[:, 32 * I - 1:32 * I]
                        S.op("dve", lambda e: e.tensor_scalar(out=aq[:, 32 * I:32 * I + 32], in0=bct[:, 32 * I:32 * I + 32],
                                                               scalar1=rc, scalar2=None, op0=ALU.subtract),
                             r=[K("bct")], w=[K("aq")])
                        S.op("dve", lambda e: e.tensor_scalar(out=ak[:, I, :], in0=bct[:], scalar1=rc, scalar2=-60.0,
                                                               op0=ALU.subtract, op1=ALU.max),
                             r=[K("bct")], w=[K("ak")])
                S.op("act", lambda e: e.activation(out=aq[:], in_=aq[:], func=AF.Exp), r=[K("aq")], w=[K("aq")])
                S.op("act", lambda e: e.activation(out=ak[:], in_=ak[:], func=AF.Exp, scale=-1.0), r=[K("ak")], w=[K("ak")])
                S.op("dve", lambda e: e.tensor_tensor(out=qtl[:], in0=aq[:], in1=qT[gs][:, c0:c0 + 128], op=ALU.mult),
                     r=[K("aq"), K("qT%d" % gs)], w=[K("qtl")])
                S.op("dve", lambda e: e.tensor_tensor(out=q1[:], in0=eq1[:], in1=qT[gs][:, c0:c0 + 128], op=ALU.mult),
                     r=[K("eq1"), K("qT%d" % gs)], w=[K("q1")])
                for I in range(4):
                    S.op("pool" if I % 2 else "dve",
                         lambda e: e.tensor_tensor(out=ktl[:, I, :], in0=ak[:, I, :], in1=kTt[:, c0:c0 + 128], op=ALU.mult),
                         r=[K("ak"), K("kTt")], w=[K("ktl")])
                for I in range(4):
                    S.op("pe", lambda e: e.matmul(p_a[:, 32 * I:32 * I + 32], ktl[:, I, :], qtl[:, 32 * I:32 * I + 32],
                                                  start=True, stop=True),
                         r=[K("ktl"), K("qtl")], w=[K("pa")])
                S.op("dve", lambda e: e.tensor_tensor(out=atb[:], in0=p_a[:, 0:128], in1=triu[:], op=ALU.mult),
                     r=[K("pa"), K("triu")], w=[K("atb")])
                S.op("dve", lambda e: e.tensor_tensor(out=dlt[:], in0=p_bcl[:, 0:128], in1=bcs[:], op=ALU.subtract),
                     r=[K("pbcl"), K("bcs")], w=[K("dlt")])
                S.op("act", lambda e: e.activation(out=dlt[:], in_=dlt[:], func=AF.Exp), r=[K("dlt")], w=[K("dlt")])
                S.op("dve", lambda e: e.tensor_tensor(out=kd[:], in0=dlt[:], in1=ktm[:, s4, :], op=ALU.mult),
                     r=[K("dlt"), K("ktm")], w=[K("kd")])
                S.op("pe", lambda e: e.matmul(p_o[:, c0:c0 + 128], sbf[:], q1[:], start=True, stop=False),
                     r=[K("sbf"), K("q1")], w=[K("po")])
                S.op("pe", lambda e: e.matmul(p_o[:, c0:c0 + 128], vbf[:, s4, :], atb[:], start=False, stop=True),
                     r=[K("vbf"), K("atb")], w=[K("po")])
                S.op("pe", lambda e: e.matmul(p_s[:, 0:128], kd[:], vbf[:, s4, :], start=True, stop=True),
                     r=[K("kd"), K("vbf")], w=[K("pS")])
                S.op("dve", lambda e: e.scalar_tensor_tensor(out=s32[:], in0=s32[:], scalar=eq1[:, 127:128], in1=p_s[:, 0:128],
                                                             op0=ALU.mult, op1=ALU.add),
                     r=[K("s32"), K("eq1"), K("pS")], w=[K("s32")])
                S.op("act", lambda e: e.activation(out=sbf[:], in_=s32[:], func=AF.Copy), r=[K("s32")], w=[K("sbf")])
            S.op("dve", lambda e: e.tensor_copy(out=osb[:], in_=p_o[:]), r=[K("po")], w=[K("osb")])
            S.op("act", lambda e: e.activation(out=sqb[:], in_=osb[:], func=AF.Square), r=[K("osb")], w=[K("sqb")])
            S.op("pe", lambda e: e.matmul(p_q[:], ones_n[:], sqb[:], start=True, stop=True), r=[K("ones_n"), K("sqb")], w=[K("pq")])
            S.op("act", lambda e: e.activation(out=rstd[:], in_=p_q[:], func=AF.Sqrt, bias=1e-6, scale=1.0), r=[K("pq")], w=[K("rstd")])
            S.op("dve", lambda e: e.reciprocal(out=rstd[:], in_=rstd[:]), r=[K("rstd")], w=[K("rstd")])
            S.op("dve", lambda e: e.tensor_tensor(out=osb[:], in0=osb[:], in1=rstd[:], op=ALU.mult), r=[K("osb"), K("rstd")], w=[K("osb")])
            S.op("dve", lambda e: e.scalar_tensor_tensor(out=yb[gs][:], in0=osb[:], scalar=gcol[:, 0:1], in1=gt[gs][:],
                                                         op0=ALU.mult, op1=ALU.mult),
                 r=[K("osb"), K("gcol"), K("gt%d" % gs)], w=[K("yb%d" % gs)])
            S.dma("pool", yT[384:512, q0:q0 + 512], yb[gs][:], r=[K("yb%d" % gs)], w=[("ysrc", q0 // 1024)])
            if on_tile is not None:
                on_tile(G)
        S.barrier()


def phase_gdn(nc, S, T, scr_fm, scr_bd, consts, aux, yT):
    NB = T // 128
    NG = T // 512
    pf = "c_"
    with contextlib.ExitStack() as es:
        sb = lambda n, s, d: es.enter_context(nc.sbuf_tensor(uq(pf + n), s, d))
        ps = lambda n: (S.psum_keys.add(pf + n), es.enter_context(nc.psum_tensor(uq(pf + n), [128, 512], F32)))[1]
        K = lambda n: pf + n
        triu = sb("triu", [128, 128], F32)
        trius = sb("trius", [128, 128], F32)
        trils = sb("trils", [128, 128], F32)
        ident = sb("ident", [128, 128], F32)
        ones32 = sb("ones32", [128, 128], F32)
        ones_b = sb("ones_b", [128, 128], BF16)
        ones_n = sb("ones_n", [128, 128], BF16)
        cw = sb("cw", [128, 3, 4], F32)
        ab = sb("ab", [128, 2], F32)
        negA = sb("negA", [128, 1], F32)
        gcol = sb("gcol", [128, 1], F32)
        bd = sb("bd", [128, NB, 2], F32)
        beta = sb("beta", [128, NB], F32)
        nbeta = sb("nbeta", [128, NB], F32)
        gall = sb("gall", [128, NB], F32)
        Gc = sb("Gc", [128, NB], F32)
        bg = sb("bg", [128, NB], F32)
        xin = [sb("xin%d" % i, [128, 3, 515], F32) for i in range(2)]
        gt = [sb("gt%d" % i, [128, 512], F32) for i in range(2)]
        cv = sb("cv", [128, 3, 512], F32)
        sq = sb("sq", [128, 512], BF16)
        rs = sb("rs", [128, 512], F32)
        qTb = sb("qTb", [128, 512], BF16)
        kTb = sb("kTb", [128, 512], BF16)
        vTb = sb("vTb", [128, 512], BF16)
        identb = sb("identb", [128, 128], BF16)
        ktm = sb("ktm", [128, 128], F32)
        vtm = sb("vtm", [128, 128], F32)
        bv = sb("bv", [128, 128], F32)
        kbg = sb("kbg", [128, 128], F32)
        kdec = sb("kdec", [128, 128], BF16)
        gb = sb("gb", [128, 128], F32)
        grow = sb("grow", [128, 128], F32)
        x1 = sb("x1", [128, 128], F32)
        x2 = sb("x2", [128, 128], F32)
        gTi = sb("gTi", [128, 128], F32)
        gLs = sb("gLs", [128, 128], F32)
        eg = sb("eg", [128, 128], F32)
        Y = [sb("Y%d" % i, [128, 128], F32) for i in range(2)]
        YT = [sb("YT%d" % i, [128, 128], F32) for i in range(2)]
        XT = sb("XT", [128, 128], F32)
        yh = sb("yh", [128, 128], BF16)
        yl = sb("yl", [128, 128], BF16)
        yh32 = sb("yh32", [128, 128], F32)
        usb = sb("usb", [128, 128], F32)
        wT = sb("wT", [128, 128], BF16)
        aqk = sb("aqk", [128, 128], BF16)
        qdec = sb("qdec", [128, 128], BF16)
        cl = sb("cl", [128, 1], F32)
        vnew = sb("vnew", [128, 128], BF16)
        s32 = sb("s32", [128, 128], F32)
        sbf = sb("sbf", [128, 128], BF16)
        osb = sb("osb", [128, 512], F32)
        rstd = sb("rstd", [128, 512], F32)
        yb = [sb("yb%d" % i, [128, 512], BF16) for i in range(2)]
        p_ss = ps("pss"); p_tr = ps("ptr"); p_gk = ps("pgk"); p_yy = ps("pyy"); p_yt = ps("pyt"); p_yx = ps("pyx")
        p_o = ps("po"); p_s = ps("pS")

        S.dma("sp", triu[:], consts["triu"], w=[K("triu")])
        S.dma("sp", trius[:], consts["trius"], w=[K("trius")])
        S.dma("sp", trils[:], consts["trils"], w=[K("trils")])
        S.dma("sp", ident[:], consts["ident"], w=[K("ident")])
        S.op("dve", lambda e: e.memset(ones32[:], 1.0), w=[K("ones32")])
        S.op("dve", lambda e: e.tensor_copy(out=identb[:], in_=ident[:]), r=[K("ident")], w=[K("identb")])
        S.op("dve", lambda e: e.memset(ones_b[:], 1.0), w=[K("ones_b")])
        S.op("dve", lambda e: e.memset(ones_n[:], 1.0 / 128), w=[K("ones_n")])
        S.op("dve", lambda e: e.memset(s32[:], 0.0), w=[K("s32")])
        S.op("dve", lambda e: e.memset(sbf[:], 0.0), w=[K("sbf")])
        S.dma("sp", cw[:], aux["conv"], w=[K("cw")])
        S.dma("sp", ab[:], aux["a_dt"], w=[K("ab")])
        S.dma("sp", gcol[:], aux["dn_gain"], w=[K("gcol")])
        for n0 in range(0, NB, 16):
            n1 = min(NB, n0 + 16)
            S.dma("sp", bd[:, n0:n1, :], scr_bd[n0 * 128:n1 * 128, :].rearrange("(n p) c -> p n c", p=128),
                  r=["scr_bd"], w=[K("bd")])
        S.op("act", lambda e: e.activation(out=beta[:], in_=bd[:, :, 0], func=AF.Sigmoid), r=[K("bd")], w=[K("beta")])
        S.op("dve", lambda e: e.tensor_scalar(out=nbeta[:], in0=beta[:], scalar1=-1.0, scalar2=None, op0=ALU.mult),
             r=[K("beta")], w=[K("nbeta")])
        S.op("act", lambda e: e.activation(out=negA[:], in_=ab[:, 0:1], func=AF.Exp), r=[K("ab")], w=[K("negA")])
        S.op("dve", lambda e: e.tensor_scalar(out=negA[:], in0=negA[:], scalar1=-1.0, scalar2=None, op0=ALU.mult),
             r=[K("negA")], w=[K("negA")])
        S.op("act", lambda e: e.activation(out=gall[:], in_=bd[:, :, 1], func=AF.Exp, bias=ab[:, 1:2], scale=1.0),
             r=[K("bd"), K("ab")], w=[K("gall")])
        S.op("act", lambda e: e.activation(out=gall[:], in_=gall[:], func=AF.Ln, bias=1.0, scale=1.0),
             r=[K("gall")], w=[K("gall")])
        S.op("dve", lambda e: e.tensor_scalar(out=gall[:], in0=gall[:], scalar1=negA[:, 0:1], scalar2=None, op0=ALU.mult),
             r=[K("gall"), K("negA")], w=[K("gall")])
        S.op("pe", lambda e: e.matmul(p_ss[:, 0:NB], triu[:], gall[:], start=True, stop=True),
             r=[K("triu"), K("gall")], w=[K("pss")])
        S.op("dve", lambda e: e.tensor_copy(out=Gc[:], in_=p_ss[:, 0:NB]), r=[K("pss")], w=[K("Gc")])
        S.op("act", lambda e: e.activation(out=bg[:], in_=Gc[:], func=AF.Exp), r=[K("Gc")], w=[K("bg")])
        S.op("dve", lambda e: e.tensor_tensor(out=bg[:], in0=bg[:], in1=beta[:], op=ALU.mult), r=[K("bg"), K("beta")], w=[K("bg")])

        for G in range(NG):
            gs = G % 2
            q0 = G * 512
            kx = K("xin%d" % gs)
            for j, grp in enumerate((G_CQ, G_CK, G_CV)):
                if G == 0:
                    S.op("dve", lambda e: e.memset(xin[gs][:, j, 0:3], 0.0), w=[kx])
                    S.dma("sp", xin[gs][:, j, 3:515], scr_fm[grp][:, 0:512], r=["scr_fm"], w=[kx])
                else:
                    S.dma("sp", xin[gs][:, j, :], scr_fm[grp][:, q0 - 3:q0 + 512], r=["scr_fm"], w=[kx])
            S.dma("sp", gt[gs][:], scr_fm[G_CZ][:, q0:q0 + 512], r=["scr_fm"], w=[K("gt%d" % gs)])
            for j in range(3):
                en = "dve"
                S.op(en, lambda e: e.tensor_scalar(out=cv[:, j, :], in0=xin[gs][:, j, 3:515], scalar1=cw[:, j, 3:4],
                                                   scalar2=None, op0=ALU.mult), r=[kx, K("cw")], w=[K("cv%d" % j)])
                for tap in (2, 1, 0):
                    S.op(en, lambda e: e.scalar_tensor_tensor(out=cv[:, j, :], in0=xin[gs][:, j, tap:tap + 512],
                                                              scalar=cw[:, j, tap:tap + 1], in1=cv[:, j, :],
                                                              op0=ALU.mult, op1=ALU.add),
                         r=[kx, K("cw"), K("cv%d" % j)], w=[K("cv%d" % j)])
                S.op("act", lambda e: e.activation(out=cv[:, j, :], in_=cv[:, j, :], func=AF.Silu),
                     r=[K("cv%d" % j)], w=[K("cv%d" % j)])
            S.op("pool", lambda e: e.tensor_copy(out=vTb[:], in_=cv[:, 2, :]), r=[K("cv2")], w=[K("vTb")])
            for j in range(2):
                S.op("act", lambda e: e.activation(out=sq[:], in_=cv[:, j, :], func=AF.Square), r=[K("cv%d" % j)], w=[K("sq")])
                S.op("pe", lambda e: e.matmul(p_ss[:], ones_b[:], sq[:], start=True, stop=True), r=[K("ones_b"), K("sq")], w=[K("pss")])
                S.op("act", lambda e: e.activation(out=rs[:], in_=p_ss[:], func=AF.Sqrt, bias=1e-6, scale=1.0), r=[K("pss")], w=[K("rs")])
                S.op("dve", lambda e: e.reciprocal(out=rs[:], in_=rs[:]), r=[K("rs")], w=[K("rs")])
                if j == 0:
                    S.op("dve", lambda e: e.scalar_tensor_tensor(out=cv[:, 0, :], in0=cv[:, 0, :], scalar=128.0 ** -0.5, in1=rs[:],
                                                                 op0=ALU.mult, op1=ALU.mult),
                         r=[K("cv0"), K("rs")], w=[K("cv0")])
                    S.op("act", lambda e: e.activation(out=qTb[:], in_=cv[:, 0, :], func=AF.Copy), r=[K("cv0")], w=[K("qTb")])
                else:
                    S.op("dve", lambda e: e.tensor_tensor(out=cv[:, 1, :], in0=cv[:, 1, :], in1=rs[:], op=ALU.mult),
                         r=[K("cv1"), K("rs")], w=[K("cv1")])
                    S.op("act", lambda e: e.activation(out=kTb[:], in_=cv[:, 1, :], func=AF.Copy), r=[K("cv1")], w=[K("kTb")])
            for s4 in range(4):
                n = 4 * G + s4
                c0 = s4 * 128
                gcn = Gc[:, n:n + 1]
                S.op("pe", lambda e: e.matmul(p_tr[:, 0:128], kTb[:, c0:c0 + 128], identb[:], start=True, stop=True),
                     r=[K("kTb"), K("identb")], w=[K("ptr")])
                S.op("pe", lambda e: e.matmul(p_tr[:, 128:256], vTb[:, c0:c0 + 128], identb[:], start=True, stop=True),
                     r=[K("vTb"), K("identb")], w=[K("ptr")])
                S.op("dve", lambda e: e.tensor_copy(out=ktm[:], in_=p_tr[:, 0:128]), r=[K("ptr")], w=[K("ktm")])
                S.op("dve", lambda e: e.tensor_scalar(out=bv[:], in0=p_tr[:, 128:256], scalar1=beta[:, n:n + 1], scalar2=None, op0=ALU.mult),
                     r=[K("ptr"), K("beta")], w=[K("bv")])
                S.op("dve", lambda e: e.tensor_scalar(out=kbg[:], in0=ktm[:], scalar1=bg[:, n:n + 1], scalar2=None, op0=ALU.mult),
                     r=[K("ktm"), K("bg")], w=[K("kbg")])
                S.op("dve", lambda e: e.tensor_scalar(out=gb[:], in0=ones32[:], scalar1=gall[:, n:n + 1], scalar2=None, op0=ALU.mult),
                     r=[K("ones32"), K("gall")], w=[K("gb")])
                S.op("pe", lambda e: e.matmul(p_gk[:, 0:128], gb[:], triu[:], start=True, stop=True), r=[K("gb"), K("triu")], w=[K("pgk")])
                S.op("act", lambda e: e.activation(out=grow[:], in_=p_gk[:, 0:128], func=AF.Copy), r=[K("pgk")], w=[K("grow")])
                S.op("dve", lambda e: e.tensor_scalar(out=x1[:], in0=grow[:], scalar1=gcn, scalar2=0.0, op0=ALU.subtract, op1=ALU.min),
                     r=[K("grow"), K("Gc")], w=[K("x1")])
                S.op("dve", lambda e: e.tensor_scalar(out=x2[:], in0=grow[:], scalar1=gcn, scalar2=0.0, op0=ALU.subtract, op1=ALU.max),
                     r=[K("grow"), K("Gc")], w=[K("x2")])
                S.op("act", lambda e: e.activation(out=x1[:], in_=x1[:], func=AF.Exp), r=[K("x1")], w=[K("x1")])
                S.op("act", lambda e: e.activation(out=x2[:], in_=x2[:], func=AF.Exp, scale=-1.0), r=[K("x2")], w=[K("x2")])
                S.op("act", lambda e: e.activation(out=eg[:], in_=grow[:], func=AF.Exp), r=[K("grow")], w=[K("eg")])
                S.op("pool", lambda e: e.tensor_tensor(out=gTi[:], in0=x1[:], in1=triu[:], op=ALU.mult), r=[K("x1"), K("triu")], w=[K("gTi")])
                S.op("pool", lambda e: e.tensor_tensor(out=gLs[:], in0=x2[:], in1=trils[:], op=ALU.mult), r=[K("x2"), K("trils")], w=[K("gLs")])
                S.op("pe", lambda e: e.matmul(p_tr[:, 256:384], kTb[:, c0:c0 + 128], kTb[:, c0:c0 + 128], start=True, stop=True),
                     r=[K("kTb")], w=[K("ptr")])
                S.op("dve", lambda e: e.scalar_tensor_tensor(out=Y[0][:], in0=p_tr[:, 256:384], scalar=nbeta[:, n:n + 1], in1=gLs[:],
                                                             op0=ALU.mult, op1=ALU.mult),
                     r=[K("ptr"), K("nbeta"), K("gLs")], w=[K("Y0")])
                S.op("act", lambda e: e.activation(out=yh[:], in_=Y[0][:], func=AF.Copy), r=[K("Y0")], w=[K("yh")])
                S.op("dve", lambda e: e.tensor_copy(out=yh32[:], in_=yh[:]), r=[K("yh")], w=[K("yh32")])
                S.op("dve", lambda e: e.tensor_tensor(out=yl[:], in0=Y[0][:], in1=yh32[:], op=ALU.subtract), r=[K("Y0"), K("yh32")], w=[K("yl")])
                S.op("pe", lambda e: e.matmul(p_tr[:, 384:512], yh[:], identb[:], start=True, stop=False), r=[K("yh"), K("identb")], w=[K("ptr")])
                S.op("pe", lambda e: e.matmul(p_tr[:, 384:512], yl[:], identb[:], start=False, stop=True), r=[K("yl"), K("identb")], w=[K("ptr")])
                S.op("dve", lambda e: e.tensor_copy(out=YT[0][:], in_=p_tr[:, 384:512]), r=[K("ptr")], w=[K("YT0")])
                S.op("dve", lambda e: e.tensor_tensor(out=XT[:], in0=p_tr[:, 384:512], in1=ident[:], op=ALU.add),
                     r=[K("ptr"), K("ident")], w=[K("XT")])
                cur = 0
                for lvl in range(1, 7):
                    nxt = 1 - cur
                    S.op("pe", lambda e: e.matmul(p_yy[:, 0:128], YT[cur][:], Y[cur][:], start=True, stop=True),
                         r=[K("Y%d" % cur), K("YT%d" % cur)], w=[K("pyy")])
                    if lvl < 6:
                        S.op("pe", lambda e: e.matmul(p_yt[:, 0:128], Y[cur][:], YT[cur][:], start=True, stop=True),
                             r=[K("Y%d" % cur), K("YT%d" % cur)], w=[K("pyt")])
                    S.op("act", lambda e: e.activation(out=Y[nxt][:], in_=p_yy[:, 0:128], func=AF.Copy), r=[K("pyy")], w=[K("Y%d" % nxt)])
                    if lvl < 6:
                        S.op("dve", lambda e: e.tensor_copy(out=YT[nxt][:], in_=p_yt[:, 0:128]), r=[K("pyt")], w=[K("YT%d" % nxt)])
                    S.op("pe", lambda e: e.matmul(p_yx[:, 0:128], Y[nxt][:], XT[:], start=True, stop=True),
                         r=[K("Y%d" % nxt), K("XT")], w=[K("pyx")])
                    S.op("dve", lambda e: e.tensor_tensor(out=XT[:], in0=p_yx[:, 0:128], in1=XT[:], op=ALU.add),
                         r=[K("pyx"), K("XT")], w=[K("XT")])
                    cur = nxt
                S.op("pe", lambda e: e.matmul(p_yy[:, 0:128], XT[:], bv[:], start=True, stop=True), r=[K("XT"), K("bv")], w=[K("pyy")])
                S.op("pe", lambda e: e.matmul(p_yt[:, 0:128], kbg[:], XT[:], start=True, stop=True), r=[K("XT"), K("kbg")], w=[K("pyt")])
                S.op("pe", lambda e: e.matmul(p_gk[:, 256:384], kTb[:, c0:c0 + 128], qTb[:, c0:c0 + 128], start=True, stop=True),
                     r=[K("kTb"), K("qTb")], w=[K("pgk")])
                S.op("act", lambda e: e.activation(out=usb[:], in_=p_yy[:, 0:128], func=AF.Copy), r=[K("pyy")], w=[K("usb")])
                S.op("dve", lambda e: e.tensor_copy(out=wT[:], in_=p_yt[:, 0:128]), r=[K("pyt")], w=[K("wT")])
                S.op("dve", lambda e: e.tensor_tensor(out=aqk[:], in0=p_gk[:, 256:384], in1=gTi[:], op=ALU.mult),
                     r=[K("pgk"), K("gTi")], w=[K("aqk")])
                S.op("dve", lambda e: e.tensor_tensor(out=qdec[:], in0=cv[:, 0, c0:c0 + 128], in1=eg[:], op=ALU.mult),
                     r=[K("cv0"), K("eg")], w=[K("qdec")])
                S.op("dve", lambda e: e.tensor_tensor(out=cl[:], in0=grow[:, 127:128], in1=gcn, op=ALU.subtract),
                     r=[K("grow"), K("Gc")], w=[K("cl")])
                S.op("act", lambda e: e.activation(out=cl[:], in_=cl[:], func=AF.Exp), r=[K("cl")], w=[K("cl")])
                S.op("dve", lambda e: e.tensor_scalar(out=kdec[:], in0=ktm[:], scalar1=cl[:, 0:1], scalar2=None, op0=ALU.mult),
                     r=[K("ktm"), K("cl")], w=[K("kdec")])
                S.op("pe", lambda e: e.matmul(p_gk[:, 128:256], wT[:], sbf[:], start=True, stop=True), r=[K("wT"), K("sbf")], w=[K("pgk")])
                S.op("dve", lambda e: e.tensor_tensor(out=vnew[:], in0=usb[:], in1=p_gk[:, 128:256], op=ALU.subtract),
                     r=[K("usb"), K("pgk")], w=[K("vnew")])
                S.op("pe", lambda e: e.matmul(p_o[:, c0:c0 + 128], sbf[:], qdec[:], start=True, stop=False), r=[K("sbf"), K("qdec")], w=[K("po")])
                S.op("pe", lambda e: e.matmul(p_o[:, c0:c0 + 128], vnew[:], aqk[:], start=False, stop=True), r=[K("vnew"), K("aqk")], w=[K("po")])
                S.op("pe", lambda e: e.matmul(p_s[:, 0:128], kdec[:], vnew[:], start=True, stop=True), r=[K("kdec"), K("vnew")], w=[K("pS")])
                S.op("dve", lambda e: e.scalar_tensor_tensor(out=s32[:], in0=s32[:], scalar=eg[:, 127:128], in1=p_s[:, 0:128],
                                                             op0=ALU.mult, op1=ALU.add),
                     r=[K("s32"), K("eg"), K("pS")], w=[K("s32")])
                S.op("act", lambda e: e.activation(out=sbf[:], in_=s32[:], func=AF.Copy), r=[K("s32")], w=[K("sbf")])
            S.op("dve", lambda e: e.tensor_copy(out=osb[:], in_=p_o[:]), r=[K("po")], w=[K("osb")])
            S.op("act", lambda e: e.activation(out=sq[:], in_=osb[:], func=AF.Square), r=[K("osb")], w=[K("sq")])
            S.op("pe", lambda e: e.matmul(p_ss[:], ones_n[:], sq[:], start=True, stop=True), r=[K("ones_n"), K("sq")], w=[K("pss")])
            S.op("act", lambda e: e.activation(out=rstd[:], in_=p_ss[:], func=AF.Sqrt, bias=1e-6, scale=1.0), r=[K("pss")], w=[K("rstd")])
            S.op("dve", lambda e: e.reciprocal(out=rstd[:], in_=rstd[:]), r=[K("rstd")], w=[K("rstd")])
            S.op("dve", lambda e: e.tensor_tensor(out=osb[:], in0=osb[:], in1=rstd[:], op=ALU.mult), r=[K("osb"), K("rstd")], w=[K("osb")])
            S.op("dve", lambda e: e.scalar_tensor_tensor(out=yb[gs][:], in0=osb[:], scalar=gcol[:, 0:1], in1=gt[gs][:],
                                                         op0=ALU.mult, op1=ALU.mult),
                 r=[K("osb"), K("gcol"), K("gt%d" % gs)], w=[K("yb%d" % gs)])
            S.dma("pool", yT[256:384, q0:q0 + 512], yb[gs][:], r=[K("yb%d" % gs)], w=[("ysrc", q0 // 1024)])
        S.barrier()


class YOut:
    def __init__(self, ysrc):
        self.y = ysrc

    def __getitem__(self, idx):
        rs, cs = idx
        k, off = cs.start // 1024, cs.start % 1024
        return self.y[k][rs.start:rs.stop, off:off + (cs.stop - cs.start)]


def exchange(S, src, dst, n, rkey, wkey):
    for k in range(n):
        S.coll(src[k], dst[k], r=[rkey], w=[wkey])
    S.barrier()


def phase_m1(nc, S, T, wm_g, wbr_g, scr_h, ydst, msrc, mdst):
    NT = T // 512
    with contextlib.ExitStack() as es:
        sb = lambda n, s, d: es.enter_context(nc.sbuf_tensor(uq("m_" + n), s, d))
        ps = lambda n: (S.psum_keys.add("m_" + n), es.enter_context(nc.psum_tensor(uq("m_" + n), [128, 512], F32)))[1]
        K = lambda n: "m_" + n
        wm = sb("wm", [128, 16, 4, 512], BF16)
        wb = sb("wb", [128, 4, 4, 512], BF16)
        st = [sb("st%d" % i, [128, 2048], F32) for i in range(2)]
        hT = [sb("hT%d" % i, [128, 16, 512], BF16) for i in range(2)]
        yt = [sb("yt%d" % i, [128, 16, 512], BF16) for i in range(2)]
        mx = [sb("mx%d" % i, [128, 4, 512], BF16) for i in range(2)]
        gsb = [sb("g%d" % i, [128, 512], F32) for i in range(2)]
        acc = sb("acc", [128, 512], F32)
        tmp = sb("tmp", [128, 512], F32)
        p_l = [ps("pl%d" % i) for i in range(3)]
        p_z = [ps("pz%d" % i) for i in range(3)]
        wm_v = wm_g.rearrange("(c p) b j -> c p (b j)", p=128)
        for c in range(16):
            sl = c % 2
            S.dma("sp", st[sl][:, :], wm_v[c], w=[K("st%d" % sl)])
            if c % 2:
                S.op("act", lambda e: e.activation(out=wm[:, c, :, :].rearrange("p b j -> p (b j)"), in_=st[sl][:, :], func=AF.Copy),
                     r=[K("st%d" % sl)], w=[K("wm")])
            else:
                S.op("dve", lambda e: e.tensor_copy(out=wm[:, c, :, :].rearrange("p b j -> p (b j)"), in_=st[sl][:, :]),
                     r=[K("st%d" % sl)], w=[K("wm")])
        for br in range(4):
            sl = br % 2
            S.dma("sp", st[sl][:, :].rearrange("p (c j) -> p c j", c=4), wbr_g[br].rearrange("(c p) j -> p c j", p=128), w=[K("st%d" % sl)])
            S.op("dve", lambda e: e.tensor_copy(out=wb[:, :, br, :], in_=st[sl][:, :].rearrange("p (c j) -> p c j", c=4)),
                 r=[K("st%d" % sl)], w=[K("wb")])
        li = zi = 0
        for t in range(NT):
            sl = t % 2
            t0 = t * 512
            k, off = t0 // 1024, t0 % 1024
            S.dma("sp", hT[sl][:], scr_h[t], r=["scr_h"], w=[K("hT%d" % sl)])
            yv = ydst[k].rearrange("(g b p) t -> p b g t", g=4, b=4, p=128)
            for br in range(4):
                S.dma("sp", yt[sl][:, 4 * br:4 * br + 4, :], yv[:, br, :, off:off + 512], r=[("ydst", k)], w=[K("yt%d" % sl)])
            for fo in range(4):
                for br in range(4):
                    pl = p_l[li % 3]; kl = K("pl%d" % (li % 3)); li += 1
                    pz = p_z[zi % 3]; kz = K("pz%d" % (zi % 3)); zi += 1
                    g = gsb[br % 2]; kg = K("g%d" % (br % 2))
                    for c in range(16):
                        S.op("pe", lambda e: e.matmul(pl[:], wm[:, c, br, fo * 128:(fo + 1) * 128], hT[sl][:, c, :], start=(c == 0), stop=(c == 15)),
                             r=[K("wm"), K("hT%d" % sl)], w=[kl])
                    S.op("act", lambda e: e.activation(out=g[:], in_=pl[:], func=AF.Sigmoid), r=[kl], w=[kg])
                    for c4 in range(4):
                        S.op("pe", lambda e: e.matmul(pz[:], wb[:, c4, br, fo * 128:(fo + 1) * 128], yt[sl][:, 4 * br + c4, :],
                                                      start=(c4 == 0), stop=(c4 == 3)),
                             r=[K("wb"), K("yt%d" % sl)], w=[kz])
                    if br == 0:
                        S.op("dve", lambda e: e.tensor_tensor(out=acc[:], in0=pz[:], in1=g[:], op=ALU.mult), r=[kz, kg], w=[K("acc")])
                    else:
                        S.op("dve", lambda e: e.tensor_tensor(out=tmp[:], in0=pz[:], in1=g[:], op=ALU.mult), r=[kz, kg], w=[K("tmp")])
                        if br < 3:
                            S.op("dve", lambda e: e.tensor_tensor(out=acc[:], in0=acc[:], in1=tmp[:], op=ALU.add),
                                 r=[K("acc"), K("tmp")], w=[K("acc")])
                        else:
                            S.op("dve", lambda e: e.tensor_tensor(out=mx[sl][:, fo, :], in0=acc[:], in1=tmp[:], op=ALU.add),
                                 r=[K("acc"), K("tmp")], w=[K("mx%d" % sl)])
            S.dma("act", msrc[k].rearrange("(f p) t -> p f t", p=128)[:, :, off:off + 512], mx[sl][:],
                  r=[K("mx%d" % sl)], w=[("msrc", k)])
            if t % 2 == 1:
                S.coll(msrc[k], mdst[k], r=[("msrc", k)], w=[("mdst", k)])
        S.barrier()


def phase_m2(nc, S, T, wout_g, mdst, xmine_tile, xsrc_out, xdst_out, lay):
    NT = T // 512
    with contextlib.ExitStack() as es:
        sb = lambda n, s, d: es.enter_context(nc.sbuf_tensor(uq("o_" + n), s, d))
        ps = lambda n: (S.psum_keys.add("o_" + n), es.enter_context(nc.psum_tensor(uq("o_" + n), [128, 512], F32)))[1]
        K = lambda n: "o_" + n
        wo = sb("wo", [128, 16, 512], BF16)
        st = [sb("st%d" % i, [128, 2048], F32) for i in range(2)]
        mt = [sb("mt%d" % i, [128, 16, 512], BF16) for i in range(2)]
        xr = [sb("xr%d" % i, [128, 4, 512], F32) for i in range(2)]
        xn = [sb("xn%d" % i, [128, 4, 512], F32) for i in range(2)]
        p_o = [ps("po%d" % i) for i in range(4)]
        wv = wout_g.rearrange("(c p) j -> p c j", p=128)
        for c4 in range(4):
            sl = c4 % 2
            S.dma("sp", st[sl][:, :].rearrange("p (c j) -> p c j", c=4), wv[:, 4 * c4:4 * c4 + 4, :], w=[K("st%d" % sl)])
            S.op("dve", lambda e: e.tensor_copy(out=wo[:, 4 * c4:4 * c4 + 4, :], in_=st[sl][:, :].rearrange("p (c j) -> p c j", c=4)),
                 r=[K("st%d" % sl)], w=[K("wo")])
        oi = 0
        for t in range(NT):
            sl = t % 2
            t0 = t * 512
            k, off = t0 // 1024, t0 % 1024
            S.dma("sp", mt[sl][:], mdst[k].rearrange("(c p) t -> p c t", p=128)[:, :, off:off + 512], r=[("mdst", k)], w=[K("mt%d" % sl)])
            S.dma("sp", xr[sl][:], xmine_tile(t), r=[("xsrc", lay - 1, t)], w=[K("xr%d" % sl)])
            for fo in range(4):
                po = p_o[oi % 4]; kpo = K("po%d" % (oi % 4)); oi += 1
                for c in range(16):
                    S.op("pe", lambda e: e.matmul(po[:], wo[:, c, fo * 128:(fo + 1) * 128], mt[sl][:, c, :], start=(c == 0), stop=(c == 15)),
                         r=[K("wo"), K("mt%d" % sl)], w=[kpo])
                S.op("dve", lambda e: e.tensor_tensor(out=xn[sl][:, fo, :], in0=po[:], in1=xr[sl][:, fo, :], op=ALU.add),
                     r=[kpo, K("xr%d" % sl)], w=[K("xn%d" % sl)])
            S.dma("pool", xsrc_out[t].rearrange("(f p) t -> p f t", p=128), xn[sl][:], r=[K("xn%d" % sl)], w=[("xsrc", lay, t)])
            S.coll(xsrc_out[t], xdst_out[t], r=[("xsrc", lay, t)], w=[("xdst", lay, t)])
        S.barrier()


def phase_fnorm(nc, S, T, xdst, xmine, fgain4, outT):
    NT = T // 512
    with contextlib.ExitStack() as es:
        sb = lambda n, s, d: es.enter_context(nc.sbuf_tensor(uq("f_" + n), s, d))
        ps = lambda n: (S.psum_keys.add("f_" + n), es.enter_context(nc.psum_tensor(uq("f_" + n), [128, 512], F32)))[1]
        K = lambda n: "f_" + n
        fg = sb("fg", [128, 4], F32)
        ones = sb("ones", [128, 128], BF16)
        xs = [sb("xs%d" % i, [128, 16, 512], F32) for i in range(2)]
        xm = [sb("xm%d" % i, [128, 4, 512], F32) for i in range(2)]
        sq = sb("sq", [128, 16, 512], BF16)
        rstd = sb("rstd", [128, 512], F32)
        ot = [sb("ot%d" % i, [128, 4, 512], F32) for i in range(2)]
        p_q = ps("pq")
        S.op("dve", lambda e: e.memset(ones[:], 1.0 / D), w=[K("ones")])
        S.dma("sp", fg[:], fgain4, w=[K("fg")])
        for t in range(NT):
            sl = t % 2
            t0 = t * 512
            for c4 in range(4):
                S.dma("sp", xs[sl][:, 4 * c4:4 * c4 + 4, :], xdst[t].rearrange("(c p) t -> p c t", p=128)[:, 4 * c4:4 * c4 + 4, :],
                      r=[("xdst", DEPTH - 1, t)], w=[K("xs%d" % sl)])
            S.dma("sp", xm[sl][:], xmine[t].rearrange("(f p) t -> p f t", p=128), r=[("xsrc", DEPTH - 1, t)], w=[K("xm%d" % sl)])
            S.op("act", lambda e: e.activation(out=sq[:], in_=xs[sl][:], func=AF.Square), r=[K("xs%d" % sl)], w=[K("sq")])
            for c in range(16):
                S.op("pe", lambda e: e.matmul(p_q[:], ones[:], sq[:, c, :], start=(c == 0), stop=(c == 15)),
                     r=[K("sq"), K("ones")], w=[K("pq")])
            S.op("act", lambda e: e.activation(out=rstd[:], in_=p_q[:], func=AF.Sqrt, bias=1e-6, scale=1.0), r=[K("pq")], w=[K("rstd")])
            S.op("dve", lambda e: e.reciprocal(out=rstd[:], in_=rstd[:]), r=[K("rstd")], w=[K("rstd")])
            for f in range(4):
                S.op("dve", lambda e: e.scalar_tensor_tensor(out=ot[sl][:, f, :], in0=xm[sl][:, f, :], scalar=fg[:, f:f + 1], in1=rstd[:],
                                                             op0=ALU.mult, op1=ALU.mult),
                     r=[K("xm%d" % sl), K("fg"), K("rstd")], w=[K("ot%d" % sl)])
            S.dma("pool", outT.rearrange("(f p) t -> p f t", p=128)[:, :, t0:t0 + 512], ot[sl][:], r=[K("ot%d" % sl)], w=["outT"])
        S.barrier()


OFF = {"a_q": 0, "a_k": 512, "a_v": 1024, "a_gate": 1536, "b_q": 2048, "b_k": 2560, "b_v": 3072, "b_gate": 3584,
       "c_q": 4096, "c_k": 4608, "c_v": 5120, "c_z": 5632, "c_beta": 6144, "c_a": 6148,
       "d_q": 6152, "d_f": 6664, "d_i": 7176, "d_gate": 7688, "merge": 8200}
FM_ORDER = ("a_q", "a_k", "b_q", "b_k", "a_gate", "b_gate", "c_q", "c_k", "c_v", "c_z", "d_q", "d_f", "d_gate")
TM_ORDER = ("a_v", "b_v", "d_f", "d_i")
DEPTH = 2


def build_fused(T):
    nc = bass.Bass("TRN2", target_bir_lowering=False)
    inp = lambda n, s, d=F32: nc.dram_tensor(n, s, d, kind="ExternalInput").ap()
    NT = T // 512
    NK = T // 1024
    xT = inp("xT", [D, T])
    xmine0 = inp("xmine0", [512, T])
    btA = inp("btA", [2, 128, 2944]); mtA = inp("mtA", [2, 128, 2944])
    btB = inp("btB", [1, 128, 2432]); mtB = inp("mtB", [1, 128, 2432])
    b31 = inp("b31", [128, 1])
    consts = {k: inp(k, [128, 128]) for k in ("triu", "trius", "trils", "ident")}
    lbz_col = inp("lbz_col", [128, 2]); lbz_row = inp("lbz_row", [128, 2, 128])
    fgain4 = inp("fgain4", [128, 4])
    L = []
    for l in range(DEPTH):
        L.append({
            "wfm": inp("wfm%d" % l, [D, NFM * 128]), "wtm": inp("wtm%d" % l, [D, NTM + 2]), "gain16": inp("gain16_%d" % l, [128, 16]),
            "auxB": {"b31": b31, "dl": inp("dl%d" % l, [128, 256]), "subln": inp("subln%d" % l, [128, 1]),
                     "lam_init": 0.8 - 0.6 * math.exp(-0.3 * l)},
            "auxC": {"conv": inp("conv%d" % l, [128, 3, 4]), "a_dt": inp("a_dt%d" % l, [128, 2]), "dn_gain": inp("dn_gain%d" % l, [128, 1])},
            "auxD": {"hg_gain": inp("hg_gain%d" % l, [128, 1]), "lbz_col": lbz_col, "lbz_row": lbz_row},
            "wm_g": inp("wm_g%d" % l, [D, 4, 512]), "wbr_g": inp("wbr_g%d" % l, [4, 512, 512]), "wout_g": inp("wout_g%d" % l, [D, 512]),
        })
    scr_qk = nc.dram_tensor("scr_qk", [NQK, 128, T], BF16).ap()
    scr_fm = nc.dram_tensor("scr_fm", [NFM32, 128, T], F32).ap()
    scr_v = nc.dram_tensor("scr_v", [T, 256], BF16).ap()
    scr_tm = nc.dram_tensor("scr_tm", [T, 256], F32).ap()
    scr_bd = nc.dram_tensor("scr_bd", [T, 2], F32).ap()
    scr_h = nc.dram_tensor("scr_h", [NT, 128, 16, 512], BF16).ap()
    ysrc = nc.dram_tensor("ysrc", [NK, 512, 1024], BF16).ap()
    ydst = nc.dram_tensor("ydst", [NK, 2048, 1024], BF16).ap()
    msrc = nc.dram_tensor("msrc", [NK, 512, 1024], BF16).ap()
    mdst = nc.dram_tensor("mdst", [NK, 2048, 1024], BF16).ap()
    xsrc = [nc.dram_tensor("xsrc%d" % l, [NT, 512, 512], F32).ap() for l in range(DEPTH)]
    xdst = [nc.dram_tensor("xdst%d" % l, [NT, 2048, 512], F32).ap() for l in range(DEPTH)]
    outT = nc.dram_tensor("outT", [512, T], F32, kind="ExternalOutput").ap()
    yT = YOut(ysrc)
    xT_v = xT.rearrange("(c p) t -> p c t", p=128)
    xm0_v = xmine0.rearrange("(f p) t -> p f t", p=128)
    with contextlib.ExitStack() as es:
        S = Sched(nc, es)
        for l in range(DEPTH):
            P = L[l]
            UNIQ[0] = l
            if l == 0:
                xtile = lambda t, c4: xT_v[:, 4 * c4:4 * c4 + 4, t * 512:(t + 1) * 512]
                xmine_tile = lambda t: xm0_v[:, :, t * 512:(t + 1) * 512]
            else:
                xd = xdst[l - 1]
                xp = xsrc[l - 1]
                xtile = lambda t, c4: xd[t].rearrange("(c p) t -> p c t", p=128)[:, 4 * c4:4 * c4 + 4, :]
                xmine_tile = lambda t: xp[t].rearrange("(f p) t -> p f t", p=128)
            phase_proj(nc, S, T, xtile, P["wfm"], P["wtm"], P["gain16"], scr_qk, scr_fm, scr_v, scr_tm, scr_bd, scr_h, lay=l)
            phase_attn(nc, S, T, "A", scr_qk, scr_fm, scr_v, btA, mtA, None, yT)
            phase_attn(nc, S, T, "B", scr_qk, scr_fm, scr_v, btB, mtB, P["auxB"], yT)
            phase_gdn(nc, S, T, scr_fm, scr_bd, consts, P["auxC"], yT)

            def ytile_done(G):
                if G % 2 == 1:
                    k = G // 2
                    S.coll(ysrc[k], ydst[k], r=[("ysrc", k)], w=[("ydst", k)])
            phase_hgrn(nc, S, T, l, scr_fm, scr_tm, consts, P["auxD"], yT, on_tile=ytile_done)
            phase_m1(nc, S, T, P["wm_g"], P["wbr_g"], scr_h, ydst, msrc, mdst)
            phase_m2(nc, S, T, P["wout_g"], mdst, xmine_tile, xsrc[l], xdst[l], l)
        phase_fnorm(nc, S, T, xdst[DEPTH - 1], xsrc[DEPTH - 1], fgain4, outT)
        S.finish()
    return nc


def core_inputs(inputs, xT_b, g):
    l16 = lambda v: np.ascontiguousarray(v.reshape(16, 128).T)
    col = lambda v: np.ascontiguousarray(v.reshape(128, 1))
    rel_bias = inputs["rel_bias"]
    btA, mtA = attn_tables(rel_bias, "A", g)
    btB, mtB = attn_tables(rel_bias, "B", g)
    lbz = inputs["hg_lb_logits"][:, 128 * g:128 * (g + 1)]
    one = np.ones((128, 128), np.float32)
    m = {
        "xT": xT_b, "xmine0": np.ascontiguousarray(xT_b[512 * g:512 * (g + 1)]),
        "btA": btA, "mtA": mtA, "btB": btB, "mtB": mtB,
        "b31": np.full((128, 1), rel_bias[31, 8 + g], np.float32),
        "triu": np.triu(one), "trius": np.triu(one, 1), "trils": np.tril(one, -1), "ident": np.eye(128, dtype=np.float32),
        "lbz_col": np.ascontiguousarray(lbz.T),
        "lbz_row": np.ascontiguousarray(np.broadcast_to(lbz[None], (128, 2, 128))),
        "fgain4": np.ascontiguousarray(inputs["final_gain"][512 * g:512 * (g + 1)].reshape(4, 128).T),
    }
    for l in range(DEPTH):
        w = inputs["w_in"][l]
        sl = lambda name, wd=128: w[:, OFF[name] + wd * g:OFF[name] + wd * (g + 1)]
        conv = inputs["dn_conv"][l]
        convl = np.stack([conv[:, j * 512 + 128 * g:j * 512 + 128 * (g + 1)] for j in range(3)])
        wmg = w[:, OFF["merge"]:].reshape(D, 4, D)[:, :, 512 * g:512 * (g + 1)]
        m.update({
            "wfm%d" % l: np.ascontiguousarray(np.concatenate([sl(n) for n in FM_ORDER], axis=1)),
            "wtm%d" % l: np.ascontiguousarray(np.concatenate([sl(n) for n in TM_ORDER] + [sl("c_beta", 1), sl("c_a", 1)], axis=1)),
            "gain16_%d" % l: l16(inputs["norm_gain"][l]),
            "dl%d" % l: np.ascontiguousarray(np.broadcast_to(inputs["diff_lambda"][l].reshape(1, 256), (128, 256))),
            "subln%d" % l: col(inputs["diff_subln_gain"][l]),
            "conv%d" % l: np.ascontiguousarray(convl.transpose(2, 0, 1)),
            "a_dt%d" % l: np.ascontiguousarray(np.broadcast_to(
                np.array([inputs["dn_a_log"][l, g], inputs["dn_dt_bias"][l, g]], np.float32), (128, 2))),
            "dn_gain%d" % l: col(inputs["dn_norm_gain"][l]),
            "hg_gain%d" % l: col(inputs["hg_norm_gain"][l]),
            "wm_g%d" % l: np.ascontiguousarray(wmg),
            "wbr_g%d" % l: np.ascontiguousarray(inputs["w_branch"][l][:, :, 512 * g:512 * (g + 1)]),
            "wout_g%d" % l: np.ascontiguousarray(inputs["w_out"][l][:, 512 * g:512 * (g + 1)]),
        })
    return m


def kernel_impl(inputs, n_cores=8):
    inputs = {k: np.asarray(v, dtype=np.float32) for k, v in inputs.items()}
    x = inputs["x"]
    Bsz, T, _ = x.shape
    assert Bsz * 4 == n_cores
    nc = build_fused(T)
    in_maps = []
    for b in range(Bsz):
        xT_b = np.ascontiguousarray(x[b].T)
        for g in range(4):
            in_maps.append(core_inputs(inputs, xT_b, g))
    res = run_bass_kernel_spmd(nc, in_maps, core_ids=list(range(n_cores)))
    out = np.empty((Bsz, T, D), np.float32)
    for b in range(Bsz):
        for g in range(4):
            out[b, :, 512 * g:512 * (g + 1)] = np.asarray(res.results[b * 4 + g]["outT"]).T
    return out


def kernel(**inputs):
    return kernel_impl(inputs)
```
